# Optimizing a Trainium2 kernel written in Bass

```python
import math
import jax, jax.numpy as jnp
from jax import lax
import numpy as np

D_MODEL = 2048
BATCH = 4
SEQ = 4096
DEPTH = 1

MIX_WIDTH = D_MODEL
HEAD_DIM = 128
ATTN_WIDTH = MIX_WIDTH // 2
ATTN_HEADS = ATTN_WIDTH // HEAD_DIM
KV_HEADS = 2
GQA_GROUP = ATTN_HEADS // KV_HEADS
WINDOW = 128
ATTN_BLOCK = 128
ROPE_THETA = 10000.0

GLA_WIDTH = MIX_WIDTH - ATTN_WIDTH
GLA_HEADS = 4
GLA_KEY_WIDTH = GLA_WIDTH // 2
GLA_DK = GLA_KEY_WIDTH // GLA_HEADS
GLA_DV = GLA_WIDTH // GLA_HEADS
GLA_DECAY_RANK = 16
GLA_GATE_NORMALIZER = 16.0
GLA_CHUNK = 64

D_FF = 5632
CONV_WIDTH = 3
NORM_EPS = 1e-6

IN_WIDTHS = (ATTN_WIDTH,
             KV_HEADS * HEAD_DIM,
             KV_HEADS * HEAD_DIM,
             GLA_KEY_WIDTH,
             GLA_KEY_WIDTH,
             GLA_WIDTH,
             GLA_WIDTH,
             GLA_DECAY_RANK,
             GLA_DECAY_RANK)
IN_TOTAL = sum(IN_WIDTHS)

kernel_name = "hymba_style_swa_gla_convffn_encoder"


def rms_norm(x, g):
    xf = x.astype(jnp.float32)
    y = xf * lax.rsqrt(jnp.mean(xf * xf, axis=-1, keepdims=True) + NORM_EPS)
    return (y * g.astype(jnp.float32)).astype(x.dtype)


def rope(x, pos):
    half = x.shape[-1] // 2
    inv = 1.0 / (ROPE_THETA ** (jnp.arange(half, dtype=jnp.float32) / half))
    ang = pos.astype(jnp.float32)[:, None] * inv[None, :]
    cos = jnp.cos(ang)[:, None, :]
    sin = jnp.sin(ang)[:, None, :]
    xf = x.astype(jnp.float32)
    x1, x2 = xf[..., :half], xf[..., half:]
    return jnp.concatenate([x1 * cos - x2 * sin, x2 * cos + x1 * sin], axis=-1).astype(x.dtype)


def window_attention(q, k, v, sink):
    B, T, HQ, D = q.shape
    nb = T // ATTN_BLOCK
    qb = q.reshape(B, nb, ATTN_BLOCK, KV_HEADS, GQA_GROUP, D)
    pad = ((0, 0), (ATTN_BLOCK, ATTN_BLOCK), (0, 0), (0, 0))
    kb = jnp.pad(k, pad).reshape(B, nb + 2, ATTN_BLOCK, KV_HEADS, D)
    vb = jnp.pad(v, pad).reshape(B, nb + 2, ATTN_BLOCK, KV_HEADS, D)
    kw = jnp.concatenate([kb[:, :-2], kb[:, 1:-1], kb[:, 2:]], axis=2)
    vw = jnp.concatenate([vb[:, :-2], vb[:, 1:-1], vb[:, 2:]], axis=2)
    s = jnp.einsum('bnqhgd,bnshd->bhgnqs', qb, kw).astype(jnp.float32) * (D ** -0.5)
    qi = jnp.arange(ATTN_BLOCK)[:, None]
    sj = jnp.arange(3 * ATTN_BLOCK)[None, :]
    rel = sj - ATTN_BLOCK - qi
    kpos = jnp.arange(nb)[:, None, None] * ATTN_BLOCK - ATTN_BLOCK + sj[None]
    mask = (jnp.abs(rel) <= WINDOW)[None] & (kpos >= 0) & (kpos < T)
    s = jnp.where(mask, s, -jnp.inf)
    sk = sink.astype(jnp.float32).reshape(KV_HEADS, GQA_GROUP)[None, :, :, None, None, None]
    m = jnp.maximum(jnp.max(s, axis=-1, keepdims=True), sk)
    p = jnp.exp(s - m)
    p = p / (jnp.sum(p, axis=-1, keepdims=True) + jnp.exp(sk - m))
    o = jnp.einsum('bhgnqs,bnshd->bnqhgd', p.astype(v.dtype), vw)
    return o.reshape(B, T, HQ * D)


def gla_chunked(q, k, v, g, strict):
    B, H, T, DK = q.shape
    DV = v.shape[-1]
    C = GLA_CHUNK
    n = T // C
    q = q.reshape(B, H, n, C, DK)
    k = k.reshape(B, H, n, C, DK)
    g = g.reshape(B, H, n, C, DK)
    v = v.reshape(B, H, n, C, DV)
    b = jnp.cumsum(g, axis=3)
    b_last = b[:, :, :, -1:, :]
    b_ref = b[:, :, :, C // 2:C // 2 + 1, :]
    a = jnp.einsum('bhnid,bhnjd->bhnij', q * jnp.exp(b - b_ref), k * jnp.exp(b_ref - b))
    mask = jnp.tril(jnp.ones((C, C), dtype=bool), -1 if strict else 0)
    a = jnp.where(mask, a, 0.0)
    o_intra = jnp.einsum('bhnij,bhnje->bhnie', a, v)
    q_in = q * jnp.exp(b)
    k_out = k * jnp.exp(b_last - b)
    chunk_decay = jnp.exp(b_last[:, :, :, 0, :])

    def step(state, inp):
        qc, kc, vc, dc = inp
        o = jnp.einsum('bhid,bhde->bhie', qc, state)
        state = state * dc[..., None] + jnp.einsum('bhjd,bhje->bhde', kc, vc)
        return state, o

    s0 = jnp.zeros((B, H, DK, DV), jnp.float32)
    xs = (jnp.moveaxis(q_in, 2, 0), jnp.moveaxis(k_out, 2, 0),
          jnp.moveaxis(v, 2, 0), jnp.moveaxis(chunk_decay, 2, 0))
    _, o_inter = lax.scan(step, s0, xs)
    o_inter = jnp.moveaxis(o_inter, 0, 2)
    return (o_intra + o_inter).reshape(B, H, T, DV)


def bidirectional_gla(q, k, v, lr_f, lr_b, wa2_f, ba_f, wa2_b, ba_b, gate, out_norm_g):
    B, T, _ = q.shape
    f32 = jnp.float32

    def heads(t, d):
        return jnp.transpose(t.astype(f32).reshape(B, T, GLA_HEADS, d), (0, 2, 1, 3))

    qh = heads(q, GLA_DK) * (GLA_DK ** -0.5)
    kh = heads(k, GLA_DK)
    vh = heads(v, GLA_DV)
    g_f = jax.nn.log_sigmoid((lr_f @ wa2_f + ba_f).astype(f32)) / GLA_GATE_NORMALIZER
    g_b = jax.nn.log_sigmoid((lr_b @ wa2_b + ba_b).astype(f32)) / GLA_GATE_NORMALIZER
    gf = heads(g_f, GLA_DK)
    gb = heads(g_b, GLA_DK)
    o_fwd = gla_chunked(qh, kh, vh, gf, strict=False)
    flip = lambda t: jnp.flip(t, axis=2)
    o_bwd = flip(gla_chunked(flip(qh), flip(kh), flip(vh), flip(gb), strict=True))
    o = jnp.transpose(o_fwd + o_bwd, (0, 2, 1, 3)).astype(v.dtype)
    o = rms_norm(o, out_norm_g)
    o = o * jax.nn.silu(gate.reshape(B, T, GLA_HEADS, GLA_DV))
    return o.reshape(B, T, GLA_WIDTH)


def conv_gated_ffn(h, w_up, conv_w, conv_b, w_down):
    u = h @ w_up
    up_ = jnp.pad(u, ((0, 0), (1, 1), (0, 0)))
    u = up_[:, :-2] * conv_w[0] + up_[:, 1:-1] * conv_w[1] + up_[:, 2:] * conv_w[2] + conv_b
    gate, val = jnp.split(u, 2, axis=-1)
    return (jax.nn.silu(gate) * val) @ w_down


def hybrid_layer(x, norm1_g, w_in, q_norm_g, k_norm_g, sink, wa2_f, ba_f, wa2_b, ba_b,
                 gla_out_norm_g, w_out, norm2_g, w_up, conv_w, conv_b, w_down):
    B, T, _ = x.shape
    pos = jnp.arange(T, dtype=jnp.int32)
    h = rms_norm(x, norm1_g)
    proj = h @ w_in
    offsets = []
    acc = 0
    for w in IN_WIDTHS[:-1]:
        acc += w
        offsets.append(acc)
    q_a, k_a, v_a, q_g, k_g, v_g, gate_g, lr_f, lr_b = jnp.split(proj, offsets, axis=-1)
    qa = rope(rms_norm(q_a.reshape(B, T, ATTN_HEADS, HEAD_DIM), q_norm_g), pos)
    ka = rope(rms_norm(k_a.reshape(B, T, KV_HEADS, HEAD_DIM), k_norm_g), pos)
    va = v_a.reshape(B, T, KV_HEADS, HEAD_DIM)
    o_attn = window_attention(qa, ka, va, sink)
    o_gla = bidirectional_gla(q_g, k_g, v_g, lr_f, lr_b, wa2_f, ba_f, wa2_b, ba_b,
                              gate_g, gla_out_norm_g)
    x = x + jnp.concatenate([o_attn, o_gla], axis=-1) @ w_out
    x = x + conv_gated_ffn(rms_norm(x, norm2_g), w_up, conv_w, conv_b, w_down)
    return x


def setup_inputs(seed: int = 0) -> dict:
    key = jax.random.key(seed)
    ks = jax.random.split(key, 20)
    f32 = jnp.float32
    L = DEPTH

    def nrm(k, shape, scale):
        return jax.random.normal(k, shape, f32) * scale

    return {
        "x": nrm(ks[0], (BATCH, SEQ, D_MODEL), 1.0),
        "norm1_g": 1.0 + nrm(ks[1], (L, D_MODEL), 0.02),
        "w_in": nrm(ks[2], (L, D_MODEL, IN_TOTAL), D_MODEL ** -0.5),
        "attn_q_norm_g": 1.0 + nrm(ks[3], (L, HEAD_DIM), 0.02),
        "attn_k_norm_g": 1.0 + nrm(ks[4], (L, HEAD_DIM), 0.02),
        "attn_sink": nrm(ks[5], (L, ATTN_HEADS), 0.5),
        "gla_wa2_fwd": nrm(ks[6], (L, GLA_DECAY_RANK, GLA_KEY_WIDTH), GLA_DECAY_RANK ** -0.5),
        "gla_ba_fwd": nrm(ks[7], (L, GLA_KEY_WIDTH), 0.1),
        "gla_wa2_bwd": nrm(ks[8], (L, GLA_DECAY_RANK, GLA_KEY_WIDTH), GLA_DECAY_RANK ** -0.5),
        "gla_ba_bwd": nrm(ks[9], (L, GLA_KEY_WIDTH), 0.1),
        "gla_out_norm_g": 1.0 + nrm(ks[10], (L, GLA_DV), 0.02),
        "w_out": nrm(ks[11], (L, MIX_WIDTH, D_MODEL), MIX_WIDTH ** -0.5),
        "norm2_g": 1.0 + nrm(ks[12], (L, D_MODEL), 0.02),
        "w_up": nrm(ks[13], (L, D_MODEL, 2 * D_FF), D_MODEL ** -0.5),
        "conv_w": nrm(ks[14], (L, CONV_WIDTH, 2 * D_FF), CONV_WIDTH ** -0.5),
        "conv_b": nrm(ks[15], (L, 2 * D_FF), 0.02),
        "w_down": nrm(ks[16], (L, D_FF, D_MODEL), D_FF ** -0.5),
    }


def reference(x, norm1_g, w_in, attn_q_norm_g, attn_k_norm_g, attn_sink, gla_wa2_fwd,
              gla_ba_fwd, gla_wa2_bwd, gla_ba_bwd, gla_out_norm_g, w_out, norm2_g,
              w_up, conv_w, conv_b, w_down):
    for l in range(DEPTH):
        x = hybrid_layer(x, norm1_g[l], w_in[l], attn_q_norm_g[l], attn_k_norm_g[l],
                         attn_sink[l], gla_wa2_fwd[l], gla_ba_fwd[l], gla_wa2_bwd[l],
                         gla_ba_bwd[l], gla_out_norm_g[l], w_out[l], norm2_g[l],
                         w_up[l], conv_w[l], conv_b[l], w_down[l])
    return x
```

```python
import contextlib
import numpy as np
import concourse.bass as bass
import concourse.mybir as mybir
from concourse.bass_utils import run_bass_kernel_spmd

F32 = mybir.dt.float32
BF16 = mybir.dt.bfloat16
AF = mybir.ActivationFunctionType
ALU = mybir.AluOpType
AX = mybir.AxisListType

P = 128
D = 2048
KD = 16
SEQ = 4096
NOWN = 16
DFF = 5632
NJ = 44
EPS = 1e-6
NXB = 17


class Ev:
    __slots__ = ("sem", "key", "val", "snap", "eng")

    def __init__(self, sem, key, val, snap, eng):
        self.sem, self.key, self.val, self.snap, self.eng = sem, key, val, snap, eng


class Tl:
    def __init__(self, t, name=""):
        self.t = t
        self.name = name
        self.w = None
        self.r = {}

    def __getitem__(self, idx):
        return self.t[idx]


class Eng:
    def __init__(self, kb, h, name, is_pe=False):
        self.kb, self.h, self.name, self.is_pe = kb, h, name, is_pe
        self.sem = kb.new_sem("e_" + name)
        self.key = "e_" + name
        self.cnt = 0
        self.known = {}


class SemPool:
    def __init__(self, kb, name, n):
        self.sems = [kb.new_sem(f"{name}{i}") for i in range(n)]
        self.keys = [f"{name}{i}" for i in range(n)]
        self.vals = [0] * n
        self.last = [None] * n
        self.i = 0


class KB:
    def __init__(self, nc, es):
        self.nc, self.es = nc, es
        self.nsem = 0
        self.pe = Eng(self, nc.tensor, "pe", True)
        self.act = Eng(self, nc.scalar, "act")
        self.dve = Eng(self, nc.vector, "dve")
        self.pool = Eng(self, nc.gpsimd, "pool")
        self.sp = Eng(self, nc.sync, "sp")
        self.engines = [self.pe, self.act, self.dve, self.pool, self.sp]
        self.pools = []
        self.nwait = 0
        self.nins = 0

    def new_sem(self, name):
        self.nsem += 1
        return self.es.enter_context(self.nc.semaphore(name))

    def sem_pool(self, name, n):
        p = SemPool(self, name, n)
        self.pools.append(p)
        return p

    def _wait(self, eng, ev):
        if ev is None:
            return
        if eng.known.get(ev.key, 0) >= ev.val:
            return
        eng.h.wait_ge(ev.sem, ev.val)
        self.nwait += 1
        kn = eng.known
        for k2, v2 in ev.snap.items():
            if kn.get(k2, 0) < v2:
                kn[k2] = v2
        kn[ev.key] = ev.val

    def _deps(self, eng, reads, writes):
        for t in reads:
            ev = t.w
            if ev is not None and not (eng.is_pe and ev.eng is eng):
                self._wait(eng, ev)
        for t in writes:
            ev = t.w
            if ev is not None and not (eng.is_pe and ev.eng is eng):
                self._wait(eng, ev)
            for ev in t.r.values():
                if not (eng.is_pe and ev.eng is eng):
                    self._wait(eng, ev)

    def _record(self, ev, reads, writes):
        for t in reads:
            t.r[ev.key] = ev
        for t in writes:
            t.w = ev
            t.r = {}

    def op(self, eng, reads, writes, fn):
        self._deps(eng, reads, writes)
        ins = fn()
        eng.cnt += 1
        ins.then_inc(eng.sem, 1)
        self.nins += 1
        ev = Ev(eng.sem, eng.key, eng.cnt, dict(eng.known), eng)
        self._record(ev, reads, writes)
        return ev

    def dma(self, q, pool, out_ap, in_ap, reads, writes, **kw):
        self._deps(q, reads, writes)
        i = pool.i
        pool.i = (pool.i + 1) % len(pool.sems)
        if pool.last[i] is not None:
            self._wait(q, pool.last[i])
        ins = q.h.dma_start(out=out_ap, in_=in_ap, **kw)
        pool.vals[i] += 16
        ins.then_inc(pool.sems[i], 16)
        self.nins += 1
        ev = Ev(pool.sems[i], pool.keys[i], pool.vals[i], dict(q.known), None)
        pool.last[i] = ev
        self._record(ev, reads, writes)
        return ev

    def barrier(self):
        evs = []
        for e in self.engines:
            if e.cnt > 0:
                evs.append(Ev(e.sem, e.key, e.cnt, {}, e))
        for p in self.pools:
            for ev in p.last:
                if ev is not None:
                    evs.append(ev)
        for e in self.engines:
            for ev in evs:
                if ev.eng is e:
                    continue
                self._wait(e, ev)

    def mm(self, out_t, out_ap, lhs_t, lhs_ap, rhs_t, rhs_ap, start=True, stop=True):
        nc = self.nc
        return self.op(self.pe, [lhs_t, rhs_t], [out_t],
                       lambda: nc.tensor.matmul(out_ap, lhsT=lhs_ap, rhs=rhs_ap, start=start, stop=stop))

    def tr(self, out_t, out_ap, in_t, in_ap, id_t, id_ap):
        nc = self.nc
        return self.op(self.pe, [in_t, id_t], [out_t],
                       lambda: nc.tensor.transpose(out_ap, in_ap, id_ap))


CV_G1 = 0
CV_G2 = 16
CV_CW0 = 32
CV_CW1 = 120
CV_CW2 = 208
CV_CB = 296
NCV = 384

CM_ID = 0
CM_AFF = 128
CM_AFB = 256
CM_GMF = 384
CM_GMB = 512
CM_INF = 640
CM_INB = 644
NCM = 648

RW_GQ = 0
RW_GQS = 128
RW_GK = 256
RW_GKS = 384
RW_GOG = 512
NRW = 768
NEG = -30000.0
QS = 128 ** -0.5


def rmsnorm_to_T(kb, es_tiles, src_t, src_ap, nrows, gcol, dstT_t, dst_fn, bank_tiles, consts):
    nc = kb.nc
    junk, ss, lnv, rstd, hb = es_tiles
    if hb is None:
        hb = src_t
    ident, cvec = consts
    kb.op(kb.act, [src_t], [junk, ss],
          lambda: nc.scalar.activation(out=junk[0:nrows, :], in_=src_ap, func=AF.Square,
                                       accum_out=ss[0:nrows, :]))
    kb.op(kb.act, [ss, kb.eps_t], [lnv],
          lambda: nc.scalar.activation(out=lnv[0:nrows, :], in_=ss[0:nrows, :], func=AF.Ln,
                                       scale=1.0 / D, bias=kb.eps_t[0:nrows, :]))
    kb.op(kb.act, [lnv], [rstd],
          lambda: nc.scalar.activation(out=rstd[0:nrows, :], in_=lnv[0:nrows, :], func=AF.Exp, scale=-0.5))
    kb.op(kb.dve, [src_t, rstd], [hb],
          lambda: nc.vector.tensor_scalar(out=hb[0:nrows, 0:D], in0=src_ap, scalar1=rstd[0:nrows, :],
                                          scalar2=None, op0=ALU.mult))
    for g in range(4):
        bt = bank_tiles[g % len(bank_tiles)]
        for kk in range(4):
            k = g * 4 + kk
            kb.tr(bt, bt[:, kk * P:kk * P + nrows], hb, hb[0:nrows, k * P:(k + 1) * P],
                  ident, ident[0:nrows, 0:nrows])
        src = bt[:, :].rearrange("p (a b) -> p a b", b=P)[:, :, 0:nrows]
        gb = cvec[:, gcol + g * 4:gcol + g * 4 + 4].unsqueeze(2).broadcast_to([P, 4, nrows])
        kb.op(kb.dve, [bt, cvec], [dstT_t],
              lambda src=src, gb=gb, g=g: nc.vector.tensor_tensor(out=dst_fn(g * 4, 4), in0=src, in1=gb,
                                                                  op=ALU.mult))


def phase_ffn(kb, cst, x1_rows, out_rows, w_up, w_down, tiles=(0, 1, 2, 3)):
    nc = kb.nc
    ident, cvec = cst["cmat"], cst["cvec"]
    with contextlib.ExitStack() as es:
        def sb(name, shape, dt):
            return Tl(es.enter_context(nc.sbuf_tensor(name, shape, dt)), name)

        def psb(name):
            return Tl(es.enter_context(nc.psum_tensor(name, [P, 512], F32)), name)

        h2T = sb("f_h2T", [P, KD, 512], BF16)
        h2Th = sb("f_h2Th", [P, KD, 2], BF16)
        aT = sb("f_aT", [P, NJ, 512], BF16)
        xb = [sb(f"f_xb{i}", [P, D], F32) for i in range(5)]
        hb = sb("f_hb", [P, D], F32)
        junk = sb("f_junk", [P, D], BF16)
        ss = sb("f_ss", [P, 1], F32)
        lnv = sb("f_lnv", [P, 1], F32)
        rstd = sb("f_rstd", [P, 1], F32)
        wsl = [sb(f"f_w{i}", [P, 8192], BF16) for i in range(3)]
        NT = 2
        t1g = [sb(f"f_t1g{i}", [P, 512], F32) for i in range(NT)]
        t1v = [sb(f"f_t1v{i}", [P, 512], F32) for i in range(NT)]
        sg = [sb(f"f_sg{i}", [P, 512], F32) for i in range(NT)]
        bk = [psb(f"f_ps{i}") for i in range(8)]
        nt_tiles = (junk, ss, lnv, rstd, hb)
        wq = kb.sem_pool("f_wq", 3)
        xq = kb.sem_pool("f_xq", 4)
        oq = kb.sem_pool("f_oq", 4)

        kb.op(kb.dve, [], [h2Th], lambda: nc.vector.memset(h2Th[:, :, :], 0.0))
        wi = 0
        xi = 0
        for ti in tiles:
            r0 = ti * 512
            if ti != tiles[0]:
                kb.op(kb.dve, [h2T], [h2Th],
                      lambda: nc.vector.tensor_copy(out=h2Th[:, :, 0:1], in_=h2T[:, :, 511:512]))
            xs = []
            for b in range(4):
                xt = xb[xi % 5]
                xi += 1
                st, sap = x1_rows(r0 + b * P, P)
                kb.dma(kb.sp, xq, xt[:, :], sap, [st], [xt])
                xs.append(xt)
                rmsnorm_to_T(kb, nt_tiles, xt, xt[:, :], P, CV_G2, h2T,
                             lambda k0, nk, b=b: h2T[:, k0:k0 + nk, b * P:(b + 1) * P],
                             [bk[5], bk[6], bk[7]], (ident, cvec))
            xh = xb[xi % 5]
            xi += 1
            st, sap = x1_rows(r0 + 512, 1)
            kb.dma(kb.sp, xq, xh[0:1, :], sap, [st], [xh])
            rmsnorm_to_T(kb, nt_tiles, xh, xh[0:1, :], 1, CV_G2, h2Th,
                         lambda k0, nk: h2Th[:, k0:k0 + nk, 1:2],
                         [bk[5], bk[6], bk[7]], (ident, cvec))
            hbank = bk[2]
            for jp in range(NJ // 2):
                w = wsl[wi % 3]
                wi += 1
                kb.dma(kb.pool, wq, w[:, :], w_up[0][jp], [w_up[1]], [w], max_dma_last_dim=4096)
                wv = w[:, :].rearrange("p (j k g c) -> p j k g c", j=2, k=KD, g=2)
                for jj in range(2):
                    j = 2 * jp + jj
                    s = j % 2
                    pg, pv = (bk[0], bk[1]) if s == 0 else (bk[3], bk[4])
                    hc = 4 * s
                    for gv, pt in ((0, pg), (1, pv)):
                        for k in range(KD):
                            kb.mm(pt, pt[:, :], w, wv[:, jj, k, gv, :], h2T, h2T[:, k, :],
                                  start=(k == 0), stop=(k == KD - 1))
                            kb.mm(hbank, hbank[:, hc + 2 * gv:hc + 2 * gv + 2], w, wv[:, jj, k, gv, :],
                                  h2Th, h2Th[:, k, :], start=(k == 0), stop=(k == KD - 1))
                    tg, tv, sgt = t1g[j % NT], t1v[j % NT], sg[j % NT]
                    for gv, pt, tt, m in ((0, pg, tg, j), (1, pv, tv, NJ + j)):
                        c0 = cvec[:, CV_CW0 + m:CV_CW0 + m + 1]
                        c1 = cvec[:, CV_CW1 + m:CV_CW1 + m + 1]
                        c2 = cvec[:, CV_CW2 + m:CV_CW2 + m + 1]
                        cb = cvec[:, CV_CB + m:CV_CB + m + 1]
                        hl = hbank[:, hc + 2 * gv:hc + 2 * gv + 1]
                        hr = hbank[:, hc + 2 * gv + 1:hc + 2 * gv + 2]
                        kb.op(kb.act, [pt, cvec], [tt],
                              lambda pt=pt, tt=tt, c1=c1, cb=cb: nc.scalar.activation(
                                  out=tt[:, :], in_=pt[:, :], func=AF.Identity, scale=c1, bias=cb))
                        kb.op(kb.dve, [pt, cvec, tt], [tt],
                              lambda pt=pt, tt=tt, c0=c0: nc.vector.scalar_tensor_tensor(
                                  out=tt[:, 1:512], in0=pt[:, 0:511], scalar=c0, in1=tt[:, 1:512],
                                  op0=ALU.mult, op1=ALU.add))
                        kb.op(kb.dve, [pt, cvec, tt], [tt],
                              lambda pt=pt, tt=tt, c2=c2: nc.vector.scalar_tensor_tensor(
                                  out=tt[:, 0:511], in0=pt[:, 1:512], scalar=c2, in1=tt[:, 0:511],
                                  op0=ALU.mult, op1=ALU.add))
                        kb.op(kb.dve, [hbank, cvec, tt], [tt],
                              lambda tt=tt, hl=hl, c0=c0: nc.vector.scalar_tensor_tensor(
                                  out=tt[:, 0:1], in0=hl, scalar=c0, in1=tt[:, 0:1],
                                  op0=ALU.mult, op1=ALU.add))
                        kb.op(kb.dve, [hbank, cvec, tt], [tt],
                              lambda tt=tt, hr=hr, c2=c2: nc.vector.scalar_tensor_tensor(
                                  out=tt[:, 511:512], in0=hr, scalar=c2, in1=tt[:, 511:512],
                                  op0=ALU.mult, op1=ALU.add))
                    kb.op(kb.act, [tg], [sgt],
                          lambda tg=tg, sgt=sgt: nc.scalar.activation(out=sgt[:, :], in_=tg[:, :],
                                                                      func=AF.Exp, scale=-1.0))
                    kb.op(kb.act, [sgt, kb.one_t], [sgt],
                          lambda sgt=sgt: nc.scalar.activation(out=sgt[:, :], in_=sgt[:, :],
                                                               func=AF.Ln, bias=kb.one_t[:, :]))
                    kb.op(kb.act, [sgt], [sgt],
                          lambda sgt=sgt: nc.scalar.activation(out=sgt[:, :], in_=sgt[:, :],
                                                               func=AF.Exp, scale=-1.0))
                    kb.op(kb.dve, [tg, sgt], [sgt],
                          lambda tg=tg, sgt=sgt: nc.vector.tensor_tensor(out=sgt[:, :], in0=tg[:, :],
                                                                         in1=sgt[:, :], op=ALU.mult))
                    kb.op(kb.dve, [tv, sgt], [aT],
                          lambda tv=tv, sgt=sgt, j=j: nc.vector.tensor_tensor(out=aT[:, j, :], in0=tv[:, :],
                                                                              in1=sgt[:, :], op=ALU.mult))
            dbk = [bk[0], bk[1], bk[3], bk[4]]
            for c in range(4):
                for q in range(4):
                    w = wsl[wi % 3]
                    wi += 1
                    kb.dma(kb.pool, wq, w[:, 0:11 * 512], w_down[0][c * 4 + q], [w_down[1]], [w],
                           max_dma_last_dim=4096)
                    wv = w[:, 0:11 * 512].rearrange("p (k c) -> p k c", k=11)
                    for b in range(4):
                        for kk in range(11):
                            j = q * 11 + kk
                            kb.mm(dbk[b], dbk[b][:, :], aT, aT[:, j, b * P:(b + 1) * P], w, wv[:, kk, :],
                                  start=(j == 0), stop=(j == NJ - 1))
                for b in range(4):
                    xt = xs[b]
                    kb.op(kb.dve, [dbk[b], xt], [xt],
                          lambda b=b, xt=xt, c=c: nc.vector.tensor_tensor(
                              out=xt[:, c * 512:(c + 1) * 512], in0=dbk[b][:, :],
                              in1=xt[:, c * 512:(c + 1) * 512], op=ALU.add))
            for b in range(4):
                ot, oap = out_rows(r0 + b * P, P)
                kb.dma(kb.sp, oq, oap, xs[b][:, :], [xs[b]], [ot])
        kb.barrier()


def phase_mixer(kb, cst, dr, fwd):
    nc = kb.nc
    cvec, cmat, rows = cst["cvec"], cst["cmat"], cst["rows"]
    ident = cmat
    dr_in = dr["dr_in"]
    with contextlib.ExitStack() as es:
        def sb(name, shape, dt):
            return Tl(es.enter_context(nc.sbuf_tensor(name, shape, dt)), name)

        pfx = "c_" if fwd else "b_"
        banks = [Tl(es.enter_context(nc.psum_tensor(f"{pfx}ps{i}", [P, 512], F32)), f"ps{i}") for i in range(8)]
        bi = [0]

        def nb():
            t = banks[bi[0] % 8]
            bi[0] += 1
            return t

        hT = sb(pfx + "hT", [P, KD, 5 * P], BF16)
        xin = [sb(pfx + f"x{i}", [P, D], F32) for i in range(2)]
        hb = None
        junk = sb(pfx + "junk", [P, D], BF16)
        ss = sb(pfx + "ss", [P, 1], F32)
        lnv = sb(pfx + "lnv", [P, 1], F32)
        rstd = sb(pfx + "rstd", [P, 1], F32)
        nt_tiles = (junk, ss, lnv, rstd, hb)
        NW = 2
        wsl = [sb(pfx + f"w{i}", [P, 8192], BF16) for i in range(NW)]
        big = [sb(pfx + f"big{i}", [P, 1024], BF16) for i in range(4)]
        qeT = [sb(pfx + f"qeT{i}", [P, 512], BF16) for i in range(4)]
        keT = [sb(pfx + f"keT{i}", [P, 512], BF16) for i in range(4)]
        ke = [sb(pfx + f"ke{i}", [P, 512], BF16) for i in range(4)]
        vg = [sb(pfx + f"vg{i}", [P, 1024], BF16) for i in range(4)]
        gsc = [sb(pfx + f"gsc{i}", [P, 4, 6], F32) for i in range(4)]
        S = sb(pfx + "S", [P, 4, 256], F32)
        Sbf = [sb(pfx + f"Sbf{h}", [P, 256], BF16) for h in range(4)]
        lr_sb = sb(pfx + "lr", [P, 32], F32)
        lrT = sb(pfx + "lrT", [33, P], F32)
        spt = sb(pfx + "spt", [P, 512], F32)
        bsm = sb(pfx + "bsm", [P, 4, 4], F32)
        dlt = sb(pfx + "dlt", [P, 4, 2], F32)
        tA = [sb(pfx + f"tA{i}", [P, 512], F32) for i in range(2)]
        aTs = sb(pfx + "aTs", [P, 512], BF16)
        oin = sb(pfx + "oin", [P, 1024], F32)
        obuf = [sb(pfx + f"ob{i}", [P, 1024], BF16) for i in range(2)]
        wq = kb.sem_pool(pfx + "wq", 2)
        xq = kb.sem_pool(pfx + "xq", 4)
        oq = kb.sem_pool(pfx + "oq", 4)
        if fwd:
            qT = [sb(f"c_qT{i}", [P, 1024], BF16) for i in range(4)]
            kT = [sb(f"c_kT{i}", [P, 256], BF16) for i in range(6)]
            va = [sb(f"c_va{i}", [P, 256], BF16) for i in range(6)]
            mixT = [sb(f"c_mixT{i}", [P, KD, P], BF16) for i in range(4)]
            csb = [sb(f"c_cs{i}", [P, 256], F32) for i in range(2)]
            Gq = [sb(f"c_Gq{i}", [P, 256], F32) for i in range(2)]
            Gk = [sb(f"c_Gk{i}", [P, 256], F32) for i in range(2)]
            tB = [sb(f"c_tB{i}", [P, 512], F32) for i in range(2)]
            tq = [sb(f"c_tq{i}", [P, 512], F32) for i in range(2)]
            ssq = sb("c_ssq", [P, 4], F32)
            lnq = sb("c_lnq", [P, 4], F32)
            rsq = sb("c_rsq", [P, 4], F32)
            PT = [sb(f"c_PT{i}", [P, 512], BF16) for i in range(3)]
            rec = sb("c_rec", [P, 512], F32)
            osum = sb("c_osum", [P, 1024], F32)
            xr = [sb(f"c_xr{i}", [P, 512], F32) for i in range(2)]
            xo = [sb(f"c_xo{i}", [P, 512], F32) for i in range(2)]
            rq = kb.sem_pool("c_rq", 3)
        AFT = cmat[:, CM_AFF:CM_AFF + P] if fwd else cmat[:, CM_AFB:CM_AFB + P]
        GMT = cmat[:, CM_GMF:CM_GMF + P] if fwd else cmat[:, CM_GMB:CM_GMB + P]
        IND = cmat[:, CM_INF:CM_INF + 4] if fwd else cmat[:, CM_INB:CM_INB + 4]
        W2 = cst["W2"]
        w2c = 0 if fwd else 512
        wlr = cst["wlr"]

        kb.op(kb.dve, [], [S], lambda: nc.vector.memset(S[:, :, :], 0.0))
        kb.op(kb.dve, [], [lrT], lambda: nc.vector.memset(lrT[:, :], 1.0))
        cnt = dict(w=0, x=0, t=0, ob=0, xr=0)

        def load_w(src_ap, ncols=8192):
            w = wsl[cnt["w"] % NW]
            cnt["w"] += 1
            kb.dma(kb.pool, wq, w[:, 0:ncols], src_ap, [dr_in], [w], max_dma_last_dim=4096)
            return w

        def proj(i, w, wv_fn, ncols):
            bk = nb()
            for k in range(KD):
                kb.mm(bk, bk[:, 0:ncols], hT, hT[:, k, i * P:(i + 1) * P], w, wv_fn(k),
                      start=(k == 0), stop=(k == KD - 1))
            return bk

        def gating(i):
            bk = proj(i, wlr, lambda k: wlr[:, k * 32:(k + 1) * 32], 32)
            kb.op(kb.act, [bk], [lr_sb], lambda: nc.scalar.copy(out=lr_sb[:, :], in_=bk[:, 0:32]))
            b2 = nb()
            kb.tr(b2, b2[0:32, 0:P], lr_sb, lr_sb[:, :], ident, ident[:, 0:P])
            kb.op(kb.act, [b2], [lrT], lambda: nc.scalar.copy(out=lrT[0:32, :], in_=b2[0:32, 0:P]))
            b3 = nb()
            kb.mm(b3, b3[:, :], lrT, lrT[0:33, :], W2, W2[0:33, w2c:w2c + 512])
            kb.op(kb.act, [b3], [spt],
                  lambda: nc.scalar.activation(out=spt[:, :], in_=b3[:, :], func=AF.Exp, scale=-1.0))
            kb.op(kb.act, [spt, kb.one_t], [spt],
                  lambda: nc.scalar.activation(out=spt[:, :], in_=spt[:, :], func=AF.Ln, bias=kb.one_t[:, :]))
            b4 = nb()
            kb.mm(b4, b4[:, :], cmat, AFT, spt, spt[:, :])
            kb.op(kb.act, [b4], [big[i]],
                  lambda: nc.scalar.activation(out=big[i][:, 0:512], in_=b4[:, :], func=AF.Exp, scale=-1.0 / 16))
            kb.op(kb.act, [b4], [big[i]],
                  lambda: nc.scalar.activation(out=big[i][:, 512:1024], in_=b4[:, :], func=AF.Exp, scale=1.0 / 16))
            b5 = nb()
            for h in range(4):
                kb.mm(b5, b5[:, 4 * h:4 * h + 4], spt, spt[:, h * P:(h + 1) * P], cmat, IND)
            kb.op(kb.act, [b5], [bsm],
                  lambda: nc.scalar.copy(out=bsm[:, :, :], in_=b5[:, 0:16].rearrange("p (h c) -> p h c", c=4)))
            kb.op(kb.dve, [bsm], [dlt],
                  lambda: nc.vector.tensor_tensor(out=dlt[:, :, :], in0=bsm[:, :, 0:2], in1=bsm[:, :, 2:4],
                                                  op=ALU.subtract))
            kb.op(kb.act, [bsm], [gsc[i]],
                  lambda: nc.scalar.activation(out=gsc[i][:, :, 0:4], in_=bsm[:, :, :], func=AF.Exp,
                                               scale=-1.0 / 16))
            kb.op(kb.act, [dlt], [gsc[i]],
                  lambda: nc.scalar.activation(out=gsc[i][:, :, 4:6], in_=dlt[:, :, :], func=AF.Exp,
                                               scale=-1.0 / 16))

        def to_T(src_t, nchunks, dst_t, dst_ap):
            b2 = nb()
            for h in range(nchunks):
                kb.tr(b2, b2[:, h * P:(h + 1) * P], src_t, src_t[:, h * P:(h + 1) * P], ident, ident[:, 0:P])
            kb.op(kb.act, [b2], [dst_t], lambda: nc.scalar.copy(out=dst_ap, in_=b2[:, 0:nchunks * P]))

        def cons_qg(i, bk):
            t = tA[cnt["t"] % 2]
            cnt["t"] += 1
            kb.op(kb.dve, [bk, big[i]], [t],
                  lambda: nc.vector.scalar_tensor_tensor(out=t[:, :], in0=bk[:, :], scalar=QS,
                                                         in1=big[i][:, 0:512], op0=ALU.mult, op1=ALU.mult))
            to_T(t, 4, qeT[i], qeT[i][:, :])

        def cons_kg(i, bk):
            t = tA[cnt["t"] % 2]
            cnt["t"] += 1
            kb.op(kb.dve, [bk, big[i]], [t],
                  lambda: nc.vector.tensor_tensor(out=t[:, :], in0=bk[:, :], in1=big[i][:, 512:1024],
                                                  op=ALU.mult))
            kb.op(kb.act, [t], [ke[i]], lambda: nc.scalar.copy(out=ke[i][:, :], in_=t[:, :]))
            to_T(t, 4, keT[i], keT[i][:, :])

        def cons_vg(i, bk, c):
            kb.op(kb.act, [bk], [vg[i]],
                  lambda: nc.scalar.copy(out=vg[i][:, c * 512:(c + 1) * 512], in_=bk[:, :]))

        def cons_gate(i, bk, c):
            t = tq[cnt["t"] % 2]
            cnt["t"] += 1
            kb.op(kb.act, [bk], [t],
                  lambda: nc.scalar.activation(out=t[:, :], in_=bk[:, :], func=AF.Exp, scale=-1.0))
            kb.op(kb.act, [t, kb.one_t], [t],
                  lambda: nc.scalar.activation(out=t[:, :], in_=t[:, :], func=AF.Ln, bias=kb.one_t[:, :]))
            kb.op(kb.act, [t], [t],
                  lambda: nc.scalar.activation(out=t[:, :], in_=t[:, :], func=AF.Exp, scale=-1.0))
            kb.op(kb.dve, [bk, t], [big[i]],
                  lambda: nc.vector.tensor_tensor(out=big[i][:, c * 512:(c + 1) * 512], in0=bk[:, :],
                                                  in1=t[:, :], op=ALU.mult))

        def rope_tables(b):
            cs = csb[b % 2]
            kb.dma(kb.sp, rq, cs[:, :], dr["rope"][b * P:(b + 1) * P, :], [dr_in], [cs])
            for G, g0, gs0 in ((Gq[b % 2], RW_GQ, RW_GQS), (Gk[b % 2], RW_GK, RW_GKS)):
                kb.op(kb.dve, [cs, rows], [G],
                      lambda G=G, g0=g0: nc.vector.tensor_tensor(out=G[:, 0:P], in0=cs[:, 0:P],
                                                                 in1=rows[:, g0:g0 + P], op=ALU.mult))
                kb.op(kb.dve, [cs, rows], [G],
                      lambda G=G, gs0=gs0: nc.vector.tensor_tensor(out=G[:, P:2 * P], in0=cs[:, P:2 * P],
                                                                   in1=rows[:, gs0:gs0 + P], op=ALU.mult))

        def norm_rope(bk, c0, nh, G):
            t = tq[cnt["t"] % 2]
            a = tA[cnt["t"] % 2]
            bb = tB[cnt["t"] % 2]
            cnt["t"] += 1
            n = nh * P
            src = bk[:, c0:c0 + n]
            src3 = src.rearrange("p (h d) -> p h d", d=P)
            kb.op(kb.act, [bk], [t], lambda: nc.scalar.activation(out=t[:, 0:n], in_=src, func=AF.Square))
            kb.op(kb.dve, [t], [ssq],
                  lambda: nc.vector.tensor_reduce(out=ssq[:, 0:nh], in_=t[:, 0:n].rearrange("p (h d) -> p h d", d=P),
                                                  axis=AX.X, op=ALU.add))
            kb.op(kb.act, [ssq, kb.eps_t], [lnq],
                  lambda: nc.scalar.activation(out=lnq[:, 0:nh], in_=ssq[:, 0:nh], func=AF.Ln, scale=1.0 / P,
                                               bias=kb.eps_t[:, :]))
            kb.op(kb.act, [lnq], [rsq],
                  lambda: nc.scalar.activation(out=rsq[:, 0:nh], in_=lnq[:, 0:nh], func=AF.Exp, scale=-0.5))
            a3 = a[:, 0:n].rearrange("p (h d) -> p h d", d=P)
            b3 = bb[:, 0:n].rearrange("p (h d) -> p h d", d=P)
            kb.op(kb.dve, [bk, G], [a],
                  lambda: nc.vector.tensor_tensor(out=a3, in0=src3,
                                                  in1=G[:, 0:P].unsqueeze(1).broadcast_to([P, nh, P]),
                                                  op=ALU.mult))
            kb.op(kb.dve, [bk, G], [bb],
                  lambda: nc.vector.tensor_tensor(out=b3[:, :, 0:64], in0=src3[:, :, 64:128],
                                                  in1=G[:, P:P + 64].unsqueeze(1).broadcast_to([P, nh, 64]),
                                                  op=ALU.mult))
            kb.op(kb.dve, [bk, G], [bb],
                  lambda: nc.vector.tensor_tensor(out=b3[:, :, 64:128], in0=src3[:, :, 0:64],
                                                  in1=G[:, P + 64:2 * P].unsqueeze(1).broadcast_to([P, nh, 64]),
                                                  op=ALU.mult))
            kb.op(kb.dve, [a, bb], [a],
                  lambda: nc.vector.tensor_tensor(out=a[:, 0:n], in0=a[:, 0:n], in1=bb[:, 0:n], op=ALU.add))
            kb.op(kb.dve, [a, rsq], [a],
                  lambda: nc.vector.tensor_tensor(out=a3, in0=a3,
                                                  in1=rsq[:, 0:nh].unsqueeze(2).broadcast_to([P, nh, P]),
                                                  op=ALU.mult))
            return a

        def cons_q(i, b, bk, c):
            a = norm_rope(bk, 0, 4, Gq[b % 2])
            to_T(a, 4, qT[i], qT[i][:, c * 512:(c + 1) * 512])

        def cons_kv(b, bk):
            a = norm_rope(bk, 0, 2, Gk[b % 2])
            to_T(a, 2, kT[b % 6], kT[b % 6][:, :])
            kb.op(kb.act, [bk], [va[b % 6]], lambda: nc.scalar.copy(out=va[b % 6][:, :], in_=bk[:, 256:512]))

        def gla_block(i, b, order, emit):
            bI = bN = None
            if emit:
                bA = nb()
                for h in range(4):
                    kb.mm(bA, bA[:, h * P:(h + 1) * P], keT[i], keT[i][:, h * P:(h + 1) * P],
                          qeT[i], qeT[i][:, h * P:(h + 1) * P])
                kb.op(kb.dve, [bA, cmat], [aTs],
                      lambda: nc.vector.tensor_tensor(out=aTs[:, :].rearrange("p (h i) -> p h i", i=P),
                                                      in0=bA[:, :].rearrange("p (h i) -> p h i", i=P),
                                                      in1=GMT.unsqueeze(1).broadcast_to([P, 4, P]), op=ALU.mult))
                bI = [nb(), nb()]
                for h in range(4):
                    o = bI[h // 2]
                    kb.mm(o, o[:, (h % 2) * 256:(h % 2) * 256 + 256], aTs, aTs[:, h * P:(h + 1) * P],
                          vg[i], vg[i][:, h * 256:(h + 1) * 256])
                bN = [nb(), nb()]
            for c in order:
                r0 = 64 * c
                if emit:
                    for h in range(4):
                        kb.op(kb.act, [S, gsc[i]], [Sbf[h]],
                              lambda h=h: nc.scalar.activation(out=Sbf[h][:, :], in_=S[:, h, :], func=AF.Copy,
                                                               scale=gsc[i][:, h, 2 + c:3 + c]))
                    for h in range(4):
                        o = bN[h // 2]
                        kb.mm(o, o[r0:r0 + 64, (h % 2) * 256:(h % 2) * 256 + 256],
                              qeT[i], qeT[i][:, h * P + r0:h * P + r0 + 64], Sbf[h], Sbf[h][:, :])
                bM = [nb(), nb()]
                for h in range(4):
                    o = bM[h // 2]
                    kb.mm(o, o[:, (h % 2) * 256:(h % 2) * 256 + 256], ke[i], ke[i][r0:r0 + 64, h * P:(h + 1) * P],
                          vg[i], vg[i][r0:r0 + 64, h * 256:(h + 1) * 256])
                for h in range(4):
                    o = bM[h // 2]
                    kb.op(kb.dve, [S, gsc[i]], [S],
                          lambda h=h: nc.vector.tensor_scalar(out=S[:, h, :], in0=S[:, h, :],
                                                              scalar1=gsc[i][:, h, c:c + 1], scalar2=None,
                                                              op0=ALU.mult))
                    kb.op(kb.dve, [o, S, gsc[i]], [S],
                          lambda h=h, o=o: nc.vector.scalar_tensor_tensor(
                              out=S[:, h, :], in0=o[:, (h % 2) * 256:(h % 2) * 256 + 256],
                              scalar=gsc[i][:, h, 4 + c:5 + c], in1=S[:, h, :], op0=ALU.mult, op1=ALU.add))
            return bI, bN

        def attention(i, b):
            for h2 in range(2):
                kbs = [kk for kk in (b - 1, b, b + 1) if kk >= 0]
                bD = nb()
                bO = nb()
                for n_, kk in enumerate(kbs):
                    bS = nb()
                    masked = kk != b
                    kb.mm(bS, bS[:, :], kT[kk % 6], kT[kk % 6][:, h2 * P:(h2 + 1) * P],
                          qT[i], qT[i][:, h2 * 512:(h2 + 1) * 512], start=True, stop=not masked)
                    if masked:
                        mk = cst["amask"]
                        mc = 0 if kk < b else 512
                        kb.mm(bS, bS[:, :], cst["identb"], cst["identb"][:, :], mk, mk[:, mc:mc + 512],
                              start=False, stop=True)
                    pt = PT[n_]
                    kb.op(kb.act, [bS], [pt],
                          lambda pt=pt, bS=bS: nc.scalar.activation(out=pt[:, :], in_=bS[:, :], func=AF.Exp,
                                                                    scale=QS))
                for n_, kk in enumerate(kbs):
                    kb.mm(bD, bD[:, :], cst["onesb"], cst["onesb"][:, :], PT[n_], PT[n_][:, :],
                          start=(n_ == 0), stop=False)
                kb.mm(bD, bD[:, :], cst["onesf"], cst["onesf"][0:1, :], cst["esink"],
                      cst["esink"][0:1, h2 * 512:(h2 + 1) * 512], start=False, stop=True)
                for n_, kk in enumerate(kbs):
                    kb.mm(bO, bO[:, :], va[kk % 6], va[kk % 6][:, h2 * P:(h2 + 1) * P], PT[n_], PT[n_][:, :],
                          start=(n_ == 0), stop=(n_ == len(kbs) - 1))
                kb.op(kb.dve, [bD], [rec], lambda: nc.vector.reciprocal(out=rec[:, :], in_=bD[:, :]))
                kb.op(kb.dve, [bO, rec], [mixT[i]],
                      lambda: nc.vector.tensor_tensor(
                          out=mixT[i][:, 4 * h2:4 * h2 + 4, :], in0=bO[:, :].rearrange("p (h t) -> p h t", t=P),
                          in1=rec[:, :].rearrange("p (h t) -> p h t", t=P), op=ALU.mult))

        def combine(i, b, bI, bN):
            ob = obuf[cnt["ob"] % 2]
            cnt["ob"] += 1
            kb.dma(kb.sp, xq, ob[:, :], dr["obs"][b * P:(b + 1) * P, :], [dr["obs_t"][b]], [ob])
            for hh in range(2):
                kb.op(kb.act, [bN[hh]], [oin],
                      lambda hh=hh: nc.scalar.copy(out=oin[:, hh * 512:(hh + 1) * 512], in_=bN[hh][:, :]))
                kb.op(kb.dve, [bI[hh], oin], [osum],
                      lambda hh=hh: nc.vector.tensor_tensor(out=osum[:, hh * 512:(hh + 1) * 512], in0=bI[hh][:, :],
                                                            in1=oin[:, hh * 512:(hh + 1) * 512], op=ALU.add))
            kb.op(kb.dve, [osum, ob], [osum],
                  lambda: nc.vector.tensor_tensor(out=osum[:, :], in0=osum[:, :], in1=ob[:, :], op=ALU.add))
            for h in range(4):
                kb.op(kb.act, [osum], [junk, ssq],
                      lambda h=h: nc.scalar.activation(out=junk[:, 0:256], in_=osum[:, h * 256:(h + 1) * 256],
                                                       func=AF.Square, accum_out=ssq[:, h:h + 1]))
            kb.op(kb.act, [ssq, kb.eps_t], [lnq],
                  lambda: nc.scalar.activation(out=lnq[:, :], in_=ssq[:, :], func=AF.Ln, scale=1.0 / 256,
                                               bias=kb.eps_t[:, :]))
            kb.op(kb.act, [lnq], [rsq],
                  lambda: nc.scalar.activation(out=rsq[:, :], in_=lnq[:, :], func=AF.Exp, scale=-0.5))
            for h in range(4):
                kb.op(kb.dve, [osum, rsq, big[i]], [osum],
                      lambda h=h: nc.vector.scalar_tensor_tensor(
                          out=osum[:, h * 256:(h + 1) * 256], in0=osum[:, h * 256:(h + 1) * 256],
                          scalar=rsq[:, h:h + 1], in1=big[i][:, h * 256:(h + 1) * 256],
                          op0=ALU.mult, op1=ALU.mult))
            kb.op(kb.dve, [osum, rows], [osum],
                  lambda: nc.vector.tensor_tensor(
                      out=osum[:, :].rearrange("p (h e) -> p h e", e=256),
                      in0=osum[:, :].rearrange("p (h e) -> p h e", e=256),
                      in1=rows[:, RW_GOG:RW_GOG + 256].unsqueeze(1).broadcast_to([P, 4, 256]), op=ALU.mult))
            for hh in range(2):
                b2 = nb()
                for q in range(4):
                    kb.tr(b2, b2[:, q * P:(q + 1) * P], osum, osum[:, (hh * 4 + q) * P:(hh * 4 + q + 1) * P],
                          ident, ident[:, 0:P])
                kb.op(kb.act, [b2], [mixT[i]],
                      lambda hh=hh, b2=b2: nc.scalar.copy(out=mixT[i][:, 8 + 4 * hh:12 + 4 * hh, :],
                                                          in_=b2[:, :].rearrange("p (k t) -> p k t", t=P)))

        def load_norm(b, i):
            xt = xin[cnt["x"] % 2]
            cnt["x"] += 1
            kb.dma(kb.sp, xq, xt[:, :], dr["x"][b * P:(b + 1) * P, :], [dr_in], [xt])
            rmsnorm_to_T(kb, nt_tiles, xt, xt[:, :], P, CV_G1, hT,
                         lambda k0, nk: hT[:, k0:k0 + nk, i * P:(i + 1) * P],
                         [nb(), nb(), nb(), nb()], (ident, cvec))

        w_in = dr["w_in"]

        def chunk(c, idxs, cons):
            w = load_w(w_in[c])
            wv = w[:, :].rearrange("p (k c) -> p k c", k=KD)
            for i in idxs:
                bk = proj(i, w, lambda k: wv[:, k, :], 512)
                cons(i, bk)

        if not fwd:
            tiles = [list(range(t * 4 + 3, t * 4 - 1, -1)) for t in range(7, -1, -1)]
            for blocks in tiles:
                emits = [b <= NXB - 1 for b in blocks]
                for i, b in enumerate(blocks):
                    load_norm(b, i)
                for i, b in enumerate(blocks):
                    gating(i)
                if any(emits):
                    chunk(3, [i for i in range(4) if emits[i]], cons_qg)
                chunk(4, range(4), cons_kg)
                chunk(5, range(4), lambda i, bk: cons_vg(i, bk, 0))
                chunk(6, range(4), lambda i, bk: cons_vg(i, bk, 1))
                for i, b in enumerate(blocks):
                    bI, bN = gla_block(i, b, (1, 0), emits[i])
                    if emits[i]:
                        ob = obuf[cnt["ob"] % 2]
                        cnt["ob"] += 1
                        for hh in range(2):
                            kb.op(kb.act, [bN[hh]], [oin],
                                  lambda hh=hh: nc.scalar.copy(out=oin[:, hh * 512:(hh + 1) * 512],
                                                               in_=bN[hh][:, :]))
                            kb.op(kb.dve, [bI[hh], oin], [ob],
                                  lambda hh=hh: nc.vector.tensor_tensor(
                                      out=ob[:, hh * 512:(hh + 1) * 512], in0=bI[hh][:, :],
                                      in1=oin[:, hh * 512:(hh + 1) * 512], op=ALU.add))
                        kb.dma(kb.sp, oq, dr["obs"][b * P:(b + 1) * P, :], ob[:, :], [ob], [dr["obs_t"][b]])
        else:
            tiles = [[0, 1, 2, 3], [4, 5, 6, 7], [8, 9, 10, 11], [12, 13, 14, 15], [16]]
            for blocks in tiles:
                nbk = len(blocks)
                ext = blocks[-1] + 1
                for i, b in enumerate(blocks + [ext]):
                    load_norm(b, i)
                for i, b in enumerate(blocks):
                    gating(i)
                chunk(3, range(nbk), cons_qg)
                chunk(4, range(nbk), cons_kg)
                chunk(5, range(nbk), lambda i, bk: cons_vg(i, bk, 0))
                chunk(6, range(nbk), lambda i, bk: cons_vg(i, bk, 1))
                chunk(7, range(nbk), lambda i, bk: cons_gate(i, bk, 0))
                chunk(8, range(nbk), lambda i, bk: cons_gate(i, bk, 1))
                w = load_w(w_in[2])
                wv = w[:, :].rearrange("p (k c) -> p k c", k=KD)
                for i, b in enumerate(blocks + [ext]):
                    rope_tables(b)
                    bk = proj(i, w, lambda k: wv[:, k, :], 512)
                    cons_kv(b, bk)
                for c in (0, 1):
                    w = load_w(w_in[c])
                    wv = w[:, :].rearrange("p (k c) -> p k c", k=KD)
                    for i, b in enumerate(blocks):
                        rope_tables(b)
                        bk = proj(i, w, lambda k: wv[:, k, :], 512)
                        cons_q(i, b, bk, c)
                for i, b in enumerate(blocks):
                    attention(i, b)
                    bI, bN = gla_block(i, b, (0, 1), True)
                    combine(i, b, bI, bN)
                for c in range(4):
                    w = load_w(dr["w_out"][c])
                    wv = w[:, :].rearrange("p (k c) -> p k c", k=KD)
                    for i, b in enumerate(blocks):
                        xrt = xr[cnt["xr"] % 2]
                        xot = xo[cnt["xr"] % 2]
                        cnt["xr"] += 1
                        kb.dma(kb.sp, xq, xrt[:, :], dr["x"][b * P:(b + 1) * P, c * 512:(c + 1) * 512],
                               [dr_in], [xrt])
                        bk = nb()
                        for k in range(KD):
                            kb.mm(bk, bk[:, :], mixT[i], mixT[i][:, k, :], w, wv[:, k, :],
                                  start=(k == 0), stop=(k == KD - 1))
                        kb.op(kb.dve, [bk, xrt], [xot],
                              lambda bk=bk, xrt=xrt, xot=xot: nc.vector.tensor_tensor(
                                  out=xot[:, :], in0=bk[:, :], in1=xrt[:, :], op=ALU.add))
                        kb.dma(kb.sp, oq, dr["x1s"][b * P:(b + 1) * P, c * 512:(c + 1) * 512], xot[:, :],
                               [xot], [dr["x1_t"][(b, c)]])
        kb.barrier()


def build_program(mode="full"):
    nc = bass.Bass("TRN2", target_bir_lowering=False)

    def din(name, shape, dt=F32):
        return nc.dram_tensor(name, shape, dt, kind="ExternalInput").ap()

    dr = {}
    dr["x"] = din("x", [SEQ, D])
    dr["w_in"] = din("w_in", [9, P, KD * 512])
    wlr_d = din("w_lr", [P, KD * 32])
    w2_d = din("w2", [33, 1024])
    dr["w_out"] = din("w_out", [4, P, KD * 512])
    w_up = din("w_up", [NJ // 2, P, 2 * KD * 2 * P])
    w_down = din("w_down", [16, P, 11 * 512])
    cvec_d = din("cvec", [P, NCV])
    cmat_d = din("cmat", [P, NCM])
    rows_d = din("rows", [1, NRW])
    sink_d = din("sink", [1, 8])
    dr["rope"] = din("rope", [18 * P, 256])
    amask_d = din("amask", [P, 1024])
    out = nc.dram_tensor("out", [NOWN * P, D], F32, kind="ExternalOutput").ap()
    dr["x1s"] = nc.dram_tensor("x1s", [NXB * P, D], F32, kind="Internal").ap()
    dr["obs"] = nc.dram_tensor("obs", [NXB * P, 1024], BF16, kind="Internal").ap()
    dbg = None
    if mode.startswith("dbg"):
        dbg = nc.dram_tensor("dbg", [NXB * P, D], F32, kind="ExternalOutput").ap()

    with contextlib.ExitStack() as es:
        kb = KB(nc, es)

        def sb(name, shape, dt):
            return Tl(es.enter_context(nc.sbuf_tensor(name, shape, dt)), name)

        cst = {}
        cst["cvec"] = sb("c_cvec", [P, NCV], F32)
        cst["cmat"] = sb("c_cmat", [P, NCM], F32)
        cst["rows"] = sb("c_rows", [P, NRW], F32)
        cst["W2"] = sb("c_W2", [33, 1024], F32)
        cst["wlr"] = sb("c_wlr", [P, KD * 32], BF16)
        cst["amask"] = sb("c_amask", [P, 1024], BF16)
        cst["identb"] = sb("c_identb", [P, P], BF16)
        cst["onesb"] = sb("c_onesb", [P, P], BF16)
        cst["onesf"] = sb("c_onesf", [1, P], F32)
        cst["esink"] = sb("c_esink", [1, 1024], F32)
        sink_sb = sb("c_sink", [1, 8], F32)
        kb.eps_t = sb("c_eps", [P, 1], F32)
        kb.one_t = sb("c_one", [P, 1], F32)
        cq = kb.sem_pool("cq", 4)
        cq2 = kb.sem_pool("cq2", 3)
        dr_in = Tl(None, "dram_in")
        dr["dr_in"] = dr_in
        dr["obs_t"] = [Tl(None, f"obs{b}") for b in range(NXB)]
        dr["x1_t"] = {(b, c): Tl(None, f"x1_{b}_{c}") for b in range(NXB) for c in range(4)}
        kb.dma(kb.sp, cq, cst["cvec"][:, :], cvec_d, [dr_in], [cst["cvec"]])
        kb.dma(kb.sp, cq, cst["cmat"][:, :], cmat_d, [dr_in], [cst["cmat"]])
        kb.dma(kb.sp, cq, cst["rows"][:, :], rows_d.partition_broadcast(P), [dr_in], [cst["rows"]])
        kb.dma(kb.sp, cq, cst["W2"][:, :], w2_d, [dr_in], [cst["W2"]])
        kb.dma(kb.sp, cq, sink_sb[:, :], sink_d, [dr_in], [sink_sb])
        kb.dma(kb.pool, cq2, cst["wlr"][:, :], wlr_d, [dr_in], [cst["wlr"]])
        kb.dma(kb.pool, cq2, cst["amask"][:, :], amask_d, [dr_in], [cst["amask"]])
        kb.dma(kb.pool, cq2, cst["identb"][:, :], cmat_d[:, CM_ID:CM_ID + P], [dr_in], [cst["identb"]])
        kb.op(kb.dve, [], [kb.eps_t], lambda: nc.vector.memset(kb.eps_t[:, :], EPS))
        kb.op(kb.dve, [], [kb.one_t], lambda: nc.vector.memset(kb.one_t[:, :], 1.0))
        kb.op(kb.dve, [], [cst["onesb"]], lambda: nc.vector.memset(cst["onesb"][:, :], 1.0))
        kb.op(kb.dve, [], [cst["onesf"]], lambda: nc.vector.memset(cst["onesf"][:, :], 1.0))
        kb.op(kb.act, [sink_sb], [sink_sb],
              lambda: nc.scalar.activation(out=sink_sb[:, :], in_=sink_sb[:, :], func=AF.Exp))
        kb.op(kb.dve, [sink_sb], [cst["esink"]],
              lambda: nc.vector.tensor_copy(out=cst["esink"][:, :].rearrange("p (h t) -> p h t", t=P),
                                            in_=sink_sb[:, :].unsqueeze(2).broadcast_to([1, 8, P])))

        out_tl = {}

        def out_rows(r0, n):
            if r0 not in out_tl:
                out_tl[r0] = Tl(None, f"dram_out{r0}")
            return out_tl[r0], out[r0:r0 + n, :]

        if mode == "ffn":
            def x1_rows(r0, n):
                return dr_in, dr["x"][r0:r0 + n, :]
        else:
            x1_rt = Tl(None, "x1_all")

            def x1_rows(r0, n):
                return x1_rt, dr["x1s"][r0:r0 + n, :]

        if mode in ("full", "dbg_bwd", "dbg_mix"):
            phase_mixer(kb, cst, dr, fwd=False)
        if mode in ("full", "dbg_mix"):
            phase_mixer(kb, cst, dr, fwd=True)
        if mode in ("full", "ffn"):
            phase_ffn(kb, cst, x1_rows, out_rows, (w_up, dr_in), (w_down, dr_in))
        if mode == "dbg_bwd":
            with nc.sbuf_tensor("dbg_t", [P, 1024], BF16) as t0, nc.sbuf_tensor("dbg_f", [P, 1024], F32) as t1:
                tt0, tt1 = Tl(t0), Tl(t1)
                dq = kb.sem_pool("dq", 2)
                for b in range(NXB):
                    kb.dma(kb.sp, dq, tt0[:, :], dr["obs"][b * P:(b + 1) * P, :], [dr_in], [tt0])
                    kb.op(kb.dve, [tt0], [tt1], lambda: nc.vector.tensor_copy(out=tt1[:, :], in_=tt0[:, :]))
                    kb.dma(kb.sp, dq, dbg[b * P:(b + 1) * P, 0:1024], tt1[:, :], [tt1], [dr_in])
                kb.barrier()
        if mode == "dbg_mix":
            with nc.sbuf_tensor("dbg_t", [P, D], F32) as t0:
                tt0 = Tl(t0)
                dq = kb.sem_pool("dq", 2)
                for b in range(NXB):
                    kb.dma(kb.sp, dq, tt0[:, :], dr["x1s"][b * P:(b + 1) * P, :], [dr_in], [tt0])
                    kb.dma(kb.sp, dq, dbg[b * P:(b + 1) * P, :], tt0[:, :], [tt0], [dr_in])
                kb.barrier()
        print(f"[build] mode={mode} instructions={kb.nins} waits={kb.nwait} sems={kb.nsem}")
    return nc


def host_layout_common(inp):
    f = lambda a: np.asarray(a, dtype=np.float32)
    w_in = f(inp["w_in"][0])
    w_out = f(inp["w_out"][0])
    w_up = f(inp["w_up"][0])
    w_down = f(inp["w_down"][0])
    wi = w_in[:, :4608].reshape(KD, P, 9, 512)
    w_in_l = np.ascontiguousarray(wi.transpose(2, 1, 0, 3)).reshape(9, P, KD * 512)
    wo = w_out.reshape(KD, P, 4, 512)
    w_out_l = np.ascontiguousarray(wo.transpose(2, 1, 0, 3)).reshape(4, P, KD * 512)
    wu = w_up.reshape(KD, P, 2, NJ // 2, 2, P)
    w_up_l = np.ascontiguousarray(wu.transpose(3, 1, 4, 0, 2, 5)).reshape(NJ // 2, P, 2 * KD * 2 * P)
    wd = w_down.reshape(4, 11, P, 4, 512)
    w_down_l = np.ascontiguousarray(wd.transpose(3, 0, 2, 1, 4)).reshape(16, P, 11 * 512)
    return dict(w_in=w_in_l, w_out=w_out_l, w_up=w_up_l, w_down=w_down_l,
                sink=f(inp["attn_sink"][0]).reshape(1, 8).copy())


def host_cvec(inp, flip):
    cv = np.zeros((P, NCV), np.float32)
    cv[:, CV_G1:CV_G1 + 16] = np.asarray(inp["norm1_g"][0]).reshape(KD, P).T
    cv[:, CV_G2:CV_G2 + 16] = np.asarray(inp["norm2_g"][0]).reshape(KD, P).T
    cw = np.asarray(inp["conv_w"][0], dtype=np.float32)
    cb = np.asarray(inp["conv_b"][0], dtype=np.float32)
    taps = (2, 1, 0) if flip else (0, 1, 2)
    cv[:, CV_CW0:CV_CW0 + 88] = cw[taps[0]].reshape(88, P).T
    cv[:, CV_CW1:CV_CW1 + 88] = cw[taps[1]].reshape(88, P).T
    cv[:, CV_CW2:CV_CW2 + 88] = cw[taps[2]].reshape(88, P).T
    cv[:, CV_CB:CV_CB + 88] = cb.reshape(88, P).T
    return cv


def host_cmat(flip=False):
    cm = np.zeros((P, NCM), np.float32)
    cm[:, CM_ID:CM_ID + P] = np.eye(P, dtype=np.float32)
    idx = np.arange(P)
    ch = idx // 64
    l = idx % 64
    same = ch[:, None] == ch[None, :]
    L = same & (l[None, :] <= l[:, None])
    Lref = same & (l[None, :] <= 32)
    U = same & (l[None, :] >= l[:, None])
    Uref = same & (l[None, :] >= 31)
    cm[:, CM_AFF:CM_AFF + P] = (L.astype(np.float32) - Lref.astype(np.float32)).T
    cm[:, CM_AFB:CM_AFB + P] = (U.astype(np.float32) - Uref.astype(np.float32)).T
    f_strict = flip
    b_strict = not flip
    mf = same & ((l[None, :] < l[:, None]) if f_strict else (l[None, :] <= l[:, None]))
    mb = same & ((l[None, :] > l[:, None]) if b_strict else (l[None, :] >= l[:, None]))
    cm[:, CM_GMF:CM_GMF + P] = mf.astype(np.float32).T
    cm[:, CM_GMB:CM_GMB + P] = mb.astype(np.float32).T
    for c in range(2):
        cm[:, CM_INF + c] = (ch == c)
        cm[:, CM_INF + 2 + c] = (ch == c) & (l <= 32)
        cm[:, CM_INB + c] = (ch == c)
        cm[:, CM_INB + 2 + c] = (ch == c) & (l >= 31)
    return cm


def host_core_consts(inp, flip):
    f = lambda a: np.asarray(a, dtype=np.float32)
    w_in = f(inp["w_in"][0])
    lr_f, lr_b = w_in[:, 4608:4624], w_in[:, 4624:4640]
    wa_f, wa_b = f(inp["gla_wa2_fwd"][0]), f(inp["gla_wa2_bwd"][0])
    ba_f, ba_b = f(inp["gla_ba_fwd"][0]), f(inp["gla_ba_bwd"][0])
    if flip:
        lr_f, lr_b, wa_f, wa_b, ba_f, ba_b = lr_b, lr_f, wa_b, wa_f, ba_b, ba_f
    wlr = np.concatenate([lr_f, lr_b], axis=1).reshape(KD, P, 32)
    w_lr = np.ascontiguousarray(wlr.transpose(1, 0, 2)).reshape(P, KD * 32)
    w2 = np.zeros((33, 1024), np.float32)
    w2[0:16, 0:512] = wa_f
    w2[16:32, 512:1024] = wa_b
    w2[32, 0:512] = ba_f
    w2[32, 512:1024] = ba_b
    rows = np.zeros((1, NRW), np.float32)
    gq, gk = f(inp["attn_q_norm_g"][0]), f(inp["attn_k_norm_g"][0])
    rows[0, RW_GQ:RW_GQ + P] = gq
    rows[0, RW_GQS:RW_GQS + P] = np.concatenate([gq[64:], gq[:64]])
    rows[0, RW_GK:RW_GK + P] = gk
    rows[0, RW_GKS:RW_GKS + P] = np.concatenate([gk[64:], gk[:64]])
    rows[0, RW_GOG:RW_GOG + 256] = f(inp["gla_out_norm_g"][0])
    t = np.arange(18 * P)
    pos = (SEQ - 1 - t) if flip else t
    inv = 1.0 / (10000.0 ** (np.arange(64, dtype=np.float32) / 64))
    ang = pos.astype(np.float32)[:, None] * inv[None, :]
    cos, sin = np.cos(ang).astype(np.float32), np.sin(ang).astype(np.float32)
    rope = np.concatenate([cos, cos, -sin, sin], axis=1).astype(np.float32)
    j = np.arange(P)[:, None]
    i = np.arange(P)[None, :]
    mL = np.where(j >= i, 0.0, NEG).astype(np.float32)
    mR = np.where(j <= i, 0.0, NEG).astype(np.float32)
    amask = np.concatenate([np.tile(mL, (1, 4)), np.tile(mR, (1, 4))], axis=1)
    return dict(w_lr=w_lr, w2=w2, rows=rows, rope=rope, amask=np.ascontiguousarray(amask),
                cvec=host_cvec(inp, flip), cmat=host_cmat(flip))


def make_in_maps(inputs, cores=range(8)):
    com = host_layout_common(inputs)
    cc = [host_core_consts(inputs, False), host_core_consts(inputs, True)]
    x = np.asarray(inputs["x"], dtype=np.float32)
    maps = []
    for c in cores:
        b, half = c // 2, c % 2
        xl = x[b] if half == 0 else x[b][::-1]
        m = dict(x=np.ascontiguousarray(xl))
        m.update(com)
        m.update(cc[half])
        maps.append(m)
    return maps


_NC_CACHE = {}


def kernel(**inputs):
    if "full" not in _NC_CACHE:
        _NC_CACHE["full"] = build_program("full")
    nc = _NC_CACHE["full"]
    maps = make_in_maps(inputs)
    res = run_bass_kernel_spmd(nc, maps, core_ids=list(range(8)))
    B = inputs["x"].shape[0]
    out = np.empty((B, SEQ, D), np.float32)
    for c in range(8):
        b, half = c // 2, c % 2
        o = res.results[c]["out"]
        if half == 0:
            out[b, :2048] = o
        else:
            out[b, 2048:] = o[::-1]
    return out
```

```python
import contextlib
import numpy as np
import concourse.bass as bass
import concourse.mybir as mybir
from concourse.bass_utils import run_bass_kernel_spmd

F32 = mybir.dt.float32
BF16 = mybir.dt.bfloat16
AF = mybir.ActivationFunctionType
ALU = mybir.AluOpType
AX = mybir.AxisListType

P = 128
D = 2048
KD = 16
SEQ = 4096
NOWN = 16
DFF = 5632
NJ = 44
EPS = 1e-6
NXB = 17


class Ev:
    __slots__ = ("sem", "key", "val", "snap", "eng")

    def __init__(self, sem, key, val, snap, eng):
        self.sem, self.key, self.val, self.snap, self.eng = sem, key, val, snap, eng


class Tl:
    def __init__(self, t, name="", psum=False):
        self.t = t
        self.name = name
        self.w = None
        self.r = {}
        self.psum = psum

    def __getitem__(self, idx):
        return self.t[idx]


class Eng:
    def __init__(self, kb, h, name, is_pe=False):
        self.kb, self.h, self.name, self.is_pe = kb, h, name, is_pe
        self.sem = kb.new_sem("e_" + name)
        self.key = "e_" + name
        self.cnt = 0
        self.known = {}


class SemPool:
    def __init__(self, kb, name, n):
        self.sems = [kb.new_sem(f"{name}{i}") for i in range(n)]
        self.keys = [f"{name}{i}" for i in range(n)]
        self.vals = [0] * n
        self.last = [None] * n
        self.i = 0


class KB:
    def __init__(self, nc, es):
        self.nc, self.es = nc, es
        self.nsem = 0
        self.pe = Eng(self, nc.tensor, "pe", True)
        self.act = Eng(self, nc.scalar, "act")
        self.dve = Eng(self, nc.vector, "dve")
        self.pool = Eng(self, nc.gpsimd, "pool")
        self.sp = Eng(self, nc.sync, "sp")
        self.engines = [self.pe, self.act, self.dve, self.pool, self.sp]
        self.pools = []
        self.nwait = 0
        self.nins = 0

    def new_sem(self, name):
        self.nsem += 1
        return self.es.enter_context(self.nc.semaphore(name))

    def sem_pool(self, name, n):
        p = SemPool(self, name, n)
        self.pools.append(p)
        return p

    def _wait(self, eng, ev):
        if ev is None:
            return
        if eng.known.get(ev.key, 0) >= ev.val:
            return
        eng.h.wait_ge(ev.sem, ev.val)
        self.nwait += 1
        kn = eng.known
        for k2, v2 in ev.snap.items():
            if kn.get(k2, 0) < v2:
                kn[k2] = v2
        kn[ev.key] = ev.val

    def _deps(self, eng, reads, writes):
        for t in reads:
            ev = t.w
            if ev is not None and not (eng.is_pe and ev.eng is eng):
                self._wait(eng, ev)
            if t.psum:
                for ev in t.r.values():
                    if ev.eng is not eng:
                        self._wait(eng, ev)
        for t in writes:
            ev = t.w
            if ev is not None and not (eng.is_pe and ev.eng is eng):
                self._wait(eng, ev)
            for ev in t.r.values():
                if not (eng.is_pe and ev.eng is eng):
                    self._wait(eng, ev)

    def _record(self, ev, reads, writes):
        for t in reads:
            t.r[ev.key] = ev
        for t in writes:
            t.w = ev
            t.r = {}

    def op(self, eng, reads, writes, fn):
        self._deps(eng, reads, writes)
        ins = fn()
        eng.cnt += 1
        ins.then_inc(eng.sem, 1)
        self.nins += 1
        ev = Ev(eng.sem, eng.key, eng.cnt, dict(eng.known), eng)
        self._record(ev, reads, writes)
        return ev

    def dma(self, q, pool, out_ap, in_ap, reads, writes, **kw):
        self._deps(q, reads, writes)
        i = pool.i
        pool.i = (pool.i + 1) % len(pool.sems)
        if pool.last[i] is not None:
            self._wait(q, pool.last[i])
        ins = q.h.dma_start(out=out_ap, in_=in_ap, **kw)
        pool.vals[i] += 16
        ins.then_inc(pool.sems[i], 16)
        self.nins += 1
        ev = Ev(pool.sems[i], pool.keys[i], pool.vals[i], dict(q.known), None)
        pool.last[i] = ev
        self._record(ev, reads, writes)
        return ev

    def barrier(self):
        evs = []
        for e in self.engines:
            if e.cnt > 0:
                evs.append(Ev(e.sem, e.key, e.cnt, {}, e))
        for p in self.pools:
            for ev in p.last:
                if ev is not None:
                    evs.append(ev)
        for e in self.engines:
            for ev in evs:
                if ev.eng is e:
                    continue
                self._wait(e, ev)

    def mm(self, out_t, out_ap, lhs_t, lhs_ap, rhs_t, rhs_ap, start=True, stop=True):
        nc = self.nc
        return self.op(self.pe, [lhs_t, rhs_t], [out_t],
                       lambda: nc.tensor.matmul(out_ap, lhsT=lhs_ap, rhs=rhs_ap, start=start, stop=stop))

    def tr(self, out_t, out_ap, in_t, in_ap, id_t, id_ap):
        nc = self.nc
        return self.op(self.pe, [in_t, id_t], [out_t],
                       lambda: nc.tensor.transpose(out_ap, in_ap, id_ap))


CV_G1 = 0
CV_G2 = 16
CV_CW0 = 32
CV_CW1 = 120
CV_CW2 = 208
CV_CB = 296
NCV = 384

CM_ID = 0
CM_AFF = 128
CM_AFB = 256
CM_GMF = 384
CM_GMB = 512
CM_INF = 640
CM_INB = 644
NCM = 648

RW_GQ = 0
RW_GQS = 128
RW_GK = 256
RW_GKS = 384
RW_GOG = 512
NRW = 768
NEG = -30000.0
QS = 128 ** -0.5


def rmsnorm_to_T(kb, es_tiles, src_t, src_ap, nrows, gcol, dstT_t, dst_fn, bank_tiles, consts):
    nc = kb.nc
    junk, ss, lnv, rstd, hb = es_tiles
    if hb is None:
        hb = src_t
    ident, cvec = consts
    kb.op(kb.act, [src_t], [junk, ss],
          lambda: nc.scalar.activation(out=junk[0:nrows, :], in_=src_ap, func=AF.Square,
                                       accum_out=ss[0:nrows, :]))
    kb.op(kb.act, [ss, kb.eps_t], [lnv],
          lambda: nc.scalar.activation(out=lnv[0:nrows, :], in_=ss[0:nrows, :], func=AF.Ln,
                                       scale=1.0 / D, bias=kb.eps_t[0:nrows, :]))
    kb.op(kb.act, [lnv], [rstd],
          lambda: nc.scalar.activation(out=rstd[0:nrows, :], in_=lnv[0:nrows, :], func=AF.Exp, scale=-0.5))
    kb.op(kb.dve, [src_t, rstd], [hb],
          lambda: nc.vector.tensor_scalar(out=hb[0:nrows, 0:D], in0=src_ap, scalar1=rstd[0:nrows, :],
                                          scalar2=None, op0=ALU.mult))
    for g in range(4):
        bt = bank_tiles[g % len(bank_tiles)]
        for kk in range(4):
            k = g * 4 + kk
            kb.tr(bt, bt[:, kk * P:kk * P + nrows], hb, hb[0:nrows, k * P:(k + 1) * P],
                  ident, ident[0:nrows, 0:nrows])
        src = bt[:, :].rearrange("p (a b) -> p a b", b=P)[:, :, 0:nrows]
        gb = cvec[:, gcol + g * 4:gcol + g * 4 + 4].unsqueeze(2).broadcast_to([P, 4, nrows])
        kb.op(kb.dve, [bt, cvec], [dstT_t],
              lambda src=src, gb=gb, g=g: nc.vector.tensor_tensor(out=dst_fn(g * 4, 4), in0=src, in1=gb,
                                                                  op=ALU.mult))


def phase_ffn(kb, cst, x1_rows, out_rows, w_up, w_down, tiles=(0, 1, 2, 3)):
    nc = kb.nc
    ident, cvec = cst["cmat"], cst["cvec"]
    with contextlib.ExitStack() as es:
        def sb(name, shape, dt):
            return Tl(es.enter_context(nc.sbuf_tensor(name, shape, dt)), name)

        def psb(name):
            return Tl(es.enter_context(nc.psum_tensor(name, [P, 512], F32)), name, psum=True)

        h2T = sb("f_h2T", [P, KD, 514], BF16)
        aT = sb("f_aT", [P, NJ, 512], BF16)
        xb = [sb(f"f_xb{i}", [P, D], F32) for i in range(5)]
        hb = sb("f_hb", [P, D], F32)
        junk = sb("f_junk", [P, D], BF16)
        ss = sb("f_ss", [P, 1], F32)
        lnv = sb("f_lnv", [P, 1], F32)
        rstd = sb("f_rstd", [P, 1], F32)
        wsl = [sb(f"f_w{i}", [P, 8192], BF16) for i in range(3)]
        NT = 2
        t1g = [sb(f"f_t1g{i}", [P, 512], F32) for i in range(NT)]
        t1v = [sb(f"f_t1v{i}", [P, 512], F32) for i in range(NT)]
        sg = [sb(f"f_sg{i}", [P, 512], F32) for i in range(NT)]
        bk = [psb(f"f_ps{i}") for i in range(8)]
        nt_tiles = (junk, ss, lnv, rstd, hb)
        wq = kb.sem_pool("f_wq", 3)
        xq = kb.sem_pool("f_xq", 4)
        oq = kb.sem_pool("f_oq", 4)

        kb.op(kb.dve, [], [h2T], lambda: nc.vector.memset(h2T[:, :, 0:1], 0.0))
        wi = 0
        xi = 0
        for ti in tiles:
            r0 = ti * 512
            if ti != tiles[0]:
                kb.op(kb.dve, [h2T], [h2T],
                      lambda: nc.vector.tensor_copy(out=h2T[:, :, 0:1], in_=h2T[:, :, 512:513]))
            xs = []
            for b in range(4):
                xt = xb[xi % 5]
                xi += 1
                st, sap = x1_rows(r0 + b * P, P)
                kb.dma(kb.sp, xq, xt[:, :], sap, [st], [xt])
                xs.append(xt)
                rmsnorm_to_T(kb, nt_tiles, xt, xt[:, :], P, CV_G2, h2T,
                             lambda k0, nk, b=b: h2T[:, k0:k0 + nk, 1 + b * P:1 + (b + 1) * P],
                             [bk[5], bk[6], bk[7]], (ident, cvec))
            xh = xb[xi % 5]
            xi += 1
            st, sap = x1_rows(r0 + 512, 1)
            kb.dma(kb.sp, xq, xh[0:1, :], sap, [st], [xh])
            rmsnorm_to_T(kb, nt_tiles, xh, xh[0:1, :], 1, CV_G2, h2T,
                         lambda k0, nk: h2T[:, k0:k0 + nk, 513:514],
                         [bk[5], bk[6], bk[7]], (ident, cvec))
            for jp in range(NJ // 2):
                w = wsl[wi % 3]
                wi += 1
                kb.dma(kb.pool, wq, w[:, :], w_up[0][jp], [w_up[1]], [w], max_dma_last_dim=4096)
                wv = w[:, :].rearrange("p (j k g c) -> p j k g c", j=2, k=KD, g=2)
                for jj in range(2):
                    j = 2 * jp + jj
                    s = j % 2
                    pg = (bk[4 * s], bk[4 * s + 1])
                    pv = (bk[4 * s + 2], bk[4 * s + 3])
                    for gv, pts in ((0, pg), (1, pv)):
                        for k in range(KD):
                            for hf in range(2):
                                pt = pts[hf]
                                kb.mm(pt, pt[:, 0:258], w, wv[:, jj, k, gv, :], h2T,
                                      h2T[:, k, 256 * hf:256 * hf + 258],
                                      start=(k == 0), stop=(k == KD - 1))
                    tg, tv, sgt = t1g[j % NT], t1v[j % NT], sg[j % NT]
                    for gv, pts, tt, m in ((0, pg, tg, j), (1, pv, tv, NJ + j)):
                        c0 = cvec[:, CV_CW0 + m:CV_CW0 + m + 1]
                        c1 = cvec[:, CV_CW1 + m:CV_CW1 + m + 1]
                        c2 = cvec[:, CV_CW2 + m:CV_CW2 + m + 1]
                        cb = cvec[:, CV_CB + m:CV_CB + m + 1]
                        for hf in range(2):
                            pt = pts[hf]
                            o0 = 256 * hf
                            kb.op(kb.act, [pt, cvec], [tt],
                                  lambda pt=pt, tt=tt, c1=c1, cb=cb, o0=o0: nc.scalar.activation(
                                      out=tt[:, o0:o0 + 256], in_=pt[:, 1:257], func=AF.Identity,
                                      scale=c1, bias=cb))
                            kb.op(kb.dve, [pt, cvec, tt], [tt],
                                  lambda pt=pt, tt=tt, c0=c0, o0=o0: nc.vector.scalar_tensor_tensor(
                                      out=tt[:, o0:o0 + 256], in0=pt[:, 0:256], scalar=c0,
                                      in1=tt[:, o0:o0 + 256], op0=ALU.mult, op1=ALU.add))
                            kb.op(kb.dve, [pt, cvec, tt], [tt],
                                  lambda pt=pt, tt=tt, c2=c2, o0=o0: nc.vector.scalar_tensor_tensor(
                                      out=tt[:, o0:o0 + 256], in0=pt[:, 2:258], scalar=c2,
                                      in1=tt[:, o0:o0 + 256], op0=ALU.mult, op1=ALU.add))
                    kb.op(kb.act, [tg], [sgt],
                          lambda tg=tg, sgt=sgt: nc.scalar.activation(out=sgt[:, :], in_=tg[:, :],
                                                                      func=AF.Exp, scale=-1.0))
                    kb.op(kb.act, [sgt, kb.one_t], [sgt],
                          lambda sgt=sgt: nc.scalar.activation(out=sgt[:, :], in_=sgt[:, :],
                                                               func=AF.Ln, bias=kb.one_t[:, :]))
                    kb.op(kb.act, [sgt], [sgt],
                          lambda sgt=sgt: nc.scalar.activation(out=sgt[:, :], in_=sgt[:, :],
                                                               func=AF.Exp, scale=-1.0))
                    kb.op(kb.dve, [tg, sgt], [sgt],
                          lambda tg=tg, sgt=sgt: nc.vector.tensor_tensor(out=sgt[:, :], in0=tg[:, :],
                                                                         in1=sgt[:, :], op=ALU.mult))
                    kb.op(kb.dve, [tv, sgt], [aT],
                          lambda tv=tv, sgt=sgt, j=j: nc.vector.tensor_tensor(out=aT[:, j, :], in0=tv[:, :],
                                                                              in1=sgt[:, :], op=ALU.mult))
            dbk = [bk[0], bk[1], bk[2], bk[3]]
            for c in range(4):
                for q in range(4):
                    w = wsl[wi % 3]
                    wi += 1
                    kb.dma(kb.pool, wq, w[:, 0:11 * 512], w_down[0][c * 4 + q], [w_down[1]], [w],
                           max_dma_last_dim=4096)
                    wv = w[:, 0:11 * 512].rearrange("p (k c) -> p k c", k=11)
                    for b in range(4):
                        for kk in range(11):
                            j = q * 11 + kk
                            kb.mm(dbk[b], dbk[b][:, :], aT, aT[:, j, b * P:(b + 1) * P], w, wv[:, kk, :],
                                  start=(j == 0), stop=(j == NJ - 1))
                for b in range(4):
                    xt = xs[b]
                    kb.op(kb.dve, [dbk[b], xt], [xt],
                          lambda b=b, xt=xt, c=c: nc.vector.tensor_tensor(
                              out=xt[:, c * 512:(c + 1) * 512], in0=dbk[b][:, :],
                              in1=xt[:, c * 512:(c + 1) * 512], op=ALU.add))
            for b in range(4):
                ot, oap = out_rows(r0 + b * P, P)
                kb.dma(kb.sp, oq, oap, xs[b][:, :], [xs[b]], [ot])
        kb.barrier()


def phase_mixer(kb, cst, dr, fwd):
    nc = kb.nc
    cvec, cmat, rows = cst["cvec"], cst["cmat"], cst["rows"]
    ident = cmat
    dr_in = dr["dr_in"]
    with contextlib.ExitStack() as es:
        def sb(name, shape, dt):
            return Tl(es.enter_context(nc.sbuf_tensor(name, shape, dt)), name)

        pfx = "c_" if fwd else "b_"
        banks = [Tl(es.enter_context(nc.psum_tensor(f"{pfx}ps{i}", [P, 512], F32)), f"ps{i}", psum=True)
                 for i in range(8)]
        bi = [0]

        def nb():
            t = banks[bi[0] % 8]
            bi[0] += 1
            return t

        b4i = [0]

        def nb4():
            t = banks[4 + b4i[0] % 4]
            b4i[0] += 1
            return t

        hT = sb(pfx + "hT", [P, KD, 5 * P], BF16)
        NXI = 1 if fwd else 2
        xin = [sb(pfx + f"x{i}", [P, D], F32) for i in range(NXI)]
        hb = None
        junk = sb(pfx + "junk", [P, D], BF16)
        ss = sb(pfx + "ss", [P, 1], F32)
        lnv = sb(pfx + "lnv", [P, 1], F32)
        rstd = sb(pfx + "rstd", [P, 1], F32)
        nt_tiles = (junk, ss, lnv, rstd, hb)
        NW = 2 if fwd else 3
        wsl = [sb(pfx + f"w{i}", [P, 8192], BF16) for i in range(NW)]
        big = [sb(pfx + f"big{i}", [P, 1024], BF16) for i in range(4)]
        qeT = [sb(pfx + f"qeT{i}", [P, 512], BF16) for i in range(4)]
        keT = [sb(pfx + f"keT{i}", [P, 512], BF16) for i in range(4)]
        ke = [sb(pfx + f"ke{i}", [P, 512], BF16) for i in range(4)]
        vg = [sb(pfx + f"vg{i}", [P, 1024], BF16) for i in range(4)]
        gsc = [sb(pfx + f"gsc{i}", [P, 4, 6], F32) for i in range(4)]
        S = sb(pfx + "S", [P, 4, 256], F32)
        Sbf = [sb(pfx + f"Sbf{h}", [P, 256], BF16) for h in range(4)]
        lr_sb = [sb(pfx + f"lr{i}", [P, 32], F32) for i in range(4)]
        lrT = [sb(pfx + f"lrT{i}", [33, P], F32) for i in range(4)]
        spt = [sb(pfx + f"spt{i}", [P, 512], F32) for i in range(4)]
        bsm = [sb(pfx + f"bsm{i}", [P, 4, 4], F32) for i in range(4)]
        dlt = [sb(pfx + f"dlt{i}", [P, 4, 2], F32) for i in range(4)]
        tA = [sb(pfx + f"tA{i}", [P, 512], F32) for i in range(2)]
        aTs = sb(pfx + "aTs", [P, 512], BF16)
        oin = None if fwd else sb(pfx + "oin", [P, 1024], F32)
        obuf = [sb(pfx + f"ob{i}", [P, 1024], BF16) for i in range(2)]
        wq = kb.sem_pool(pfx + "wq", NW)
        xq = kb.sem_pool(pfx + "xq", 4)
        oq = kb.sem_pool(pfx + "oq", 4)
        if fwd:
            qT = [sb(f"c_qT{i}", [P, 1024], BF16) for i in range(4)]
            kT = [sb(f"c_kT{i}", [P, 256], BF16) for i in range(6)]
            va = [sb(f"c_va{i}", [P, 256], BF16) for i in range(6)]
            mixA = [sb(f"c_mixA{i}", [P, 8, P], BF16) for i in range(4)]
            mixG = [sb(f"c_mixG{i}", [P, 8, P], BF16) for i in range(4)]
            csb = [sb(f"c_cs{i}", [P, 256], F32) for i in range(2)]
            Gq = [sb(f"c_Gq{i}", [P, 256], F32) for i in range(2)]
            Gk = [sb(f"c_Gk{i}", [P, 256], F32) for i in range(2)]
            tB = [sb(f"c_tB{i}", [P, 512], F32) for i in range(2)]
            tq = [sb(f"c_tq{i}", [P, 512], F32) for i in range(2)]
            ssq = sb("c_ssq", [P, 4], F32)
            sqj = sb("c_sqj", [P, 256], BF16)
            lnq = sb("c_lnq", [P, 4], F32)
            rsq = sb("c_rsq", [P, 4], F32)
            PT = [sb(f"c_PT{i}", [P, 512], BF16) for i in range(6)]
            rec = [sb(f"c_rec{i}", [P, 512], F32) for i in range(2)]
            osum = sb("c_osum", [P, 1024], F32)
            xr = [sb(f"c_xr{i}", [P, 512], F32) for i in range(2)]
            xo = [sb(f"c_xo{i}", [P, 512], F32) for i in range(2)]
            rq = kb.sem_pool("c_rq", 3)
        AFT = cmat[:, CM_AFF:CM_AFF + P] if fwd else cmat[:, CM_AFB:CM_AFB + P]
        GMT = cmat[:, CM_GMF:CM_GMF + P] if fwd else cmat[:, CM_GMB:CM_GMB + P]
        IND = cmat[:, CM_INF:CM_INF + 4] if fwd else cmat[:, CM_INB:CM_INB + 4]
        W2 = cst["W2"]
        w2c = 0 if fwd else 512
        wlr = cst["wlr"]

        kb.op(kb.dve, [], [S], lambda: nc.vector.memset(S[:, :, :], 0.0))
        for i in range(4):
            kb.op(kb.dve, [], [lrT[i]], lambda i=i: nc.vector.memset(lrT[i][:, :], 1.0))
        cnt = dict(w=0, x=0, t=0, ob=0, xr=0)

        def load_w(src_ap, ncols=8192):
            w = wsl[cnt["w"] % NW]
            cnt["w"] += 1
            kb.dma(kb.pool, wq, w[:, 0:ncols], src_ap, [dr_in], [w], max_dma_last_dim=4096)
            return w

        def proj(i, w, wv_fn, ncols):
            bk = nb()
            for k in range(KD):
                kb.mm(bk, bk[:, 0:ncols], hT, hT[:, k, i * P:(i + 1) * P], w, wv_fn(k),
                      start=(k == 0), stop=(k == KD - 1))
            return bk

        def gating_all(idxs):
            A = AF
            bk = {i: proj(i, wlr, lambda k: wlr[:, k * 32:(k + 1) * 32], 32) for i in idxs}
            for i in idxs:
                kb.op(kb.act, [bk[i]], [lr_sb[i]],
                      lambda i=i: nc.scalar.copy(out=lr_sb[i][:, :], in_=bk[i][:, 0:32]))
            b2 = {}
            for i in idxs:
                b2[i] = nb()
                kb.tr(b2[i], b2[i][0:32, 0:P], lr_sb[i], lr_sb[i][:, :], ident, ident[:, 0:P])
            for i in idxs:
                kb.op(kb.act, [b2[i]], [lrT[i]],
                      lambda i=i: nc.scalar.copy(out=lrT[i][0:32, :], in_=b2[i][0:32, 0:P]))
            b3 = {}
            for i in idxs:
                b3[i] = nb()
                kb.mm(b3[i], b3[i][:, :], lrT[i], lrT[i][0:33, :], W2, W2[0:33, w2c:w2c + 512])
            for i in idxs:
                kb.op(kb.act, [b3[i]], [spt[i]],
                      lambda i=i: nc.scalar.activation(out=spt[i][:, :], in_=b3[i][:, :], func=A.Exp, scale=-1.0))
            for i in idxs:
                kb.op(kb.act, [spt[i], kb.one_t], [spt[i]],
                      lambda i=i: nc.scalar.activation(out=spt[i][:, :], in_=spt[i][:, :], func=A.Ln,
                                                       bias=kb.one_t[:, :]))
            b4 = {}
            for i in idxs:
                b4[i] = nb()
                kb.mm(b4[i], b4[i][:, :], cmat, AFT, spt[i], spt[i][:, :])
            for i in idxs:
                kb.op(kb.act, [b4[i]], [big[i]],
                      lambda i=i: nc.scalar.activation(out=big[i][:, 0:512], in_=b4[i][:, :], func=A.Exp,
                                                       scale=-1.0 / 16))
                kb.op(kb.act, [b4[i]], [big[i]],
                      lambda i=i: nc.scalar.activation(out=big[i][:, 512:1024], in_=b4[i][:, :], func=A.Exp,
                                                       scale=1.0 / 16))
            b5 = {}
            for i in idxs:
                b5[i] = nb()
                for h in range(4):
                    kb.mm(b5[i], b5[i][:, 4 * h:4 * h + 4], spt[i], spt[i][:, h * P:(h + 1) * P], cmat, IND)
            for i in idxs:
                kb.op(kb.act, [b5[i]], [bsm[i]],
                      lambda i=i: nc.scalar.copy(out=bsm[i][:, :, :],
                                                 in_=b5[i][:, 0:16].rearrange("p (h c) -> p h c", c=4)))
            for i in idxs:
                kb.op(kb.dve, [bsm[i]], [dlt[i]],
                      lambda i=i: nc.vector.tensor_tensor(out=dlt[i][:, :, :], in0=bsm[i][:, :, 0:2],
                                                          in1=bsm[i][:, :, 2:4], op=ALU.subtract))
            for i in idxs:
                kb.op(kb.act, [bsm[i]], [gsc[i]],
                      lambda i=i: nc.scalar.activation(out=gsc[i][:, :, 0:4], in_=bsm[i][:, :, :], func=A.Exp,
                                                       scale=-1.0 / 16))
                kb.op(kb.act, [dlt[i]], [gsc[i]],
                      lambda i=i: nc.scalar.activation(out=gsc[i][:, :, 4:6], in_=dlt[i][:, :, :], func=A.Exp,
                                                       scale=-1.0 / 16))

        def to_T(src_t, nchunks, dst_t, dst_ap):
            b2 = nb()
            for h in range(nchunks):
                kb.tr(b2, b2[:, h * P:(h + 1) * P], src_t, src_t[:, h * P:(h + 1) * P], ident, ident[:, 0:P])
            kb.op(kb.act, [b2], [dst_t], lambda: nc.scalar.copy(out=dst_ap, in_=b2[:, 0:nchunks * P]))

        def cons_qg(i, bk):
            t = tA[cnt["t"] % 2]
            cnt["t"] += 1
            kb.op(kb.dve, [bk, big[i]], [t],
                  lambda: nc.vector.scalar_tensor_tensor(out=t[:, :], in0=bk[:, :], scalar=QS,
                                                         in1=big[i][:, 0:512], op0=ALU.mult, op1=ALU.mult))
            return lambda: to_T(t, 4, qeT[i], qeT[i][:, :])

        def cons_kg(i, bk):
            t = tA[cnt["t"] % 2]
            cnt["t"] += 1
            kb.op(kb.dve, [bk, big[i]], [t],
                  lambda: nc.vector.tensor_tensor(out=t[:, :], in0=bk[:, :], in1=big[i][:, 512:1024],
                                                  op=ALU.mult))
            kb.op(kb.act, [t], [ke[i]], lambda: nc.scalar.copy(out=ke[i][:, :], in_=t[:, :]))
            return lambda: to_T(t, 4, keT[i], keT[i][:, :])

        def cons_vg(i, bk, c):
            kb.op(kb.act, [bk], [vg[i]],
                  lambda: nc.scalar.copy(out=vg[i][:, c * 512:(c + 1) * 512], in_=bk[:, :]))

        def cons_gate(i, bk, c):
            t = tq[cnt["t"] % 2]
            cnt["t"] += 1
            kb.op(kb.act, [bk], [t],
                  lambda: nc.scalar.activation(out=t[:, :], in_=bk[:, :], func=AF.Exp, scale=-1.0))
            kb.op(kb.act, [t, kb.one_t], [t],
                  lambda: nc.scalar.activation(out=t[:, :], in_=t[:, :], func=AF.Ln, bias=kb.one_t[:, :]))
            kb.op(kb.act, [t], [t],
                  lambda: nc.scalar.activation(out=t[:, :], in_=t[:, :], func=AF.Exp, scale=-1.0))
            kb.op(kb.dve, [bk, t], [big[i]],
                  lambda: nc.vector.tensor_tensor(out=big[i][:, c * 512:(c + 1) * 512], in0=bk[:, :],
                                                  in1=t[:, :], op=ALU.mult))

        def rope_tables(b):
            cs = csb[b % 2]
            kb.dma(kb.sp, rq, cs[:, :], dr["rope"][b * P:(b + 1) * P, :], [dr_in], [cs])
            for G, g0, gs0 in ((Gq[b % 2], RW_GQ, RW_GQS), (Gk[b % 2], RW_GK, RW_GKS)):
                kb.op(kb.dve, [cs, rows], [G],
                      lambda G=G, g0=g0: nc.vector.tensor_tensor(out=G[:, 0:P], in0=cs[:, 0:P],
                                                                 in1=rows[:, g0:g0 + P], op=ALU.mult))
                kb.op(kb.dve, [cs, rows], [G],
                      lambda G=G, gs0=gs0: nc.vector.tensor_tensor(out=G[:, P:2 * P], in0=cs[:, P:2 * P],
                                                                   in1=rows[:, gs0:gs0 + P], op=ALU.mult))

        def norm_rope(bk, c0, nh, G):
            t = tq[cnt["t"] % 2]
            a = tA[cnt["t"] % 2]
            bb = tB[cnt["t"] % 2]
            cnt["t"] += 1
            n = nh * P
            src = bk[:, c0:c0 + n]
            src3 = src.rearrange("p (h d) -> p h d", d=P)
            kb.op(kb.act, [bk], [t], lambda: nc.scalar.activation(out=t[:, 0:n], in_=src, func=AF.Square))
            kb.op(kb.dve, [t], [ssq],
                  lambda: nc.vector.tensor_reduce(out=ssq[:, 0:nh], in_=t[:, 0:n].rearrange("p (h d) -> p h d", d=P),
                                                  axis=AX.X, op=ALU.add))
            kb.op(kb.act, [ssq, kb.eps_t], [lnq],
                  lambda: nc.scalar.activation(out=lnq[:, 0:nh], in_=ssq[:, 0:nh], func=AF.Ln, scale=1.0 / P,
                                               bias=kb.eps_t[:, :]))
            kb.op(kb.act, [lnq], [rsq],
                  lambda: nc.scalar.activation(out=rsq[:, 0:nh], in_=lnq[:, 0:nh], func=AF.Exp, scale=-0.5))
            a3 = a[:, 0:n].rearrange("p (h d) -> p h d", d=P)
            b3 = bb[:, 0:n].rearrange("p (h d) -> p h d", d=P)
            kb.op(kb.dve, [bk, G], [a],
                  lambda: nc.vector.tensor_tensor(out=a3, in0=src3,
                                                  in1=G[:, 0:P].unsqueeze(1).broadcast_to([P, nh, P]),
                                                  op=ALU.mult))
            kb.op(kb.dve, [bk, G], [bb],
                  lambda: nc.vector.tensor_tensor(out=b3[:, :, 0:64], in0=src3[:, :, 64:128],
                                                  in1=G[:, P:P + 64].unsqueeze(1).broadcast_to([P, nh, 64]),
                                                  op=ALU.mult))
            kb.op(kb.dve, [bk, G], [bb],
                  lambda: nc.vector.tensor_tensor(out=b3[:, :, 64:128], in0=src3[:, :, 0:64],
                                                  in1=G[:, P + 64:2 * P].unsqueeze(1).broadcast_to([P, nh, 64]),
                                                  op=ALU.mult))
            kb.op(kb.dve, [a, bb], [a],
                  lambda: nc.vector.tensor_tensor(out=a[:, 0:n], in0=a[:, 0:n], in1=bb[:, 0:n], op=ALU.add))
            kb.op(kb.dve, [a, rsq], [a],
                  lambda: nc.vector.tensor_tensor(out=a3, in0=a3,
                                                  in1=rsq[:, 0:nh].unsqueeze(2).broadcast_to([P, nh, P]),
                                                  op=ALU.mult))
            return a

        def cons_q(i, b, bk, c):
            a = norm_rope(bk, 0, 4, Gq[b % 2])
            return lambda: to_T(a, 4, qT[i], qT[i][:, c * 512:(c + 1) * 512])

        def cons_kv(b, bk):
            a = norm_rope(bk, 0, 2, Gk[b % 2])
            kb.op(kb.act, [bk], [va[b % 6]], lambda: nc.scalar.copy(out=va[b % 6][:, :], in_=bk[:, 256:512]))
            return lambda: to_T(a, 2, kT[b % 6], kT[b % 6][:, :])

        def gla_block(i, b, order, emit):
            bI = bN = None
            if emit:
                bA = nb4()
                for h in range(4):
                    kb.mm(bA, bA[:, h * P:(h + 1) * P], keT[i], keT[i][:, h * P:(h + 1) * P],
                          qeT[i], qeT[i][:, h * P:(h + 1) * P])
                kb.op(kb.dve, [bA, cmat], [aTs],
                      lambda: nc.vector.tensor_tensor(out=aTs[:, :].rearrange("p (h i) -> p h i", i=P),
                                                      in0=bA[:, :].rearrange("p (h i) -> p h i", i=P),
                                                      in1=GMT.unsqueeze(1).broadcast_to([P, 4, P]), op=ALU.mult))
                bI = [banks[0], banks[1]]
                for h in range(4):
                    o = bI[h // 2]
                    kb.mm(o, o[:, (h % 2) * 256:(h % 2) * 256 + 256], aTs, aTs[:, h * P:(h + 1) * P],
                          vg[i], vg[i][:, h * 256:(h + 1) * 256])
                bN = [banks[2], banks[3]]
            for c in order:
                r0 = 64 * c
                if emit:
                    for h in range(4):
                        kb.op(kb.act, [S, gsc[i]], [Sbf[h]],
                              lambda h=h: nc.scalar.activation(out=Sbf[h][:, :], in_=S[:, h, :], func=AF.Copy,
                                                               scale=gsc[i][:, h, 2 + c:3 + c]))
                    for h in range(4):
                        o = bN[h // 2]
                        kb.mm(o, o[r0:r0 + 64, (h % 2) * 256:(h % 2) * 256 + 256],
                              qeT[i], qeT[i][:, h * P + r0:h * P + r0 + 64], Sbf[h], Sbf[h][:, :])
                bM = [nb4(), nb4()]
                for h in range(4):
                    o = bM[h // 2]
                    kb.mm(o, o[:, (h % 2) * 256:(h % 2) * 256 + 256], ke[i], ke[i][r0:r0 + 64, h * P:(h + 1) * P],
                          vg[i], vg[i][r0:r0 + 64, h * 256:(h + 1) * 256])
                for h in range(4):
                    o = bM[h // 2]
                    kb.op(kb.dve, [S, gsc[i]], [S],
                          lambda h=h: nc.vector.tensor_scalar(out=S[:, h, :], in0=S[:, h, :],
                                                              scalar1=gsc[i][:, h, c:c + 1], scalar2=None,
                                                              op0=ALU.mult))
                    kb.op(kb.dve, [o, S, gsc[i]], [S],
                          lambda h=h, o=o: nc.vector.scalar_tensor_tensor(
                              out=S[:, h, :], in0=o[:, (h % 2) * 256:(h % 2) * 256 + 256],
                              scalar=gsc[i][:, h, 4 + c:5 + c], in1=S[:, h, :], op0=ALU.mult, op1=ALU.add))
            return bI, bN

        def attention(i, b):
            for h2 in range(2):
                kbs = [kk for kk in (b - 1, b, b + 1) if kk >= 0]
                bD = nb4()
                bO = nb4()
                for n_, kk in enumerate(kbs):
                    bS = banks[n_ % 2]
                    masked = kk != b
                    kb.mm(bS, bS[:, :], kT[kk % 6], kT[kk % 6][:, h2 * P:(h2 + 1) * P],
                          qT[i], qT[i][:, h2 * 512:(h2 + 1) * 512], start=True, stop=not masked)
                    if masked:
                        mk = cst["amask"]
                        mc = 0 if kk < b else 512
                        kb.mm(bS, bS[:, :], cst["identb"], cst["identb"][:, :], mk, mk[:, mc:mc + 512],
                              start=False, stop=True)
                    pt = PT[n_]
                    kb.op(kb.act, [bS], [pt],
                          lambda pt=pt, bS=bS: nc.scalar.activation(out=pt[:, :], in_=bS[:, :], func=AF.Exp,
                                                                    scale=QS))
                for n_, kk in enumerate(kbs):
                    kb.mm(bD, bD[:, :], cst["onesb"], cst["onesb"][:, :], PT[n_], PT[n_][:, :],
                          start=(n_ == 0), stop=False)
                kb.mm(bD, bD[:, :], cst["onesf"], cst["onesf"][0:1, :], cst["esink"],
                      cst["esink"][0:1, h2 * 512:(h2 + 1) * 512], start=False, stop=True)
                for n_, kk in enumerate(kbs):
                    kb.mm(bO, bO[:, :], va[kk % 6], va[kk % 6][:, h2 * P:(h2 + 1) * P], PT[n_], PT[n_][:, :],
                          start=(n_ == 0), stop=(n_ == len(kbs) - 1))
                rc = rec[0]
                kb.op(kb.dve, [bD], [rc], lambda rc=rc, bD=bD: nc.vector.reciprocal(out=rc[:, :], in_=bD[:, :]))
                kb.op(kb.dve, [bO, rc], [mixA[i]],
                      lambda rc=rc, bO=bO, h2=h2: nc.vector.tensor_tensor(
                          out=mixA[i][:, 4 * h2:4 * h2 + 4, :], in0=bO[:, :].rearrange("p (h t) -> p h t", t=P),
                          in1=rc[:, :].rearrange("p (h t) -> p h t", t=P), op=ALU.mult))

        def combine(i, b, bI, bN):
            ob = obuf[cnt["ob"] % 2]
            cnt["ob"] += 1
            kb.dma(kb.sp, xq, ob[:, :], dr["obs"][b * P:(b + 1) * P, :], [dr["obs_t"][b]], [ob])
            for hh in range(2):
                kb.op(kb.act, [bN[hh]], [osum],
                      lambda hh=hh: nc.scalar.copy(out=osum[:, hh * 512:(hh + 1) * 512], in_=bN[hh][:, :]))
            for hh in range(2):
                kb.op(kb.dve, [bI[hh], osum], [osum],
                      lambda hh=hh: nc.vector.tensor_tensor(out=osum[:, hh * 512:(hh + 1) * 512], in0=bI[hh][:, :],
                                                            in1=osum[:, hh * 512:(hh + 1) * 512], op=ALU.add))
            kb.op(kb.dve, [osum, ob], [osum],
                  lambda: nc.vector.tensor_tensor(out=osum[:, :], in0=osum[:, :], in1=ob[:, :], op=ALU.add))
            for h in range(4):
                kb.op(kb.act, [osum], [sqj, ssq],
                      lambda h=h: nc.scalar.activation(out=sqj[:, 0:256], in_=osum[:, h * 256:(h + 1) * 256],
                                                       func=AF.Square, accum_out=ssq[:, h:h + 1]))
            kb.op(kb.act, [ssq, kb.eps_t], [lnq],
                  lambda: nc.scalar.activation(out=lnq[:, :], in_=ssq[:, :], func=AF.Ln, scale=1.0 / 256,
                                               bias=kb.eps_t[:, :]))
            kb.op(kb.act, [lnq], [rsq],
                  lambda: nc.scalar.activation(out=rsq[:, :], in_=lnq[:, :], func=AF.Exp, scale=-0.5))
            for h in range(4):
                kb.op(kb.dve, [osum, rsq, big[i]], [osum],
                      lambda h=h: nc.vector.scalar_tensor_tensor(
                          out=osum[:, h * 256:(h + 1) * 256], in0=osum[:, h * 256:(h + 1) * 256],
                          scalar=rsq[:, h:h + 1], in1=big[i][:, h * 256:(h + 1) * 256],
                          op0=ALU.mult, op1=ALU.mult))
            kb.op(kb.dve, [osum, rows], [osum],
                  lambda: nc.vector.tensor_tensor(
                      out=osum[:, :].rearrange("p (h e) -> p h e", e=256),
                      in0=osum[:, :].rearrange("p (h e) -> p h e", e=256),
                      in1=rows[:, RW_GOG:RW_GOG + 256].unsqueeze(1).broadcast_to([P, 4, 256]), op=ALU.mult))
            for hh in range(2):
                b2 = nb4()
                for q in range(4):
                    kb.tr(b2, b2[:, q * P:(q + 1) * P], osum, osum[:, (hh * 4 + q) * P:(hh * 4 + q + 1) * P],
                          ident, ident[:, 0:P])
                kb.op(kb.act, [b2], [mixG[i]],
                      lambda hh=hh, b2=b2: nc.scalar.copy(out=mixG[i][:, 4 * hh:4 * hh + 4, :],
                                                          in_=b2[:, :].rearrange("p (k t) -> p k t", t=P)))

        def load_norm(b, i):
            xt = xin[cnt["x"] % NXI]
            cnt["x"] += 1
            kb.dma(kb.sp, xq, xt[:, :], dr["x"][b * P:(b + 1) * P, :], [dr_in], [xt])
            rmsnorm_to_T(kb, nt_tiles, xt, xt[:, :], P, CV_G1, hT,
                         lambda k0, nk: hT[:, k0:k0 + nk, i * P:(i + 1) * P],
                         [nb(), nb(), nb(), nb()], (ident, cvec))

        w_in = dr["w_in"]

        def chunk(c, idxs, cons):
            w = load_w(w_in[c])
            wv = w[:, :].rearrange("p (k c) -> p k c", k=KD)
            pend = None
            for i in idxs:
                bk = proj(i, w, lambda k: wv[:, k, :], 512)
                fin = cons(i, bk)
                if pend is not None:
                    pend()
                pend = fin
            if pend is not None:
                pend()

        if not fwd:
            tiles = [list(range(t * 4 + 3, t * 4 - 1, -1)) for t in range(7, -1, -1)]
            for blocks in tiles:
                emits = [b <= NXB - 1 for b in blocks]
                for i, b in enumerate(blocks):
                    load_norm(b, i)
                gating_all(range(4))
                if any(emits):
                    chunk(3, [i for i in range(4) if emits[i]], cons_qg)
                chunk(4, range(4), cons_kg)
                chunk(5, range(4), lambda i, bk: cons_vg(i, bk, 0))
                chunk(6, range(4), lambda i, bk: cons_vg(i, bk, 1))
                for i, b in enumerate(blocks):
                    bI, bN = gla_block(i, b, (1, 0), emits[i])
                    if emits[i]:
                        ob = obuf[cnt["ob"] % 2]
                        cnt["ob"] += 1
                        for hh in range(2):
                            kb.op(kb.act, [bN[hh]], [oin],
                                  lambda hh=hh: nc.scalar.copy(out=oin[:, hh * 512:(hh + 1) * 512],
                                                               in_=bN[hh][:, :]))
                            kb.op(kb.dve, [bI[hh], oin], [ob],
                                  lambda hh=hh: nc.vector.tensor_tensor(
                                      out=ob[:, hh * 512:(hh + 1) * 512], in0=bI[hh][:, :],
                                      in1=oin[:, hh * 512:(hh + 1) * 512], op=ALU.add))
                        kb.dma(kb.sp, oq, dr["obs"][b * P:(b + 1) * P, :], ob[:, :], [ob], [dr["obs_t"][b]])
        else:
            tiles = [[0, 1, 2, 3], [4, 5, 6, 7], [8, 9, 10, 11], [12, 13, 14, 15], [16]]
            for blocks in tiles:
                nbk = len(blocks)
                ext = blocks[-1] + 1
                for i, b in enumerate(blocks + [ext]):
                    load_norm(b, i)
                gating_all(range(nbk))
                chunk(3, range(nbk), cons_qg)
                chunk(4, range(nbk), cons_kg)
                chunk(5, range(nbk), lambda i, bk: cons_vg(i, bk, 0))
                chunk(6, range(nbk), lambda i, bk: cons_vg(i, bk, 1))
                chunk(7, range(nbk), lambda i, bk: cons_gate(i, bk, 0))
                chunk(8, range(nbk), lambda i, bk: cons_gate(i, bk, 1))
                w = load_w(w_in[2])
                wv = w[:, :].rearrange("p (k c) -> p k c", k=KD)
                pend = None
                for i, b in enumerate(blocks + [ext]):
                    rope_tables(b)
                    bk = proj(i, w, lambda k: wv[:, k, :], 512)
                    fin = cons_kv(b, bk)
                    if pend is not None:
                        pend()
                    pend = fin
                pend()
                for c in (0, 1):
                    w = load_w(w_in[c])
                    wv = w[:, :].rearrange("p (k c) -> p k c", k=KD)
                    pend = None
                    for i, b in enumerate(blocks):
                        rope_tables(b)
                        bk = proj(i, w, lambda k: wv[:, k, :], 512)
                        fin = cons_q(i, b, bk, c)
                        if pend is not None:
                            pend()
                        pend = fin
                    pend()
                for i, b in enumerate(blocks):
                    attention(i, b)
                    bI, bN = gla_block(i, b, (0, 1), True)
                    combine(i, b, bI, bN)
                for c in range(4):
                    w = load_w(dr["w_out"][c])
                    wv = w[:, :].rearrange("p (k c) -> p k c", k=KD)
                    for i, b in enumerate(blocks):
                        xrt = xr[cnt["xr"] % 2]
                        xot = xo[cnt["xr"] % 2]
                        cnt["xr"] += 1
                        kb.dma(kb.sp, xq, xrt[:, :], dr["x"][b * P:(b + 1) * P, c * 512:(c + 1) * 512],
                               [dr_in], [xrt])
                        bk = nb()
                        for k in range(KD):
                            mt = mixA[i] if k < 8 else mixG[i]
                            kb.mm(bk, bk[:, :], mt, mt[:, k % 8, :], w, wv[:, k, :],
                                  start=(k == 0), stop=(k == KD - 1))
                        kb.op(kb.dve, [bk, xrt], [xot],
                              lambda bk=bk, xrt=xrt, xot=xot: nc.vector.tensor_tensor(
                                  out=xot[:, :], in0=bk[:, :], in1=xrt[:, :], op=ALU.add))
                        kb.dma(kb.sp, oq, dr["x1s"][b * P:(b + 1) * P, c * 512:(c + 1) * 512], xot[:, :],
                               [xot], [dr["x1_t"][(b, c)]])
        kb.barrier()


def build_program(mode="full"):
    nc = bass.Bass("TRN2", target_bir_lowering=False)

    def din(name, shape, dt=F32):
        return nc.dram_tensor(name, shape, dt, kind="ExternalInput").ap()

    dr = {}
    dr["x"] = din("x", [SEQ, D])
    dr["w_in"] = din("w_in", [9, P, KD * 512])
    wlr_d = din("w_lr", [P, KD * 32])
    w2_d = din("w2", [33, 1024])
    dr["w_out"] = din("w_out", [4, P, KD * 512])
    w_up = din("w_up", [NJ // 2, P, 2 * KD * 2 * P])
    w_down = din("w_down", [16, P, 11 * 512])
    cvec_d = din("cvec", [P, NCV])
    cmat_d = din("cmat", [P, NCM])
    rows_d = din("rows", [1, NRW])
    sink_d = din("sink", [1, 8])
    dr["rope"] = din("rope", [18 * P, 256])
    amask_d = din("amask", [P, 1024])
    out = nc.dram_tensor("out", [NOWN * P, D], F32, kind="ExternalOutput").ap()
    dr["x1s"] = nc.dram_tensor("x1s", [NXB * P, D], F32, kind="Internal").ap()
    dr["obs"] = nc.dram_tensor("obs", [NXB * P, 1024], BF16, kind="Internal").ap()
    dbg = None
    if mode.startswith("dbg"):
        dbg = nc.dram_tensor("dbg", [NXB * P, D], F32, kind="ExternalOutput").ap()

    with contextlib.ExitStack() as es:
        kb = KB(nc, es)

        def sb(name, shape, dt):
            return Tl(es.enter_context(nc.sbuf_tensor(name, shape, dt)), name)

        cst = {}
        cst["cvec"] = sb("c_cvec", [P, NCV], F32)
        cst["cmat"] = sb("c_cmat", [P, NCM], F32)
        cst["rows"] = sb("c_rows", [P, NRW], F32)
        cst["W2"] = sb("c_W2", [33, 1024], F32)
        cst["wlr"] = sb("c_wlr", [P, KD * 32], BF16)
        cst["amask"] = sb("c_amask", [P, 1024], BF16)
        cst["identb"] = sb("c_identb", [P, P], BF16)
        cst["onesb"] = sb("c_onesb", [P, P], BF16)
        cst["onesf"] = sb("c_onesf", [1, P], F32)
        cst["esink"] = sb("c_esink", [1, 1024], F32)
        sink_sb = sb("c_sink", [1, 8], F32)
        kb.eps_t = sb("c_eps", [P, 1], F32)
        kb.one_t = sb("c_one", [P, 1], F32)
        cq = kb.sem_pool("cq", 4)
        cq2 = kb.sem_pool("cq2", 3)
        dr_in = Tl(None, "dram_in")
        dr["dr_in"] = dr_in
        dr["obs_t"] = [Tl(None, f"obs{b}") for b in range(NXB)]
        dr["x1_t"] = {(b, c): Tl(None, f"x1_{b}_{c}") for b in range(NXB) for c in range(4)}
        kb.dma(kb.sp, cq, cst["cvec"][:, :], cvec_d, [dr_in], [cst["cvec"]])
        kb.dma(kb.sp, cq, cst["cmat"][:, :], cmat_d, [dr_in], [cst["cmat"]])
        kb.dma(kb.sp, cq, cst["rows"][:, :], rows_d.partition_broadcast(P), [dr_in], [cst["rows"]])
        kb.dma(kb.sp, cq, cst["W2"][:, :], w2_d, [dr_in], [cst["W2"]])
        kb.dma(kb.sp, cq, sink_sb[:, :], sink_d, [dr_in], [sink_sb])
        kb.dma(kb.pool, cq2, cst["wlr"][:, :], wlr_d, [dr_in], [cst["wlr"]])
        kb.dma(kb.pool, cq2, cst["amask"][:, :], amask_d, [dr_in], [cst["amask"]])
        kb.dma(kb.pool, cq2, cst["identb"][:, :], cmat_d[:, CM_ID:CM_ID + P], [dr_in], [cst["identb"]])
        kb.op(kb.dve, [], [kb.eps_t], lambda: nc.vector.memset(kb.eps_t[:, :], EPS))
        kb.op(kb.dve, [], [kb.one_t], lambda: nc.vector.memset(kb.one_t[:, :], 1.0))
        kb.op(kb.dve, [], [cst["onesb"]], lambda: nc.vector.memset(cst["onesb"][:, :], 1.0))
        kb.op(kb.dve, [], [cst["onesf"]], lambda: nc.vector.memset(cst["onesf"][:, :], 1.0))
        kb.op(kb.act, [sink_sb], [sink_sb],
              lambda: nc.scalar.activation(out=sink_sb[:, :], in_=sink_sb[:, :], func=AF.Exp))
        kb.op(kb.dve, [sink_sb], [cst["esink"]],
              lambda: nc.vector.tensor_copy(out=cst["esink"][:, :].rearrange("p (h t) -> p h t", t=P),
                                            in_=sink_sb[:, :].unsqueeze(2).broadcast_to([1, 8, P])))

        out_tl = {}

        def out_rows(r0, n):
            if r0 not in out_tl:
                out_tl[r0] = Tl(None, f"dram_out{r0}")
            return out_tl[r0], out[r0:r0 + n, :]

        if mode == "ffn":
            def x1_rows(r0, n):
                return dr_in, dr["x"][r0:r0 + n, :]
        else:
            x1_rt = Tl(None, "x1_all")

            def x1_rows(r0, n):
                return x1_rt, dr["x1s"][r0:r0 + n, :]

        if mode in ("full", "dbg_bwd", "dbg_mix"):
            phase_mixer(kb, cst, dr, fwd=False)
        if mode in ("full", "dbg_mix"):
            phase_mixer(kb, cst, dr, fwd=True)
        if mode in ("full", "ffn"):
            phase_ffn(kb, cst, x1_rows, out_rows, (w_up, dr_in), (w_down, dr_in))
        if mode == "dbg_bwd":
            with nc.sbuf_tensor("dbg_t", [P, 1024], BF16) as t0, nc.sbuf_tensor("dbg_f", [P, 1024], F32) as t1:
                tt0, tt1 = Tl(t0), Tl(t1)
                dq = kb.sem_pool("dq", 2)
                for b in range(NXB):
                    kb.dma(kb.sp, dq, tt0[:, :], dr["obs"][b * P:(b + 1) * P, :], [dr_in], [tt0])
                    kb.op(kb.dve, [tt0], [tt1], lambda: nc.vector.tensor_copy(out=tt1[:, :], in_=tt0[:, :]))
                    kb.dma(kb.sp, dq, dbg[b * P:(b + 1) * P, 0:1024], tt1[:, :], [tt1], [dr_in])
                kb.barrier()
        if mode == "dbg_mix":
            with nc.sbuf_tensor("dbg_t", [P, D], F32) as t0:
                tt0 = Tl(t0)
                dq = kb.sem_pool("dq", 2)
                for b in range(NXB):
                    kb.dma(kb.sp, dq, tt0[:, :], dr["x1s"][b * P:(b + 1) * P, :], [dr_in], [tt0])
                    kb.dma(kb.sp, dq, dbg[b * P:(b + 1) * P, :], tt0[:, :], [tt0], [dr_in])
                kb.barrier()
        print(f"[build] mode={mode} instructions={kb.nins} waits={kb.nwait} sems={kb.nsem}")
    return nc


def host_layout_common(inp):
    f = lambda a: np.asarray(a, dtype=np.float32)
    w_in = f(inp["w_in"][0])
    w_out = f(inp["w_out"][0])
    w_up = f(inp["w_up"][0])
    w_down = f(inp["w_down"][0])
    wi = w_in[:, :4608].reshape(KD, P, 9, 512)
    w_in_l = np.ascontiguousarray(wi.transpose(2, 1, 0, 3)).reshape(9, P, KD * 512)
    wo = w_out.reshape(KD, P, 4, 512)
    w_out_l = np.ascontiguousarray(wo.transpose(2, 1, 0, 3)).reshape(4, P, KD * 512)
    wu = w_up.reshape(KD, P, 2, NJ // 2, 2, P)
    w_up_l = np.ascontiguousarray(wu.transpose(3, 1, 4, 0, 2, 5)).reshape(NJ // 2, P, 2 * KD * 2 * P)
    wd = w_down.reshape(4, 11, P, 4, 512)
    w_down_l = np.ascontiguousarray(wd.transpose(3, 0, 2, 1, 4)).reshape(16, P, 11 * 512)
    return dict(w_in=w_in_l, w_out=w_out_l, w_up=w_up_l, w_down=w_down_l,
                sink=f(inp["attn_sink"][0]).reshape(1, 8).copy())


def host_cvec(inp, flip):
    cv = np.zeros((P, NCV), np.float32)
    cv[:, CV_G1:CV_G1 + 16] = np.asarray(inp["norm1_g"][0]).reshape(KD, P).T
    cv[:, CV_G2:CV_G2 + 16] = np.asarray(inp["norm2_g"][0]).reshape(KD, P).T
    cw = np.asarray(inp["conv_w"][0], dtype=np.float32)
    cb = np.asarray(inp["conv_b"][0], dtype=np.float32)
    taps = (2, 1, 0) if flip else (0, 1, 2)
    cv[:, CV_CW0:CV_CW0 + 88] = cw[taps[0]].reshape(88, P).T
    cv[:, CV_CW1:CV_CW1 + 88] = cw[taps[1]].reshape(88, P).T
    cv[:, CV_CW2:CV_CW2 + 88] = cw[taps[2]].reshape(88, P).T
    cv[:, CV_CB:CV_CB + 88] = cb.reshape(88, P).T
    return cv


def host_cmat(flip=False):
    cm = np.zeros((P, NCM), np.float32)
    cm[:, CM_ID:CM_ID + P] = np.eye(P, dtype=np.float32)
    idx = np.arange(P)
    ch = idx // 64
    l = idx % 64
    same = ch[:, None] == ch[None, :]
    L = same & (l[None, :] <= l[:, None])
    Lref = same & (l[None, :] <= 32)
    U = same & (l[None, :] >= l[:, None])
    Uref = same & (l[None, :] >= 31)
    cm[:, CM_AFF:CM_AFF + P] = (L.astype(np.float32) - Lref.astype(np.float32)).T
    cm[:, CM_AFB:CM_AFB + P] = (U.astype(np.float32) - Uref.astype(np.float32)).T
    f_strict = flip
    b_strict = not flip
    mf = same & ((l[None, :] < l[:, None]) if f_strict else (l[None, :] <= l[:, None]))
    mb = same & ((l[None, :] > l[:, None]) if b_strict else (l[None, :] >= l[:, None]))
    cm[:, CM_GMF:CM_GMF + P] = mf.astype(np.float32).T
    cm[:, CM_GMB:CM_GMB + P] = mb.astype(np.float32).T
    for c in range(2):
        cm[:, CM_INF + c] = (ch == c)
        cm[:, CM_INF + 2 + c] = (ch == c) & (l <= 32)
        cm[:, CM_INB + c] = (ch == c)
        cm[:, CM_INB + 2 + c] = (ch == c) & (l >= 31)
    return cm


def host_core_consts(inp, flip):
    f = lambda a: np.asarray(a, dtype=np.float32)
    w_in = f(inp["w_in"][0])
    lr_f, lr_b = w_in[:, 4608:4624], w_in[:, 4624:4640]
    wa_f, wa_b = f(inp["gla_wa2_fwd"][0]), f(inp["gla_wa2_bwd"][0])
    ba_f, ba_b = f(inp["gla_ba_fwd"][0]), f(inp["gla_ba_bwd"][0])
    if flip:
        lr_f, lr_b, wa_f, wa_b, ba_f, ba_b = lr_b, lr_f, wa_b, wa_f, ba_b, ba_f
    wlr = np.concatenate([lr_f, lr_b], axis=1).reshape(KD, P, 32)
    w_lr = np.ascontiguousarray(wlr.transpose(1, 0, 2)).reshape(P, KD * 32)
    w2 = np.zeros((33, 1024), np.float32)
    w2[0:16, 0:512] = wa_f
    w2[16:32, 512:1024] = wa_b
    w2[32, 0:512] = ba_f
    w2[32, 512:1024] = ba_b
    rows = np.zeros((1, NRW), np.float32)
    gq, gk = f(inp["attn_q_norm_g"][0]), f(inp["attn_k_norm_g"][0])
    rows[0, RW_GQ:RW_GQ + P] = gq
    rows[0, RW_GQS:RW_GQS + P] = np.concatenate([gq[64:], gq[:64]])
    rows[0, RW_GK:RW_GK + P] = gk
    rows[0, RW_GKS:RW_GKS + P] = np.concatenate([gk[64:], gk[:64]])
    rows[0, RW_GOG:RW_GOG + 256] = f(inp["gla_out_norm_g"][0])
    t = np.arange(18 * P)
    pos = (SEQ - 1 - t) if flip else t
    inv = 1.0 / (10000.0 ** (np.arange(64, dtype=np.float32) / 64))
    ang = pos.astype(np.float32)[:, None] * inv[None, :]
    cos, sin = np.cos(ang).astype(np.float32), np.sin(ang).astype(np.float32)
    rope = np.concatenate([cos, cos, -sin, sin], axis=1).astype(np.float32)
    j = np.arange(P)[:, None]
    i = np.arange(P)[None, :]
    mL = np.where(j >= i, 0.0, NEG).astype(np.float32)
    mR = np.where(j <= i, 0.0, NEG).astype(np.float32)
    amask = np.concatenate([np.tile(mL, (1, 4)), np.tile(mR, (1, 4))], axis=1)
    return dict(w_lr=w_lr, w2=w2, rows=rows, rope=rope, amask=np.ascontiguousarray(amask),
                cvec=host_cvec(inp, flip), cmat=host_cmat(flip))


def make_in_maps(inputs, cores=range(8)):
    com = host_layout_common(inputs)
    cc = [host_core_consts(inputs, False), host_core_consts(inputs, True)]
    x = np.asarray(inputs["x"], dtype=np.float32)
    maps = []
    for c in cores:
        b, half = c // 2, c % 2
        xl = x[b] if half == 0 else x[b][::-1]
        m = dict(x=np.ascontiguousarray(xl))
        m.update(com)
        m.update(cc[half])
        maps.append(m)
    return maps


_NC_CACHE = {}


def kernel(**inputs):
    if "full" not in _NC_CACHE:
        _NC_CACHE["full"] = build_program("full")
    nc = _NC_CACHE["full"]
    maps = make_in_maps(inputs)
    res = run_bass_kernel_spmd(nc, maps, core_ids=list(range(8)))
    B = inputs["x"].shape[0]
    out = np.empty((B, SEQ, D), np.float32)
    for c in range(8):
        b, half = c // 2, c % 2
        o = res.results[c]["out"]
        if half == 0:
            out[b, :2048] = o
        else:
            out[b, 2048:] = o[::-1]
    return out
```

```python
import contextlib
import numpy as np
import concourse.bass as bass
import concourse.mybir as mybir
from concourse.bass_utils import run_bass_kernel_spmd

F32 = mybir.dt.float32
BF16 = mybir.dt.bfloat16
AF = mybir.ActivationFunctionType
ALU = mybir.AluOpType
AX = mybir.AxisListType

P = 128
D = 2048
KD = 16
SEQ = 4096
NOWN = 16
DFF = 5632
NJ = 44
EPS = 1e-6
NXB = 17


class Ev:
    __slots__ = ("sem", "key", "val", "snap", "eng")

    def __init__(self, sem, key, val, snap, eng):
        self.sem, self.key, self.val, self.snap, self.eng = sem, key, val, snap, eng


class Tl:
    def __init__(self, t, name="", psum=False):
        self.t = t
        self.name = name
        self.w = None
        self.r = {}
        self.psum = psum

    def __getitem__(self, idx):
        return self.t[idx]


class Eng:
    def __init__(self, kb, h, name, is_pe=False):
        self.kb, self.h, self.name, self.is_pe = kb, h, name, is_pe
        self.sem = kb.new_sem("e_" + name)
        self.key = "e_" + name
        self.cnt = 0
        self.known = {}


class SemPool:
    def __init__(self, kb, name, n):
        self.sems = [kb.new_sem(f"{name}{i}") for i in range(n)]
        self.keys = [f"{name}{i}" for i in range(n)]
        self.vals = [0] * n
        self.last = [None] * n
        self.i = 0


class KB:
    def __init__(self, nc, es):
        self.nc, self.es = nc, es
        self.nsem = 0
        self.pe = Eng(self, nc.tensor, "pe", True)
        self.act = Eng(self, nc.scalar, "act")
        self.dve = Eng(self, nc.vector, "dve")
        self.pool = Eng(self, nc.gpsimd, "pool")
        self.sp = Eng(self, nc.sync, "sp")
        self.engines = [self.pe, self.act, self.dve, self.pool, self.sp]
        self.pools = []
        self.nwait = 0
        self.nins = 0

    def new_sem(self, name):
        self.nsem += 1
        return self.es.enter_context(self.nc.semaphore(name))

    def sem_pool(self, name, n):
        p = SemPool(self, name, n)
        self.pools.append(p)
        return p

    def _wait(self, eng, ev):
        if ev is None:
            return
        if eng.known.get(ev.key, 0) >= ev.val:
            return
        eng.h.wait_ge(ev.sem, ev.val)
        self.nwait += 1
        kn = eng.known
        for k2, v2 in ev.snap.items():
            if kn.get(k2, 0) < v2:
                kn[k2] = v2
        kn[ev.key] = ev.val

    def _deps(self, eng, reads, writes):
        for t in reads:
            ev = t.w
            if ev is not None and not (eng.is_pe and ev.eng is eng):
                self._wait(eng, ev)
            if t.psum:
                for ev in t.r.values():
                    if ev.eng is not eng:
                        self._wait(eng, ev)
        for t in writes:
            ev = t.w
            if ev is not None and not (eng.is_pe and ev.eng is eng):
                self._wait(eng, ev)
            for ev in t.r.values():
                if not (eng.is_pe and ev.eng is eng):
                    self._wait(eng, ev)

    def _record(self, ev, reads, writes):
        for t in reads:
            t.r[ev.key] = ev
        for t in writes:
            t.w = ev
            t.r = {}

    def op(self, eng, reads, writes, fn):
        self._deps(eng, reads, writes)
        ins = fn()
        eng.cnt += 1
        ins.then_inc(eng.sem, 1)
        self.nins += 1
        ev = Ev(eng.sem, eng.key, eng.cnt, dict(eng.known), eng)
        self._record(ev, reads, writes)
        return ev

    def dma(self, q, pool, out_ap, in_ap, reads, writes, **kw):
        self._deps(q, reads, writes)
        i = pool.i
        pool.i = (pool.i + 1) % len(pool.sems)
        if pool.last[i] is not None:
            self._wait(q, pool.last[i])
        ins = q.h.dma_start(out=out_ap, in_=in_ap, **kw)
        pool.vals[i] += 16
        ins.then_inc(pool.sems[i], 16)
        self.nins += 1
        ev = Ev(pool.sems[i], pool.keys[i], pool.vals[i], dict(q.known), None)
        pool.last[i] = ev
        self._record(ev, reads, writes)
        return ev

    def barrier(self):
        evs = []
        for e in self.engines:
            if e.cnt > 0:
                evs.append(Ev(e.sem, e.key, e.cnt, {}, e))
        for p in self.pools:
            for ev in p.last:
                if ev is not None:
                    evs.append(ev)
        for e in self.engines:
            for ev in evs:
                if ev.eng is e:
                    continue
                self._wait(e, ev)

    def mm(self, out_t, out_ap, lhs_t, lhs_ap, rhs_t, rhs_ap, start=True, stop=True):
        nc = self.nc
        return self.op(self.pe, [lhs_t, rhs_t], [out_t],
                       lambda: nc.tensor.matmul(out_ap, lhsT=lhs_ap, rhs=rhs_ap, start=start, stop=stop))

    def tr(self, out_t, out_ap, in_t, in_ap, id_t, id_ap):
        nc = self.nc
        return self.op(self.pe, [in_t, id_t], [out_t],
                       lambda: nc.tensor.transpose(out_ap, in_ap, id_ap))


CV_G1 = 0
CV_G2 = 16
CV_CW0 = 32
CV_CW1 = 120
CV_CW2 = 208
CV_CB = 296
NCV = 384

CM_ID = 0
CM_AFF = 128
CM_AFB = 256
CM_GMF = 384
CM_GMB = 512
CM_INF = 640
CM_INB = 644
NCM = 648

RW_GQ = 0
RW_GQS = 128
RW_GK = 256
RW_GKS = 384
RW_GOG = 512
NRW = 768
NEG = -30000.0
QS = 128 ** -0.5


def rmsnorm_to_T(kb, es_tiles, src_t, src_ap, nrows, gcol, dstT_t, dst_fn, bank_tiles, consts):
    nc = kb.nc
    junk, ss, lnv, rstd, hb = es_tiles
    if hb is None:
        hb = src_t
    ident, cvec = consts
    kb.op(kb.act, [src_t], [junk, ss],
          lambda: nc.scalar.activation(out=junk[0:nrows, :], in_=src_ap, func=AF.Square,
                                       accum_out=ss[0:nrows, :]))
    kb.op(kb.act, [ss, kb.eps_t], [lnv],
          lambda: nc.scalar.activation(out=lnv[0:nrows, :], in_=ss[0:nrows, :], func=AF.Ln,
                                       scale=1.0 / D, bias=kb.eps_t[0:nrows, :]))
    kb.op(kb.act, [lnv], [rstd],
          lambda: nc.scalar.activation(out=rstd[0:nrows, :], in_=lnv[0:nrows, :], func=AF.Exp, scale=-0.5))
    kb.op(kb.dve, [src_t, rstd], [hb],
          lambda: nc.vector.tensor_scalar(out=hb[0:nrows, 0:D], in0=src_ap, scalar1=rstd[0:nrows, :],
                                          scalar2=None, op0=ALU.mult))
    for g in range(4):
        bt = bank_tiles[g % len(bank_tiles)]
        for kk in range(4):
            k = g * 4 + kk
            kb.tr(bt, bt[:, kk * P:kk * P + nrows], hb, hb[0:nrows, k * P:(k + 1) * P],
                  ident, ident[0:nrows, 0:nrows])
        src = bt[:, :].rearrange("p (a b) -> p a b", b=P)[:, :, 0:nrows]
        gb = cvec[:, gcol + g * 4:gcol + g * 4 + 4].unsqueeze(2).broadcast_to([P, 4, nrows])
        kb.op(kb.dve, [bt, cvec], [dstT_t],
              lambda src=src, gb=gb, g=g: nc.vector.tensor_tensor(out=dst_fn(g * 4, 4), in0=src, in1=gb,
                                                                  op=ALU.mult))


def phase_ffn(kb, cst, x1_rows, out_rows, w_up, w_down, tiles=(0, 1, 2, 3)):
    nc = kb.nc
    ident, cvec = cst["cmat"], cst["cvec"]
    with contextlib.ExitStack() as es:
        def sb(name, shape, dt):
            return Tl(es.enter_context(nc.sbuf_tensor(name, shape, dt)), name)

        def psb(name):
            return Tl(es.enter_context(nc.psum_tensor(name, [P, 512], F32)), name, psum=True)

        h2T = sb("f_h2T", [P, KD, 514], BF16)
        aT = sb("f_aT", [P, NJ, 512], BF16)
        xb = [sb(f"f_xb{i}", [P, D], F32) for i in range(5)]
        hb = sb("f_hb", [P, D], F32)
        junk = sb("f_junk", [P, D], BF16)
        ss = sb("f_ss", [P, 1], F32)
        lnv = sb("f_lnv", [P, 1], F32)
        rstd = sb("f_rstd", [P, 1], F32)
        wsl = [sb(f"f_w{i}", [P, 8192], BF16) for i in range(3)]
        NT = 2
        t1g = [sb(f"f_t1g{i}", [P, 512], F32) for i in range(NT)]
        t1v = [sb(f"f_t1v{i}", [P, 512], F32) for i in range(NT)]
        sg = [sb(f"f_sg{i}", [P, 512], F32) for i in range(NT)]
        bk = [psb(f"f_ps{i}") for i in range(8)]
        nt_tiles = (junk, ss, lnv, rstd, hb)
        wq = kb.sem_pool("f_wq", 3)
        xq = kb.sem_pool("f_xq", 4)
        oq = kb.sem_pool("f_oq", 4)

        kb.op(kb.dve, [], [h2T], lambda: nc.vector.memset(h2T[:, :, 0:1], 0.0))
        wi = 0
        xi = 0
        for ti in tiles:
            r0 = ti * 512
            if ti != tiles[0]:
                kb.op(kb.dve, [h2T], [h2T],
                      lambda: nc.vector.tensor_copy(out=h2T[:, :, 0:1], in_=h2T[:, :, 512:513]))
            xs = []
            for b in range(4):
                xt = xb[xi % 5]
                xi += 1
                st, sap = x1_rows(r0 + b * P, P)
                kb.dma(kb.sp, xq, xt[:, :], sap, [st], [xt])
                xs.append(xt)
                rmsnorm_to_T(kb, nt_tiles, xt, xt[:, :], P, CV_G2, h2T,
                             lambda k0, nk, b=b: h2T[:, k0:k0 + nk, 1 + b * P:1 + (b + 1) * P],
                             [bk[5], bk[6], bk[7]], (ident, cvec))
            xh = xb[xi % 5]
            xi += 1
            st, sap = x1_rows(r0 + 512, 1)
            kb.dma(kb.sp, xq, xh[0:1, :], sap, [st], [xh])
            rmsnorm_to_T(kb, nt_tiles, xh, xh[0:1, :], 1, CV_G2, h2T,
                         lambda k0, nk: h2T[:, k0:k0 + nk, 513:514],
                         [bk[5], bk[6], bk[7]], (ident, cvec))
            for jp in range(NJ // 2):
                w = wsl[wi % 3]
                wi += 1
                kb.dma(kb.pool, wq, w[:, :], w_up[0][jp], [w_up[1]], [w], max_dma_last_dim=4096)
                wv = w[:, :].rearrange("p (j k g c) -> p j k g c", j=2, k=KD, g=2)
                for jj in range(2):
                    j = 2 * jp + jj
                    s = j % 2
                    pg = (bk[4 * s], bk[4 * s + 1])
                    pv = (bk[4 * s + 2], bk[4 * s + 3])
                    for gv, pts in ((0, pg), (1, pv)):
                        for k in range(KD):
                            for hf in range(2):
                                pt = pts[hf]
                                kb.mm(pt, pt[:, 0:258], w, wv[:, jj, k, gv, :], h2T,
                                      h2T[:, k, 256 * hf:256 * hf + 258],
                                      start=(k == 0), stop=(k == KD - 1))
                    tg, tv, sgt = t1g[j % NT], t1v[j % NT], sg[j % NT]
                    for gv, pts, tt, m in ((0, pg, tg, j), (1, pv, tv, NJ + j)):
                        c0 = cvec[:, CV_CW0 + m:CV_CW0 + m + 1]
                        c1 = cvec[:, CV_CW1 + m:CV_CW1 + m + 1]
                        c2 = cvec[:, CV_CW2 + m:CV_CW2 + m + 1]
                        cb = cvec[:, CV_CB + m:CV_CB + m + 1]
                        for hf in range(2):
                            pt = pts[hf]
                            o0 = 256 * hf
                            kb.op(kb.act, [pt, cvec], [tt],
                                  lambda pt=pt, tt=tt, c1=c1, cb=cb, o0=o0: nc.scalar.activation(
                                      out=tt[:, o0:o0 + 256], in_=pt[:, 1:257], func=AF.Identity,
                                      scale=c1, bias=cb))
                            kb.op(kb.dve, [pt, cvec, tt], [tt],
                                  lambda pt=pt, tt=tt, c0=c0, o0=o0: nc.vector.scalar_tensor_tensor(
                                      out=tt[:, o0:o0 + 256], in0=pt[:, 0:256], scalar=c0,
                                      in1=tt[:, o0:o0 + 256], op0=ALU.mult, op1=ALU.add))
                            kb.op(kb.dve, [pt, cvec, tt], [tt],
                                  lambda pt=pt, tt=tt, c2=c2, o0=o0: nc.vector.scalar_tensor_tensor(
                                      out=tt[:, o0:o0 + 256], in0=pt[:, 2:258], scalar=c2,
                                      in1=tt[:, o0:o0 + 256], op0=ALU.mult, op1=ALU.add))
                    kb.op(kb.act, [tg], [sgt],
                          lambda tg=tg, sgt=sgt: nc.scalar.activation(out=sgt[:, :], in_=tg[:, :],
                                                                      func=AF.Exp, scale=-1.0))
                    kb.op(kb.act, [sgt, kb.one_t], [sgt],
                          lambda sgt=sgt: nc.scalar.activation(out=sgt[:, :], in_=sgt[:, :],
                                                               func=AF.Ln, bias=kb.one_t[:, :]))
                    kb.op(kb.act, [sgt], [sgt],
                          lambda sgt=sgt: nc.scalar.activation(out=sgt[:, :], in_=sgt[:, :],
                                                               func=AF.Exp, scale=-1.0))
                    kb.op(kb.dve, [tg, sgt], [sgt],
                          lambda tg=tg, sgt=sgt: nc.vector.tensor_tensor(out=sgt[:, :], in0=tg[:, :],
                                                                         in1=sgt[:, :], op=ALU.mult))
                    kb.op(kb.dve, [tv, sgt], [aT],
                          lambda tv=tv, sgt=sgt, j=j: nc.vector.tensor_tensor(out=aT[:, j, :], in0=tv[:, :],
                                                                              in1=sgt[:, :], op=ALU.mult))
            dbk = [bk[0], bk[1], bk[2], bk[3]]
            for c in range(4):
                for q in range(4):
                    w = wsl[wi % 3]
                    wi += 1
                    kb.dma(kb.pool, wq, w[:, 0:11 * 512], w_down[0][c * 4 + q], [w_down[1]], [w],
                           max_dma_last_dim=4096)
                    wv = w[:, 0:11 * 512].rearrange("p (k c) -> p k c", k=11)
                    for b in range(4):
                        for kk in range(11):
                            j = q * 11 + kk
                            kb.mm(dbk[b], dbk[b][:, :], aT, aT[:, j, b * P:(b + 1) * P], w, wv[:, kk, :],
                                  start=(j == 0), stop=(j == NJ - 1))
                for b in range(4):
                    xt = xs[b]
                    kb.op(kb.dve, [dbk[b], xt], [xt],
                          lambda b=b, xt=xt, c=c: nc.vector.tensor_tensor(
                              out=xt[:, c * 512:(c + 1) * 512], in0=dbk[b][:, :],
                              in1=xt[:, c * 512:(c + 1) * 512], op=ALU.add))
            for b in range(4):
                ot, oap = out_rows(r0 + b * P, P)
                kb.dma(kb.sp, oq, oap, xs[b][:, :], [xs[b]], [ot])
        kb.barrier()


def phase_mixer(kb, cst, dr, fwd):
    nc = kb.nc
    cvec, cmat, rows = cst["cvec"], cst["cmat"], cst["rows"]
    ident = cmat
    dr_in = dr["dr_in"]
    with contextlib.ExitStack() as es:
        def sb(name, shape, dt):
            return Tl(es.enter_context(nc.sbuf_tensor(name, shape, dt)), name)

        pfx = "c_" if fwd else "b_"
        banks = [Tl(es.enter_context(nc.psum_tensor(f"{pfx}ps{i}", [P, 512], F32)), f"ps{i}", psum=True)
                 for i in range(8)]
        bi = [0]

        def nb():
            t = banks[bi[0] % 8]
            bi[0] += 1
            return t

        b4i = [0]

        def nb4():
            t = banks[4 + b4i[0] % 4]
            b4i[0] += 1
            return t

        hT = sb(pfx + "hT", [P, KD, 5 * P], BF16)
        NXI = 2
        xin = [sb(pfx + f"x{i}", [P, D], F32) for i in range(NXI)]
        hb = None
        junk = sb(pfx + "junk", [P, D], BF16)
        ss = sb(pfx + "ss", [P, 1], F32)
        lnv = sb(pfx + "lnv", [P, 1], F32)
        rstd = sb(pfx + "rstd", [P, 1], F32)
        nt_tiles = (junk, ss, lnv, rstd, hb)
        NW = 2 if fwd else 3
        wsl = [sb(pfx + f"w{i}", [P, 8192], BF16) for i in range(NW)]
        big = [sb(pfx + f"big{i}", [P, 1024], BF16) for i in range(4)]
        qeT = [sb(pfx + f"qeT{i}", [P, 512], BF16) for i in range(4)]
        keT = [sb(pfx + f"keT{i}", [P, 512], BF16) for i in range(4)]
        ke = [sb(pfx + f"ke{i}", [P, 512], BF16) for i in range(4)]
        vg = [sb(pfx + f"vg{i}", [P, 1024], BF16) for i in range(4)]
        gsc = [sb(pfx + f"gsc{i}", [P, 4, 6], F32) for i in range(4)]
        S = sb(pfx + "S", [P, 4, 256], F32)
        Sbf = [sb(pfx + f"Sbf{h}", [P, 256], BF16) for h in range(4)]
        lr_sb = [sb(pfx + f"lr{i}", [P, 32], F32) for i in range(4)]
        lrT = [sb(pfx + f"lrT{i}", [33, P], F32) for i in range(4)]
        spt = None if fwd else [sb(pfx + f"spt{i}", [P, 512], F32) for i in range(4)]
        bsm = [sb(pfx + f"bsm{i}", [P, 4, 4], F32) for i in range(4)]
        dlt = [sb(pfx + f"dlt{i}", [P, 4, 2], F32) for i in range(4)]
        tA = [sb(pfx + f"tA{i}", [P, 512], F32) for i in range(2)]
        aTs = sb(pfx + "aTs", [P, 512], BF16)
        oin = None if fwd else sb(pfx + "oin", [P, 1024], F32)
        obuf = [sb(pfx + f"ob{i}", [P, 1024], BF16) for i in range(2)]
        wq = kb.sem_pool(pfx + "wq", NW)
        xq = kb.sem_pool(pfx + "xq", 4)
        oq = kb.sem_pool(pfx + "oq", 4)
        if fwd:
            qT = [sb(f"c_qT{i}", [P, 1024], BF16) for i in range(4)]
            kT = [sb(f"c_kT{i}", [P, 256], BF16) for i in range(6)]
            va = [sb(f"c_va{i}", [P, 256], BF16) for i in range(6)]
            mixA = [sb(f"c_mixA{i}", [P, 8, P], BF16) for i in range(4)]
            mixG = [sb(f"c_mixG{i}", [P, 8, P], BF16) for i in range(4)]
            csb = [sb(f"c_cs{i}", [P, 256], F32) for i in range(2)]
            Gq = [sb(f"c_Gq{i}", [P, 256], F32) for i in range(2)]
            Gk = [sb(f"c_Gk{i}", [P, 256], F32) for i in range(2)]
            tB = [sb(f"c_tB{i}", [P, 512], F32) for i in range(2)]
            spt = [tA[0], tA[1], tB[0], tB[1]]
            tq = [sb(f"c_tq{i}", [P, 512], F32) for i in range(2)]
            ssq = sb("c_ssq", [P, 4], F32)
            sqj = sb("c_sqj", [P, 256], BF16)
            lnq = sb("c_lnq", [P, 4], F32)
            rsq = sb("c_rsq", [P, 4], F32)
            PT = [sb(f"c_PT{i}", [P, 512], BF16) for i in range(6)]
            rec = [sb(f"c_rec{i}", [P, 512], F32) for i in range(2)]
            osum = sb("c_osum", [P, 1024], F32)
            xr = [sb(f"c_xr{i}", [P, 512], F32) for i in range(2)]
            xo = [sb(f"c_xo{i}", [P, 512], F32) for i in range(2)]
            rq = kb.sem_pool("c_rq", 3)
        AFT = cmat[:, CM_AFF:CM_AFF + P] if fwd else cmat[:, CM_AFB:CM_AFB + P]
        GMT = cmat[:, CM_GMF:CM_GMF + P] if fwd else cmat[:, CM_GMB:CM_GMB + P]
        IND = cmat[:, CM_INF:CM_INF + 4] if fwd else cmat[:, CM_INB:CM_INB + 4]
        W2 = cst["W2"]
        w2c = 0 if fwd else 512
        wlr = cst["wlr"]

        kb.op(kb.dve, [], [S], lambda: nc.vector.memset(S[:, :, :], 0.0))
        for i in range(4):
            kb.op(kb.dve, [], [lrT[i]], lambda i=i: nc.vector.memset(lrT[i][:, :], 1.0))
        cnt = dict(w=0, x=0, t=0, ob=0, xr=0)

        def load_w(src_ap, ncols=8192):
            w = wsl[cnt["w"] % NW]
            cnt["w"] += 1
            kb.dma(kb.pool, wq, w[:, 0:ncols], src_ap, [dr_in], [w], max_dma_last_dim=4096)
            return w

        def proj(i, w, wv_fn, ncols):
            bk = nb()
            for k in range(KD):
                kb.mm(bk, bk[:, 0:ncols], hT, hT[:, k, i * P:(i + 1) * P], w, wv_fn(k),
                      start=(k == 0), stop=(k == KD - 1))
            return bk

        def gating_all(idxs):
            A = AF
            bk = {i: proj(i, wlr, lambda k: wlr[:, k * 32:(k + 1) * 32], 32) for i in idxs}
            for i in idxs:
                kb.op(kb.act, [bk[i]], [lr_sb[i]],
                      lambda i=i: nc.scalar.copy(out=lr_sb[i][:, :], in_=bk[i][:, 0:32]))
            b2 = {}
            for i in idxs:
                b2[i] = nb()
                kb.tr(b2[i], b2[i][0:32, 0:P], lr_sb[i], lr_sb[i][:, :], ident, ident[:, 0:P])
            for i in idxs:
                kb.op(kb.act, [b2[i]], [lrT[i]],
                      lambda i=i: nc.scalar.copy(out=lrT[i][0:32, :], in_=b2[i][0:32, 0:P]))
            b3 = {}
            for i in idxs:
                b3[i] = nb()
                kb.mm(b3[i], b3[i][:, :], lrT[i], lrT[i][0:33, :], W2, W2[0:33, w2c:w2c + 512])
            for i in idxs:
                kb.op(kb.act, [b3[i]], [spt[i]],
                      lambda i=i: nc.scalar.activation(out=spt[i][:, :], in_=b3[i][:, :], func=A.Exp, scale=-1.0))
            for i in idxs:
                kb.op(kb.act, [spt[i], kb.one_t], [spt[i]],
                      lambda i=i: nc.scalar.activation(out=spt[i][:, :], in_=spt[i][:, :], func=A.Ln,
                                                       bias=kb.one_t[:, :]))
            b4 = {}
            for i in idxs:
                b4[i] = nb()
                kb.mm(b4[i], b4[i][:, :], cmat, AFT, spt[i], spt[i][:, :])
            for i in idxs:
                kb.op(kb.act, [b4[i]], [big[i]],
                      lambda i=i: nc.scalar.activation(out=big[i][:, 0:512], in_=b4[i][:, :], func=A.Exp,
                                                       scale=-1.0 / 16))
                kb.op(kb.act, [b4[i]], [big[i]],
                      lambda i=i: nc.scalar.activation(out=big[i][:, 512:1024], in_=b4[i][:, :], func=A.Exp,
                                                       scale=1.0 / 16))
            b5 = {}
            for i in idxs:
                b5[i] = nb()
                for h in range(4):
                    kb.mm(b5[i], b5[i][:, 4 * h:4 * h + 4], spt[i], spt[i][:, h * P:(h + 1) * P], cmat, IND)
            for i in idxs:
                kb.op(kb.act, [b5[i]], [bsm[i]],
                      lambda i=i: nc.scalar.copy(out=bsm[i][:, :, :],
                                                 in_=b5[i][:, 0:16].rearrange("p (h c) -> p h c", c=4)))
            for i in idxs:
                kb.op(kb.dve, [bsm[i]], [dlt[i]],
                      lambda i=i: nc.vector.tensor_tensor(out=dlt[i][:, :, :], in0=bsm[i][:, :, 0:2],
                                                          in1=bsm[i][:, :, 2:4], op=ALU.subtract))
            for i in idxs:
                kb.op(kb.act, [bsm[i]], [gsc[i]],
                      lambda i=i: nc.scalar.activation(out=gsc[i][:, :, 0:4], in_=bsm[i][:, :, :], func=A.Exp,
                                                       scale=-1.0 / 16))
                kb.op(kb.act, [dlt[i]], [gsc[i]],
                      lambda i=i: nc.scalar.activation(out=gsc[i][:, :, 4:6], in_=dlt[i][:, :, :], func=A.Exp,
                                                       scale=-1.0 / 16))

        def to_T(src_t, nchunks, dst_t, dst_ap):
            b2 = nb()
            for h in range(nchunks):
                kb.tr(b2, b2[:, h * P:(h + 1) * P], src_t, src_t[:, h * P:(h + 1) * P], ident, ident[:, 0:P])
            kb.op(kb.act, [b2], [dst_t], lambda: nc.scalar.copy(out=dst_ap, in_=b2[:, 0:nchunks * P]))

        def cons_qg(i, bk):
            t = tA[cnt["t"] % 2]
            cnt["t"] += 1
            kb.op(kb.dve, [bk, big[i]], [t],
                  lambda: nc.vector.scalar_tensor_tensor(out=t[:, :], in0=bk[:, :], scalar=QS,
                                                         in1=big[i][:, 0:512], op0=ALU.mult, op1=ALU.mult))
            return lambda: to_T(t, 4, qeT[i], qeT[i][:, :])

        def cons_kg(i, bk):
            t = tA[cnt["t"] % 2]
            cnt["t"] += 1
            kb.op(kb.dve, [bk, big[i]], [t],
                  lambda: nc.vector.tensor_tensor(out=t[:, :], in0=bk[:, :], in1=big[i][:, 512:1024],
                                                  op=ALU.mult))
            kb.op(kb.act, [t], [ke[i]], lambda: nc.scalar.copy(out=ke[i][:, :], in_=t[:, :]))
            return lambda: to_T(t, 4, keT[i], keT[i][:, :])

        def cons_vg(i, bk, c):
            kb.op(kb.act, [bk], [vg[i]],
                  lambda: nc.scalar.copy(out=vg[i][:, c * 512:(c + 1) * 512], in_=bk[:, :]))

        def cons_gate(i, bk, c):
            t = tq[cnt["t"] % 2]
            cnt["t"] += 1
            kb.op(kb.act, [bk], [t],
                  lambda: nc.scalar.activation(out=t[:, :], in_=bk[:, :], func=AF.Exp, scale=-1.0))
            kb.op(kb.act, [t, kb.one_t], [t],
                  lambda: nc.scalar.activation(out=t[:, :], in_=t[:, :], func=AF.Ln, bias=kb.one_t[:, :]))
            kb.op(kb.act, [t], [t],
                  lambda: nc.scalar.activation(out=t[:, :], in_=t[:, :], func=AF.Exp, scale=-1.0))
            kb.op(kb.dve, [bk, t], [big[i]],
                  lambda: nc.vector.tensor_tensor(out=big[i][:, c * 512:(c + 1) * 512], in0=bk[:, :],
                                                  in1=t[:, :], op=ALU.mult))

        def rope_tables(b):
            cs = csb[b % 2]
            kb.dma(kb.sp, rq, cs[:, :], dr["rope"][b * P:(b + 1) * P, :], [dr_in], [cs])
            for G, g0, gs0 in ((Gq[b % 2], RW_GQ, RW_GQS), (Gk[b % 2], RW_GK, RW_GKS)):
                kb.op(kb.dve, [cs, rows], [G],
                      lambda G=G, g0=g0: nc.vector.tensor_tensor(out=G[:, 0:P], in0=cs[:, 0:P],
                                                                 in1=rows[:, g0:g0 + P], op=ALU.mult))
                kb.op(kb.dve, [cs, rows], [G],
                      lambda G=G, gs0=gs0: nc.vector.tensor_tensor(out=G[:, P:2 * P], in0=cs[:, P:2 * P],
                                                                   in1=rows[:, gs0:gs0 + P], op=ALU.mult))

        def norm_rope(bk, c0, nh, G):
            t = tq[cnt["t"] % 2]
            a = tA[cnt["t"] % 2]
            bb = tB[cnt["t"] % 2]
            cnt["t"] += 1
            n = nh * P
            src = bk[:, c0:c0 + n]
            src3 = src.rearrange("p (h d) -> p h d", d=P)
            kb.op(kb.act, [bk], [t], lambda: nc.scalar.activation(out=t[:, 0:n], in_=src, func=AF.Square))
            kb.op(kb.dve, [t], [ssq],
                  lambda: nc.vector.tensor_reduce(out=ssq[:, 0:nh], in_=t[:, 0:n].rearrange("p (h d) -> p h d", d=P),
                                                  axis=AX.X, op=ALU.add))
            kb.op(kb.act, [ssq, kb.eps_t], [lnq],
                  lambda: nc.scalar.activation(out=lnq[:, 0:nh], in_=ssq[:, 0:nh], func=AF.Ln, scale=1.0 / P,
                                               bias=kb.eps_t[:, :]))
            kb.op(kb.act, [lnq], [rsq],
                  lambda: nc.scalar.activation(out=rsq[:, 0:nh], in_=lnq[:, 0:nh], func=AF.Exp, scale=-0.5))
            a3 = a[:, 0:n].rearrange("p (h d) -> p h d", d=P)
            b3 = bb[:, 0:n].rearrange("p (h d) -> p h d", d=P)
            kb.op(kb.dve, [bk, G], [a],
                  lambda: nc.vector.tensor_tensor(out=a3, in0=src3,
                                                  in1=G[:, 0:P].unsqueeze(1).broadcast_to([P, nh, P]),
                                                  op=ALU.mult))
            kb.op(kb.dve, [bk, G], [bb],
                  lambda: nc.vector.tensor_tensor(out=b3[:, :, 0:64], in0=src3[:, :, 64:128],
                                                  in1=G[:, P:P + 64].unsqueeze(1).broadcast_to([P, nh, 64]),
                                                  op=ALU.mult))
            kb.op(kb.dve, [bk, G], [bb],
                  lambda: nc.vector.tensor_tensor(out=b3[:, :, 64:128], in0=src3[:, :, 0:64],
                                                  in1=G[:, P + 64:2 * P].unsqueeze(1).broadcast_to([P, nh, 64]),
                                                  op=ALU.mult))
            kb.op(kb.dve, [a, bb], [a],
                  lambda: nc.vector.tensor_tensor(out=a[:, 0:n], in0=a[:, 0:n], in1=bb[:, 0:n], op=ALU.add))
            kb.op(kb.dve, [a, rsq], [a],
                  lambda: nc.vector.tensor_tensor(out=a3, in0=a3,
                                                  in1=rsq[:, 0:nh].unsqueeze(2).broadcast_to([P, nh, P]),
                                                  op=ALU.mult))
            return a

        def cons_q(i, b, bk, c):
            a = norm_rope(bk, 0, 4, Gq[b % 2])
            return lambda: to_T(a, 4, qT[i], qT[i][:, c * 512:(c + 1) * 512])

        def cons_kv(b, bk):
            a = norm_rope(bk, 0, 2, Gk[b % 2])
            kb.op(kb.act, [bk], [va[b % 6]], lambda: nc.scalar.copy(out=va[b % 6][:, :], in_=bk[:, 256:512]))
            return lambda: to_T(a, 2, kT[b % 6], kT[b % 6][:, :])

        def gla_block(i, b, order, emit):
            bI = bN = None
            if emit:
                bA = nb4()
                for h in range(4):
                    kb.mm(bA, bA[:, h * P:(h + 1) * P], keT[i], keT[i][:, h * P:(h + 1) * P],
                          qeT[i], qeT[i][:, h * P:(h + 1) * P])
                kb.op(kb.dve, [bA, cmat], [aTs],
                      lambda: nc.vector.tensor_tensor(out=aTs[:, :].rearrange("p (h i) -> p h i", i=P),
                                                      in0=bA[:, :].rearrange("p (h i) -> p h i", i=P),
                                                      in1=GMT.unsqueeze(1).broadcast_to([P, 4, P]), op=ALU.mult))
                bI = [banks[0], banks[1]]
                for h in range(4):
                    o = bI[h // 2]
                    kb.mm(o, o[:, (h % 2) * 256:(h % 2) * 256 + 256], aTs, aTs[:, h * P:(h + 1) * P],
                          vg[i], vg[i][:, h * 256:(h + 1) * 256])
                bN = [banks[2], banks[3]]
            for c in order:
                r0 = 64 * c
                if emit:
                    for h in range(4):
                        kb.op(kb.act, [S, gsc[i]], [Sbf[h]],
                              lambda h=h: nc.scalar.activation(out=Sbf[h][:, :], in_=S[:, h, :], func=AF.Copy,
                                                               scale=gsc[i][:, h, 2 + c:3 + c]))
                    for h in range(4):
                        o = bN[h // 2]
                        kb.mm(o, o[r0:r0 + 64, (h % 2) * 256:(h % 2) * 256 + 256],
                              qeT[i], qeT[i][:, h * P + r0:h * P + r0 + 64], Sbf[h], Sbf[h][:, :])
                bM = [nb4(), nb4()]
                for h in range(4):
                    o = bM[h // 2]
                    kb.mm(o, o[:, (h % 2) * 256:(h % 2) * 256 + 256], ke[i], ke[i][r0:r0 + 64, h * P:(h + 1) * P],
                          vg[i], vg[i][r0:r0 + 64, h * 256:(h + 1) * 256])
                for h in range(4):
                    o = bM[h // 2]
                    kb.op(kb.dve, [S, gsc[i]], [S],
                          lambda h=h: nc.vector.tensor_scalar(out=S[:, h, :], in0=S[:, h, :],
                                                              scalar1=gsc[i][:, h, c:c + 1], scalar2=None,
                                                              op0=ALU.mult))
                    kb.op(kb.dve, [o, S, gsc[i]], [S],
                          lambda h=h, o=o: nc.vector.scalar_tensor_tensor(
                              out=S[:, h, :], in0=o[:, (h % 2) * 256:(h % 2) * 256 + 256],
                              scalar=gsc[i][:, h, 4 + c:5 + c], in1=S[:, h, :], op0=ALU.mult, op1=ALU.add))
            return bI, bN

        def attention(i, b):
            kbs = [kk for kk in (b - 1, b, b + 1) if kk >= 0]
            for h2 in range(2):
                for n_, kk in enumerate(kbs):
                    bS = nb4()
                    masked = kk != b
                    kb.mm(bS, bS[:, :], kT[kk % 6], kT[kk % 6][:, h2 * P:(h2 + 1) * P],
                          qT[i], qT[i][:, h2 * 512:(h2 + 1) * 512], start=True, stop=not masked)
                    if masked:
                        mk = cst["amask"]
                        mc = 0 if kk < b else 512
                        kb.mm(bS, bS[:, :], cst["identb"], cst["identb"][:, :], mk, mk[:, mc:mc + 512],
                              start=False, stop=True)
                    pt = PT[h2 * 3 + n_]
                    kb.op(kb.act, [bS], [pt],
                          lambda pt=pt, bS=bS: nc.scalar.activation(out=pt[:, :], in_=bS[:, :], func=AF.Exp,
                                                                    scale=QS))
            for h2 in range(2):
                bD = nb4()
                bO = nb4()
                pts = [PT[h2 * 3 + n_] for n_ in range(len(kbs))]
                for n_, kk in enumerate(kbs):
                    kb.mm(bD, bD[:, :], cst["onesb"], cst["onesb"][:, :], pts[n_], pts[n_][:, :],
                          start=(n_ == 0), stop=False)
                kb.mm(bD, bD[:, :], cst["onesf"], cst["onesf"][0:1, :], cst["esink"],
                      cst["esink"][0:1, h2 * 512:(h2 + 1) * 512], start=False, stop=True)
                for n_, kk in enumerate(kbs):
                    kb.mm(bO, bO[:, :], va[kk % 6], va[kk % 6][:, h2 * P:(h2 + 1) * P], pts[n_], pts[n_][:, :],
                          start=(n_ == 0), stop=(n_ == len(kbs) - 1))
                rc = rec[h2]
                kb.op(kb.dve, [bD], [rc], lambda rc=rc, bD=bD: nc.vector.reciprocal(out=rc[:, :], in_=bD[:, :]))
                kb.op(kb.dve, [bO, rc], [mixA[i]],
                      lambda rc=rc, bO=bO, h2=h2: nc.vector.tensor_tensor(
                          out=mixA[i][:, 4 * h2:4 * h2 + 4, :], in0=bO[:, :].rearrange("p (h t) -> p h t", t=P),
                          in1=rc[:, :].rearrange("p (h t) -> p h t", t=P), op=ALU.mult))

        def combine(i, b, bI, bN):
            ob = obuf[cnt["ob"] % 2]
            cnt["ob"] += 1
            kb.dma(kb.sp, xq, ob[:, :], dr["obs"][b * P:(b + 1) * P, :], [dr["obs_t"][b]], [ob])
            for hh in range(2):
                kb.op(kb.act, [bN[hh]], [osum],
                      lambda hh=hh: nc.scalar.copy(out=osum[:, hh * 512:(hh + 1) * 512], in_=bN[hh][:, :]))
            for hh in range(2):
                kb.op(kb.dve, [bI[hh], osum], [osum],
                      lambda hh=hh: nc.vector.tensor_tensor(out=osum[:, hh * 512:(hh + 1) * 512], in0=bI[hh][:, :],
                                                            in1=osum[:, hh * 512:(hh + 1) * 512], op=ALU.add))
            kb.op(kb.dve, [osum, ob], [osum],
                  lambda: nc.vector.tensor_tensor(out=osum[:, :], in0=osum[:, :], in1=ob[:, :], op=ALU.add))
            for h in range(4):
                kb.op(kb.act, [osum], [sqj, ssq],
                      lambda h=h: nc.scalar.activation(out=sqj[:, 0:256], in_=osum[:, h * 256:(h + 1) * 256],
                                                       func=AF.Square, accum_out=ssq[:, h:h + 1]))
            kb.op(kb.act, [ssq, kb.eps_t], [lnq],
                  lambda: nc.scalar.activation(out=lnq[:, :], in_=ssq[:, :], func=AF.Ln, scale=1.0 / 256,
                                               bias=kb.eps_t[:, :]))
            kb.op(kb.act, [lnq], [rsq],
                  lambda: nc.scalar.activation(out=rsq[:, :], in_=lnq[:, :], func=AF.Exp, scale=-0.5))
            for h in range(4):
                kb.op(kb.dve, [osum, rsq, big[i]], [osum],
                      lambda h=h: nc.vector.scalar_tensor_tensor(
                          out=osum[:, h * 256:(h + 1) * 256], in0=osum[:, h * 256:(h + 1) * 256],
                          scalar=rsq[:, h:h + 1], in1=big[i][:, h * 256:(h + 1) * 256],
                          op0=ALU.mult, op1=ALU.mult))
            kb.op(kb.dve, [osum, rows], [osum],
                  lambda: nc.vector.tensor_tensor(
                      out=osum[:, :].rearrange("p (h e) -> p h e", e=256),
                      in0=osum[:, :].rearrange("p (h e) -> p h e", e=256),
                      in1=rows[:, RW_GOG:RW_GOG + 256].unsqueeze(1).broadcast_to([P, 4, 256]), op=ALU.mult))
            for hh in range(2):
                b2 = nb4()
                for q in range(4):
                    kb.tr(b2, b2[:, q * P:(q + 1) * P], osum, osum[:, (hh * 4 + q) * P:(hh * 4 + q + 1) * P],
                          ident, ident[:, 0:P])
                kb.op(kb.act, [b2], [mixG[i]],
                      lambda hh=hh, b2=b2: nc.scalar.copy(out=mixG[i][:, 4 * hh:4 * hh + 4, :],
                                                          in_=b2[:, :].rearrange("p (k t) -> p k t", t=P)))

        def load_norm(b, i):
            xt = xin[cnt["x"] % NXI]
            cnt["x"] += 1
            kb.dma(kb.sp, xq, xt[:, :], dr["x"][b * P:(b + 1) * P, :], [dr_in], [xt])
            rmsnorm_to_T(kb, nt_tiles, xt, xt[:, :], P, CV_G1, hT,
                         lambda k0, nk: hT[:, k0:k0 + nk, i * P:(i + 1) * P],
                         [nb(), nb(), nb(), nb()], (ident, cvec))

        w_in = dr["w_in"]

        def chunk(c, idxs, cons):
            w = load_w(w_in[c])
            wv = w[:, :].rearrange("p (k c) -> p k c", k=KD)
            pend = None
            for i in idxs:
                bk = proj(i, w, lambda k: wv[:, k, :], 512)
                fin = cons(i, bk)
                if pend is not None:
                    pend()
                pend = fin
            if pend is not None:
                pend()

        if not fwd:
            tiles = [list(range(t * 4 + 3, t * 4 - 1, -1)) for t in range(7, -1, -1)]
            for blocks in tiles:
                emits = [b <= NXB - 1 for b in blocks]
                for i, b in enumerate(blocks):
                    load_norm(b, i)
                gating_all(range(4))
                if any(emits):
                    chunk(3, [i for i in range(4) if emits[i]], cons_qg)
                chunk(4, range(4), cons_kg)
                chunk(5, range(4), lambda i, bk: cons_vg(i, bk, 0))
                chunk(6, range(4), lambda i, bk: cons_vg(i, bk, 1))
                for i, b in enumerate(blocks):
                    bI, bN = gla_block(i, b, (1, 0), emits[i])
                    if emits[i]:
                        ob = obuf[cnt["ob"] % 2]
                        cnt["ob"] += 1
                        for hh in range(2):
                            kb.op(kb.act, [bN[hh]], [oin],
                                  lambda hh=hh: nc.scalar.copy(out=oin[:, hh * 512:(hh + 1) * 512],
                                                               in_=bN[hh][:, :]))
                            kb.op(kb.dve, [bI[hh], oin], [ob],
                                  lambda hh=hh: nc.vector.tensor_tensor(
                                      out=ob[:, hh * 512:(hh + 1) * 512], in0=bI[hh][:, :],
                                      in1=oin[:, hh * 512:(hh + 1) * 512], op=ALU.add))
                        kb.dma(kb.sp, oq, dr["obs"][b * P:(b + 1) * P, :], ob[:, :], [ob], [dr["obs_t"][b]])
        else:
            tiles = [[0, 1, 2, 3], [4, 5, 6, 7], [8, 9, 10, 11], [12, 13, 14, 15], [16]]
            for blocks in tiles:
                nbk = len(blocks)
                ext = blocks[-1] + 1
                for i, b in enumerate(blocks + [ext]):
                    load_norm(b, i)
                gating_all(range(nbk))
                chunk(3, range(nbk), cons_qg)
                chunk(4, range(nbk), cons_kg)
                chunk(5, range(nbk), lambda i, bk: cons_vg(i, bk, 0))
                chunk(6, range(nbk), lambda i, bk: cons_vg(i, bk, 1))
                chunk(7, range(nbk), lambda i, bk: cons_gate(i, bk, 0))
                chunk(8, range(nbk), lambda i, bk: cons_gate(i, bk, 1))
                w = load_w(w_in[2])
                wv = w[:, :].rearrange("p (k c) -> p k c", k=KD)
                pend = None
                for i, b in enumerate(blocks + [ext]):
                    rope_tables(b)
                    bk = proj(i, w, lambda k: wv[:, k, :], 512)
                    fin = cons_kv(b, bk)
                    if pend is not None:
                        pend()
                    pend = fin
                pend()
                for c in (0, 1):
                    w = load_w(w_in[c])
                    wv = w[:, :].rearrange("p (k c) -> p k c", k=KD)
                    pend = None
                    for i, b in enumerate(blocks):
                        rope_tables(b)
                        bk = proj(i, w, lambda k: wv[:, k, :], 512)
                        fin = cons_q(i, b, bk, c)
                        if pend is not None:
                            pend()
                        pend = fin
                    pend()
                prev = None
                for i, b in enumerate(blocks):
                    attention(i, b)
                    if prev is not None:
                        combine(*prev)
                    bI, bN = gla_block(i, b, (0, 1), True)
                    prev = (i, b, bI, bN)
                combine(*prev)
                for c in range(4):
                    w = load_w(dr["w_out"][c])
                    wv = w[:, :].rearrange("p (k c) -> p k c", k=KD)
                    for i, b in enumerate(blocks):
                        xrt = xr[cnt["xr"] % 2]
                        xot = xo[cnt["xr"] % 2]
                        cnt["xr"] += 1
                        kb.dma(kb.sp, xq, xrt[:, :], dr["x"][b * P:(b + 1) * P, c * 512:(c + 1) * 512],
                               [dr_in], [xrt])
                        bk = nb()
                        for k in range(KD):
                            mt = mixA[i] if k < 8 else mixG[i]
                            kb.mm(bk, bk[:, :], mt, mt[:, k % 8, :], w, wv[:, k, :],
                                  start=(k == 0), stop=(k == KD - 1))
                        kb.op(kb.dve, [bk, xrt], [xot],
                              lambda bk=bk, xrt=xrt, xot=xot: nc.vector.tensor_tensor(
                                  out=xot[:, :], in0=bk[:, :], in1=xrt[:, :], op=ALU.add))
                        kb.dma(kb.sp, oq, dr["x1s"][b * P:(b + 1) * P, c * 512:(c + 1) * 512], xot[:, :],
                               [xot], [dr["x1_t"][(b, c)]])
        kb.barrier()


def build_program(mode="full"):
    nc = bass.Bass("TRN2", target_bir_lowering=False)

    def din(name, shape, dt=F32):
        return nc.dram_tensor(name, shape, dt, kind="ExternalInput").ap()

    dr = {}
    dr["x"] = din("x", [SEQ, D])
    dr["w_in"] = din("w_in", [9, P, KD * 512])
    wlr_d = din("w_lr", [P, KD * 32])
    w2_d = din("w2", [33, 1024])
    dr["w_out"] = din("w_out", [4, P, KD * 512])
    w_up = din("w_up", [NJ // 2, P, 2 * KD * 2 * P])
    w_down = din("w_down", [16, P, 11 * 512])
    cvec_d = din("cvec", [P, NCV])
    cmat_d = din("cmat", [P, NCM])
    rows_d = din("rows", [1, NRW])
    sink_d = din("sink", [1, 8])
    dr["rope"] = din("rope", [18 * P, 256])
    amask_d = din("amask", [P, 1024])
    out = nc.dram_tensor("out", [NOWN * P, D], F32, kind="ExternalOutput").ap()
    dr["x1s"] = nc.dram_tensor("x1s", [NXB * P, D], F32, kind="Internal").ap()
    dr["obs"] = nc.dram_tensor("obs", [NXB * P, 1024], BF16, kind="Internal").ap()
    dbg = None
    if mode.startswith("dbg"):
        dbg = nc.dram_tensor("dbg", [NXB * P, D], F32, kind="ExternalOutput").ap()

    with contextlib.ExitStack() as es:
        kb = KB(nc, es)

        def sb(name, shape, dt):
            return Tl(es.enter_context(nc.sbuf_tensor(name, shape, dt)), name)

        cst = {}
        cst["cvec"] = sb("c_cvec", [P, NCV], F32)
        cst["cmat"] = sb("c_cmat", [P, NCM], F32)
        cst["rows"] = sb("c_rows", [P, NRW], F32)
        cst["W2"] = sb("c_W2", [33, 1024], F32)
        cst["wlr"] = sb("c_wlr", [P, KD * 32], BF16)
        cst["amask"] = sb("c_amask", [P, 1024], BF16)
        cst["identb"] = sb("c_identb", [P, P], BF16)
        cst["onesb"] = sb("c_onesb", [P, P], BF16)
        cst["onesf"] = sb("c_onesf", [1, P], F32)
        cst["esink"] = sb("c_esink", [1, 1024], F32)
        sink_sb = sb("c_sink", [1, 8], F32)
        kb.eps_t = sb("c_eps", [P, 1], F32)
        kb.one_t = sb("c_one", [P, 1], F32)
        cq = kb.sem_pool("cq", 4)
        cq2 = kb.sem_pool("cq2", 3)
        dr_in = Tl(None, "dram_in")
        dr["dr_in"] = dr_in
        dr["obs_t"] = [Tl(None, f"obs{b}") for b in range(NXB)]
        dr["x1_t"] = {(b, c): Tl(None, f"x1_{b}_{c}") for b in range(NXB) for c in range(4)}
        kb.dma(kb.sp, cq, cst["cvec"][:, :], cvec_d, [dr_in], [cst["cvec"]])
        kb.dma(kb.sp, cq, cst["cmat"][:, :], cmat_d, [dr_in], [cst["cmat"]])
        kb.dma(kb.sp, cq, cst["rows"][:, :], rows_d.partition_broadcast(P), [dr_in], [cst["rows"]])
        kb.dma(kb.sp, cq, cst["W2"][:, :], w2_d, [dr_in], [cst["W2"]])
        kb.dma(kb.sp, cq, sink_sb[:, :], sink_d, [dr_in], [sink_sb])
        kb.dma(kb.pool, cq2, cst["wlr"][:, :], wlr_d, [dr_in], [cst["wlr"]])
        kb.dma(kb.pool, cq2, cst["amask"][:, :], amask_d, [dr_in], [cst["amask"]])
        kb.dma(kb.pool, cq2, cst["identb"][:, :], cmat_d[:, CM_ID:CM_ID + P], [dr_in], [cst["identb"]])
        kb.op(kb.dve, [], [kb.eps_t], lambda: nc.vector.memset(kb.eps_t[:, :], EPS))
        kb.op(kb.dve, [], [kb.one_t], lambda: nc.vector.memset(kb.one_t[:, :], 1.0))
        kb.op(kb.dve, [], [cst["onesb"]], lambda: nc.vector.memset(cst["onesb"][:, :], 1.0))
        kb.op(kb.dve, [], [cst["onesf"]], lambda: nc.vector.memset(cst["onesf"][:, :], 1.0))
        kb.op(kb.act, [sink_sb], [sink_sb],
              lambda: nc.scalar.activation(out=sink_sb[:, :], in_=sink_sb[:, :], func=AF.Exp))
        kb.op(kb.dve, [sink_sb], [cst["esink"]],
              lambda: nc.vector.tensor_copy(out=cst["esink"][:, :].rearrange("p (h t) -> p h t", t=P),
                                            in_=sink_sb[:, :].unsqueeze(2).broadcast_to([1, 8, P])))

        out_tl = {}

        def out_rows(r0, n):
            if r0 not in out_tl:
                out_tl[r0] = Tl(None, f"dram_out{r0}")
            return out_tl[r0], out[r0:r0 + n, :]

        if mode == "ffn":
            def x1_rows(r0, n):
                return dr_in, dr["x"][r0:r0 + n, :]
        else:
            x1_rt = Tl(None, "x1_all")

            def x1_rows(r0, n):
                return x1_rt, dr["x1s"][r0:r0 + n, :]

        if mode in ("full", "dbg_bwd", "dbg_mix"):
            phase_mixer(kb, cst, dr, fwd=False)
        if mode in ("full", "dbg_mix"):
            phase_mixer(kb, cst, dr, fwd=True)
        if mode in ("full", "ffn"):
            phase_ffn(kb, cst, x1_rows, out_rows, (w_up, dr_in), (w_down, dr_in))
        if mode == "dbg_bwd":
            with nc.sbuf_tensor("dbg_t", [P, 1024], BF16) as t0, nc.sbuf_tensor("dbg_f", [P, 1024], F32) as t1:
                tt0, tt1 = Tl(t0), Tl(t1)
                dq = kb.sem_pool("dq", 2)
                for b in range(NXB):
                    kb.dma(kb.sp, dq, tt0[:, :], dr["obs"][b * P:(b + 1) * P, :], [dr_in], [tt0])
                    kb.op(kb.dve, [tt0], [tt1], lambda: nc.vector.tensor_copy(out=tt1[:, :], in_=tt0[:, :]))
                    kb.dma(kb.sp, dq, dbg[b * P:(b + 1) * P, 0:1024], tt1[:, :], [tt1], [dr_in])
                kb.barrier()
        if mode == "dbg_mix":
            with nc.sbuf_tensor("dbg_t", [P, D], F32) as t0:
                tt0 = Tl(t0)
                dq = kb.sem_pool("dq", 2)
                for b in range(NXB):
                    kb.dma(kb.sp, dq, tt0[:, :], dr["x1s"][b * P:(b + 1) * P, :], [dr_in], [tt0])
                    kb.dma(kb.sp, dq, dbg[b * P:(b + 1) * P, :], tt0[:, :], [tt0], [dr_in])
                kb.barrier()
        print(f"[build] mode={mode} instructions={kb.nins} waits={kb.nwait} sems={kb.nsem}")
    return nc


def host_layout_common(inp):
    f = lambda a: np.asarray(a, dtype=np.float32)
    w_in = f(inp["w_in"][0])
    w_out = f(inp["w_out"][0])
    w_up = f(inp["w_up"][0])
    w_down = f(inp["w_down"][0])
    wi = w_in[:, :4608].reshape(KD, P, 9, 512)
    w_in_l = np.ascontiguousarray(wi.transpose(2, 1, 0, 3)).reshape(9, P, KD * 512)
    wo = w_out.reshape(KD, P, 4, 512)
    w_out_l = np.ascontiguousarray(wo.transpose(2, 1, 0, 3)).reshape(4, P, KD * 512)
    wu = w_up.reshape(KD, P, 2, NJ // 2, 2, P)
    w_up_l = np.ascontiguousarray(wu.transpose(3, 1, 4, 0, 2, 5)).reshape(NJ // 2, P, 2 * KD * 2 * P)
    wd = w_down.reshape(4, 11, P, 4, 512)
    w_down_l = np.ascontiguousarray(wd.transpose(3, 0, 2, 1, 4)).reshape(16, P, 11 * 512)
    return dict(w_in=w_in_l, w_out=w_out_l, w_up=w_up_l, w_down=w_down_l,
                sink=f(inp["attn_sink"][0]).reshape(1, 8).copy())


def host_cvec(inp, flip):
    cv = np.zeros((P, NCV), np.float32)
    cv[:, CV_G1:CV_G1 + 16] = np.asarray(inp["norm1_g"][0]).reshape(KD, P).T
    cv[:, CV_G2:CV_G2 + 16] = np.asarray(inp["norm2_g"][0]).reshape(KD, P).T
    cw = np.asarray(inp["conv_w"][0], dtype=np.float32)
    cb = np.asarray(inp["conv_b"][0], dtype=np.float32)
    taps = (2, 1, 0) if flip else (0, 1, 2)
    cv[:, CV_CW0:CV_CW0 + 88] = cw[taps[0]].reshape(88, P).T
    cv[:, CV_CW1:CV_CW1 + 88] = cw[taps[1]].reshape(88, P).T
    cv[:, CV_CW2:CV_CW2 + 88] = cw[taps[2]].reshape(88, P).T
    cv[:, CV_CB:CV_CB + 88] = cb.reshape(88, P).T
    return cv


def host_cmat(flip=False):
    cm = np.zeros((P, NCM), np.float32)
    cm[:, CM_ID:CM_ID + P] = np.eye(P, dtype=np.float32)
    idx = np.arange(P)
    ch = idx // 64
    l = idx % 64
    same = ch[:, None] == ch[None, :]
    L = same & (l[None, :] <= l[:, None])
    Lref = same & (l[None, :] <= 32)
    U = same & (l[None, :] >= l[:, None])
    Uref = same & (l[None, :] >= 31)
    cm[:, CM_AFF:CM_AFF + P] = (L.astype(np.float32) - Lref.astype(np.float32)).T
    cm[:, CM_AFB:CM_AFB + P] = (U.astype(np.float32) - Uref.astype(np.float32)).T
    f_strict = flip
    b_strict = not flip
    mf = same & ((l[None, :] < l[:, None]) if f_strict else (l[None, :] <= l[:, None]))
    mb = same & ((l[None, :] > l[:, None]) if b_strict else (l[None, :] >= l[:, None]))
    cm[:, CM_GMF:CM_GMF + P] = mf.astype(np.float32).T
    cm[:, CM_GMB:CM_GMB + P] = mb.astype(np.float32).T
    for c in range(2):
        cm[:, CM_INF + c] = (ch == c)
        cm[:, CM_INF + 2 + c] = (ch == c) & (l <= 32)
        cm[:, CM_INB + c] = (ch == c)
        cm[:, CM_INB + 2 + c] = (ch == c) & (l >= 31)
    return cm


def host_core_consts(inp, flip):
    f = lambda a: np.asarray(a, dtype=np.float32)
    w_in = f(inp["w_in"][0])
    lr_f, lr_b = w_in[:, 4608:4624], w_in[:, 4624:4640]
    wa_f, wa_b = f(inp["gla_wa2_fwd"][0]), f(inp["gla_wa2_bwd"][0])
    ba_f, ba_b = f(inp["gla_ba_fwd"][0]), f(inp["gla_ba_bwd"][0])
    if flip:
        lr_f, lr_b, wa_f, wa_b, ba_f, ba_b = lr_b, lr_f, wa_b, wa_f, ba_b, ba_f
    wlr = np.concatenate([lr_f, lr_b], axis=1).reshape(KD, P, 32)
    w_lr = np.ascontiguousarray(wlr.transpose(1, 0, 2)).reshape(P, KD * 32)
    w2 = np.zeros((33, 1024), np.float32)
    w2[0:16, 0:512] = wa_f
    w2[16:32, 512:1024] = wa_b
    w2[32, 0:512] = ba_f
    w2[32, 512:1024] = ba_b
    rows = np.zeros((1, NRW), np.float32)
    gq, gk = f(inp["attn_q_norm_g"][0]), f(inp["attn_k_norm_g"][0])
    rows[0, RW_GQ:RW_GQ + P] = gq
    rows[0, RW_GQS:RW_GQS + P] = np.concatenate([gq[64:], gq[:64]])
    rows[0, RW_GK:RW_GK + P] = gk
    rows[0, RW_GKS:RW_GKS + P] = np.concatenate([gk[64:], gk[:64]])
    rows[0, RW_GOG:RW_GOG + 256] = f(inp["gla_out_norm_g"][0])
    t = np.arange(18 * P)
    pos = (SEQ - 1 - t) if flip else t
    inv = 1.0 / (10000.0 ** (np.arange(64, dtype=np.float32) / 64))
    ang = pos.astype(np.float32)[:, None] * inv[None, :]
    cos, sin = np.cos(ang).astype(np.float32), np.sin(ang).astype(np.float32)
    rope = np.concatenate([cos, cos, -sin, sin], axis=1).astype(np.float32)
    j = np.arange(P)[:, None]
    i = np.arange(P)[None, :]
    mL = np.where(j >= i, 0.0, NEG).astype(np.float32)
    mR = np.where(j <= i, 0.0, NEG).astype(np.float32)
    amask = np.concatenate([np.tile(mL, (1, 4)), np.tile(mR, (1, 4))], axis=1)
    return dict(w_lr=w_lr, w2=w2, rows=rows, rope=rope, amask=np.ascontiguousarray(amask),
                cvec=host_cvec(inp, flip), cmat=host_cmat(flip))


def make_in_maps(inputs, cores=range(8)):
    com = host_layout_common(inputs)
    cc = [host_core_consts(inputs, False), host_core_consts(inputs, True)]
    x = np.asarray(inputs["x"], dtype=np.float32)
    maps = []
    for c in cores:
        b, half = c // 2, c % 2
        xl = x[b] if half == 0 else x[b][::-1]
        m = dict(x=np.ascontiguousarray(xl))
        m.update(com)
        m.update(cc[half])
        maps.append(m)
    return maps


_NC_CACHE = {}


def kernel(**inputs):
    if "full" not in _NC_CACHE:
        _NC_CACHE["full"] = build_program("full")
    nc = _NC_CACHE["full"]
    maps = make_in_maps(inputs)
    res = run_bass_kernel_spmd(nc, maps, core_ids=list(range(8)))
    B = inputs["x"].shape[0]
    out = np.empty((B, SEQ, D), np.float32)
    for c in range(8):
        b, half = c // 2, c % 2
        o = res.results[c]["out"]
        if half == 0:
            out[b, :2048] = o
        else:
            out[b, 2048:] = o[::-1]
    return out
```

```python
import contextlib
import numpy as np
import concourse.bass as bass
import concourse.mybir as mybir
from concourse.bass_utils import run_bass_kernel_spmd

F32 = mybir.dt.float32
BF16 = mybir.dt.bfloat16
AF = mybir.ActivationFunctionType
ALU = mybir.AluOpType
AX = mybir.AxisListType

P = 128
D = 2048
KD = 16
SEQ = 4096
NOWN = 16
DFF = 5632
NJ = 44
EPS = 1e-6
NXB = 17


class Ev:
    __slots__ = ("sem", "key", "val", "snap", "eng")

    def __init__(self, sem, key, val, snap, eng):
        self.sem, self.key, self.val, self.snap, self.eng = sem, key, val, snap, eng


class Tl:
    def __init__(self, t, name="", psum=False):
        self.t = t
        self.name = name
        self.w = None
        self.r = {}
        self.psum = psum

    def __getitem__(self, idx):
        return self.t[idx]


class Eng:
    def __init__(self, kb, h, name, is_pe=False):
        self.kb, self.h, self.name, self.is_pe = kb, h, name, is_pe
        self.sem = kb.new_sem("e_" + name)
        self.key = "e_" + name
        self.cnt = 0
        self.known = {}


class SemPool:
    def __init__(self, kb, name, n):
        self.sems = [kb.new_sem(f"{name}{i}") for i in range(n)]
        self.keys = [f"{name}{i}" for i in range(n)]
        self.vals = [0] * n
        self.last = [None] * n
        self.i = 0


class KB:
    def __init__(self, nc, es):
        self.nc, self.es = nc, es
        self.nsem = 0
        self.pe = Eng(self, nc.tensor, "pe", True)
        self.act = Eng(self, nc.scalar, "act")
        self.dve = Eng(self, nc.vector, "dve")
        self.pool = Eng(self, nc.gpsimd, "pool")
        self.sp = Eng(self, nc.sync, "sp")
        self.engines = [self.pe, self.act, self.dve, self.pool, self.sp]
        self.pools = []
        self.nwait = 0
        self.nins = 0

    def new_sem(self, name):
        self.nsem += 1
        return self.es.enter_context(self.nc.semaphore(name))

    def sem_pool(self, name, n):
        p = SemPool(self, name, n)
        self.pools.append(p)
        return p

    def _wait(self, eng, ev):
        if ev is None:
            return
        if eng.known.get(ev.key, 0) >= ev.val:
            return
        eng.h.wait_ge(ev.sem, ev.val)
        self.nwait += 1
        kn = eng.known
        for k2, v2 in ev.snap.items():
            if kn.get(k2, 0) < v2:
                kn[k2] = v2
        kn[ev.key] = ev.val

    def _deps(self, eng, reads, writes):
        for t in reads:
            ev = t.w
            if ev is not None and not (eng.is_pe and ev.eng is eng):
                self._wait(eng, ev)
            if t.psum:
                for ev in t.r.values():
                    if ev.eng is not eng:
                        self._wait(eng, ev)
        for t in writes:
            ev = t.w
            if ev is not None and not (eng.is_pe and ev.eng is eng):
                self._wait(eng, ev)
            for ev in t.r.values():
                if not (eng.is_pe and ev.eng is eng):
                    self._wait(eng, ev)

    def _record(self, ev, reads, writes):
        for t in reads:
            t.r[ev.key] = ev
        for t in writes:
            t.w = ev
            t.r = {}

    def op(self, eng, reads, writes, fn):
        self._deps(eng, reads, writes)
        ins = fn()
        eng.cnt += 1
        ins.then_inc(eng.sem, 1)
        self.nins += 1
        ev = Ev(eng.sem, eng.key, eng.cnt, dict(eng.known), eng)
        self._record(ev, reads, writes)
        return ev

    def dma(self, q, pool, out_ap, in_ap, reads, writes, **kw):
        self._deps(q, reads, writes)
        i = pool.i
        pool.i = (pool.i + 1) % len(pool.sems)
        if pool.last[i] is not None:
            self._wait(q, pool.last[i])
        ins = q.h.dma_start(out=out_ap, in_=in_ap, **kw)
        pool.vals[i] += 16
        ins.then_inc(pool.sems[i], 16)
        self.nins += 1
        ev = Ev(pool.sems[i], pool.keys[i], pool.vals[i], dict(q.known), None)
        pool.last[i] = ev
        self._record(ev, reads, writes)
        return ev

    def barrier(self):
        evs = []
        for e in self.engines:
            if e.cnt > 0:
                evs.append(Ev(e.sem, e.key, e.cnt, {}, e))
        for p in self.pools:
            for ev in p.last:
                if ev is not None:
                    evs.append(ev)
        for e in self.engines:
            for ev in evs:
                if ev.eng is e:
                    continue
                self._wait(e, ev)

    def mm(self, out_t, out_ap, lhs_t, lhs_ap, rhs_t, rhs_ap, start=True, stop=True):
        nc = self.nc
        return self.op(self.pe, [lhs_t, rhs_t], [out_t],
                       lambda: nc.tensor.matmul(out_ap, lhsT=lhs_ap, rhs=rhs_ap, start=start, stop=stop))

    def tr(self, out_t, out_ap, in_t, in_ap, id_t, id_ap):
        nc = self.nc
        return self.op(self.pe, [in_t, id_t], [out_t],
                       lambda: nc.tensor.transpose(out_ap, in_ap, id_ap))


CV_G1 = 0
CV_G2 = 16
CV_CW0 = 32
CV_CW1 = 120
CV_CW2 = 208
CV_CB = 296
NCV = 384

CM_ID = 0
CM_AFF = 128
CM_AFB = 256
CM_GMF = 384
CM_GMB = 512
CM_INF = 640
CM_INB = 644
NCM = 648

RW_GQ = 0
RW_GQS = 128
RW_GK = 256
RW_GKS = 384
RW_GOG = 512
NRW = 768
NEG = -30000.0
QS = 128 ** -0.5


def rmsnorm_to_T(kb, es_tiles, src_t, src_ap, nrows, gcol, dstT_t, dst_fn, bank_tiles, consts):
    nc = kb.nc
    junk, ss, lnv, rstd, hb = es_tiles
    if hb is None:
        hb = src_t
    ident, cvec = consts
    kb.op(kb.act, [src_t], [junk, ss],
          lambda: nc.scalar.activation(out=junk[0:nrows, :], in_=src_ap, func=AF.Square,
                                       accum_out=ss[0:nrows, :]))
    kb.op(kb.act, [ss, kb.eps_t], [lnv],
          lambda: nc.scalar.activation(out=lnv[0:nrows, :], in_=ss[0:nrows, :], func=AF.Ln,
                                       scale=1.0 / D, bias=kb.eps_t[0:nrows, :]))
    kb.op(kb.act, [lnv], [rstd],
          lambda: nc.scalar.activation(out=rstd[0:nrows, :], in_=lnv[0:nrows, :], func=AF.Exp, scale=-0.5))
    kb.op(kb.dve, [src_t, rstd], [hb],
          lambda: nc.vector.tensor_scalar(out=hb[0:nrows, 0:D], in0=src_ap, scalar1=rstd[0:nrows, :],
                                          scalar2=None, op0=ALU.mult))
    for g in range(4):
        bt = bank_tiles[g % len(bank_tiles)]
        for kk in range(4):
            k = g * 4 + kk
            kb.tr(bt, bt[:, kk * P:kk * P + nrows], hb, hb[0:nrows, k * P:(k + 1) * P],
                  ident, ident[0:nrows, 0:nrows])
        src = bt[:, :].rearrange("p (a b) -> p a b", b=P)[:, :, 0:nrows]
        gb = cvec[:, gcol + g * 4:gcol + g * 4 + 4].unsqueeze(2).broadcast_to([P, 4, nrows])
        kb.op(kb.dve, [bt, cvec], [dstT_t],
              lambda src=src, gb=gb, g=g: nc.vector.tensor_tensor(out=dst_fn(g * 4, 4), in0=src, in1=gb,
                                                                  op=ALU.mult))


def phase_ffn(kb, cst, x1_rows, out_rows, w_up, w_down, tiles=(0, 1, 2, 3)):
    nc = kb.nc
    ident, cvec = cst["cmat"], cst["cvec"]
    with contextlib.ExitStack() as es:
        def sb(name, shape, dt):
            return Tl(es.enter_context(nc.sbuf_tensor(name, shape, dt)), name)

        def psb(name):
            return Tl(es.enter_context(nc.psum_tensor(name, [P, 512], F32)), name, psum=True)

        h2T = sb("f_h2T", [P, KD, 514], BF16)
        aT = sb("f_aT", [P, NJ, 512], BF16)
        xb = [sb(f"f_xb{i}", [P, D], F32) for i in range(5)]
        hb = sb("f_hb", [P, D], F32)
        junk = sb("f_junk", [P, D], BF16)
        ss = sb("f_ss", [P, 1], F32)
        lnv = sb("f_lnv", [P, 1], F32)
        rstd = sb("f_rstd", [P, 1], F32)
        wsl = [sb(f"f_w{i}", [P, 8192], BF16) for i in range(3)]
        NT = 2
        t1g = [sb(f"f_t1g{i}", [P, 512], F32) for i in range(NT)]
        t1v = [sb(f"f_t1v{i}", [P, 512], F32) for i in range(NT)]
        sg = [sb(f"f_sg{i}", [P, 512], F32) for i in range(NT)]
        bk = [psb(f"f_ps{i}") for i in range(8)]
        nt_tiles = (junk, ss, lnv, rstd, hb)
        wq = kb.sem_pool("f_wq", 3)
        xq = kb.sem_pool("f_xq", 4)
        oq = kb.sem_pool("f_oq", 4)

        kb.op(kb.dve, [], [h2T], lambda: nc.vector.memset(h2T[:, :, 0:1], 0.0))
        wi = 0
        xi = 0
        for ti in tiles:
            r0 = ti * 512
            if ti != tiles[0]:
                kb.op(kb.dve, [h2T], [h2T],
                      lambda: nc.vector.tensor_copy(out=h2T[:, :, 0:1], in_=h2T[:, :, 512:513]))
            xs = []
            for b in range(4):
                xt = xb[xi % 5]
                xi += 1
                st, sap = x1_rows(r0 + b * P, P)
                kb.dma(kb.sp, xq, xt[:, :], sap, [st], [xt])
                xs.append(xt)
                rmsnorm_to_T(kb, nt_tiles, xt, xt[:, :], P, CV_G2, h2T,
                             lambda k0, nk, b=b: h2T[:, k0:k0 + nk, 1 + b * P:1 + (b + 1) * P],
                             [bk[5], bk[6], bk[7]], (ident, cvec))
            xh = xb[xi % 5]
            xi += 1
            st, sap = x1_rows(r0 + 512, 1)
            kb.dma(kb.sp, xq, xh[0:1, :], sap, [st], [xh])
            rmsnorm_to_T(kb, nt_tiles, xh, xh[0:1, :], 1, CV_G2, h2T,
                         lambda k0, nk: h2T[:, k0:k0 + nk, 513:514],
                         [bk[5], bk[6], bk[7]], (ident, cvec))
            for jp in range(NJ // 2):
                w = wsl[wi % 3]
                wi += 1
                kb.dma(kb.pool, wq, w[:, :], w_up[0][jp], [w_up[1]], [w], max_dma_last_dim=4096)
                wv = w[:, :].rearrange("p (j k g c) -> p j k g c", j=2, k=KD, g=2)
                for jj in range(2):
                    j = 2 * jp + jj
                    s = j % 2
                    pg = (bk[4 * s], bk[4 * s + 1])
                    pv = (bk[4 * s + 2], bk[4 * s + 3])
                    for gv, pts in ((0, pg), (1, pv)):
                        for k in range(KD):
                            for hf in range(2):
                                pt = pts[hf]
                                kb.mm(pt, pt[:, 0:258], w, wv[:, jj, k, gv, :], h2T,
                                      h2T[:, k, 256 * hf:256 * hf + 258],
                                      start=(k == 0), stop=(k == KD - 1))
                    tg, tv, sgt = t1g[j % NT], t1v[j % NT], sg[j % NT]
                    for gv, pts, tt, m in ((0, pg, tg, j), (1, pv, tv, NJ + j)):
                        c0 = cvec[:, CV_CW0 + m:CV_CW0 + m + 1]
                        c1 = cvec[:, CV_CW1 + m:CV_CW1 + m + 1]
                        c2 = cvec[:, CV_CW2 + m:CV_CW2 + m + 1]
                        cb = cvec[:, CV_CB + m:CV_CB + m + 1]
                        for hf in range(2):
                            pt = pts[hf]
                            o0 = 256 * hf
                            kb.op(kb.act, [pt, cvec], [tt],
                                  lambda pt=pt, tt=tt, c1=c1, cb=cb, o0=o0: nc.scalar.activation(
                                      out=tt[:, o0:o0 + 256], in_=pt[:, 1:257], func=AF.Identity,
                                      scale=c1, bias=cb))
                            kb.op(kb.dve, [pt, cvec, tt], [tt],
                                  lambda pt=pt, tt=tt, c0=c0, o0=o0: nc.vector.scalar_tensor_tensor(
                                      out=tt[:, o0:o0 + 256], in0=pt[:, 0:256], scalar=c0,
                                      in1=tt[:, o0:o0 + 256], op0=ALU.mult, op1=ALU.add))
                            kb.op(kb.dve, [pt, cvec, tt], [tt],
                                  lambda pt=pt, tt=tt, c2=c2, o0=o0: nc.vector.scalar_tensor_tensor(
                                      out=tt[:, o0:o0 + 256], in0=pt[:, 2:258], scalar=c2,
                                      in1=tt[:, o0:o0 + 256], op0=ALU.mult, op1=ALU.add))
                    kb.op(kb.act, [tg], [sgt],
                          lambda tg=tg, sgt=sgt: nc.scalar.activation(out=sgt[:, :], in_=tg[:, :],
                                                                      func=AF.Exp, scale=-1.0))
                    kb.op(kb.act, [sgt, kb.one_t], [sgt],
                          lambda sgt=sgt: nc.scalar.activation(out=sgt[:, :], in_=sgt[:, :],
                                                               func=AF.Ln, bias=kb.one_t[:, :]))
                    kb.op(kb.act, [sgt], [sgt],
                          lambda sgt=sgt: nc.scalar.activation(out=sgt[:, :], in_=sgt[:, :],
                                                               func=AF.Exp, scale=-1.0))
                    kb.op(kb.dve, [tg, sgt], [sgt],
                          lambda tg=tg, sgt=sgt: nc.vector.tensor_tensor(out=sgt[:, :], in0=tg[:, :],
                                                                         in1=sgt[:, :], op=ALU.mult))
                    kb.op(kb.dve, [tv, sgt], [aT],
                          lambda tv=tv, sgt=sgt, j=j: nc.vector.tensor_tensor(out=aT[:, j, :], in0=tv[:, :],
                                                                              in1=sgt[:, :], op=ALU.mult))
            dbk = [bk[0], bk[1], bk[2], bk[3]]
            for c in range(4):
                for q in range(4):
                    w = wsl[wi % 3]
                    wi += 1
                    kb.dma(kb.pool, wq, w[:, 0:11 * 512], w_down[0][c * 4 + q], [w_down[1]], [w],
                           max_dma_last_dim=4096)
                    wv = w[:, 0:11 * 512].rearrange("p (k c) -> p k c", k=11)
                    for b in range(4):
                        for kk in range(11):
                            j = q * 11 + kk
                            kb.mm(dbk[b], dbk[b][:, :], aT, aT[:, j, b * P:(b + 1) * P], w, wv[:, kk, :],
                                  start=(j == 0), stop=(j == NJ - 1))
                for b in range(4):
                    xt = xs[b]
                    kb.op(kb.dve, [dbk[b], xt], [xt],
                          lambda b=b, xt=xt, c=c: nc.vector.tensor_tensor(
                              out=xt[:, c * 512:(c + 1) * 512], in0=dbk[b][:, :],
                              in1=xt[:, c * 512:(c + 1) * 512], op=ALU.add))
            for b in range(4):
                ot, oap = out_rows(r0 + b * P, P)
                kb.dma(kb.sp, oq, oap, xs[b][:, :], [xs[b]], [ot])
        kb.barrier()


def phase_mixer(kb, cst, dr, fwd):
    nc = kb.nc
    cvec, cmat, rows = cst["cvec"], cst["cmat"], cst["rows"]
    ident = cmat
    dr_in = dr["dr_in"]
    with contextlib.ExitStack() as es:
        def sb(name, shape, dt):
            return Tl(es.enter_context(nc.sbuf_tensor(name, shape, dt)), name)

        pfx = "c_" if fwd else "b_"
        banks = [Tl(es.enter_context(nc.psum_tensor(f"{pfx}ps{i}", [P, 512], F32)), f"ps{i}", psum=True)
                 for i in range(8)]
        bi = [0]

        def nb():
            t = banks[bi[0] % 8]
            bi[0] += 1
            return t

        b4i = [0]

        def nb4():
            t = banks[4 + b4i[0] % 4]
            b4i[0] += 1
            return t

        hT = sb(pfx + "hT", [P, KD, 5 * P], BF16)
        NXI = 2
        xin = [sb(pfx + f"x{i}", [P, D], F32) for i in range(NXI)]
        hb = None
        junk = sb(pfx + "junk", [P, D], BF16)
        ss = sb(pfx + "ss", [P, 1], F32)
        lnv = sb(pfx + "lnv", [P, 1], F32)
        rstd = sb(pfx + "rstd", [P, 1], F32)
        nt_tiles = (junk, ss, lnv, rstd, hb)
        NW = 2 if fwd else 3
        wsl = [sb(pfx + f"w{i}", [P, 8192], BF16) for i in range(NW)]
        NPS = 4 if fwd else 8
        pb = [0]
        big = [sb(pfx + f"big{i}", [P, 1024], BF16) for i in range(NPS)]
        qeT = [sb(pfx + f"qeT{i}", [P, 512], BF16) for i in range(NPS)]
        keT = [sb(pfx + f"keT{i}", [P, 512], BF16) for i in range(NPS)]
        ke = [sb(pfx + f"ke{i}", [P, 512], BF16) for i in range(NPS)]
        vg = [sb(pfx + f"vg{i}", [P, 1024], BF16) for i in range(NPS)]
        gsc = [sb(pfx + f"gsc{i}", [P, 4, 6], F32) for i in range(NPS)]
        S = sb(pfx + "S", [P, 4, 256], F32)
        Sbf = [sb(pfx + f"Sbf{h}", [P, 256], BF16) for h in range(4)]
        lr_sb = [sb(pfx + f"lr{i}", [P, 32], F32) for i in range(4)]
        lrT = [sb(pfx + f"lrT{i}", [33, P], F32) for i in range(4)]
        spt = None if fwd else [sb(pfx + f"spt{i}", [P, 512], F32) for i in range(4)]
        bsm = [sb(pfx + f"bsm{i}", [P, 4, 4], F32) for i in range(4)]
        dlt = [sb(pfx + f"dlt{i}", [P, 4, 2], F32) for i in range(4)]
        tA = [sb(pfx + f"tA{i}", [P, 512], F32) for i in range(2)]
        aTs = sb(pfx + "aTs", [P, 512], BF16)
        oin = None if fwd else sb(pfx + "oin", [P, 1024], F32)
        obuf = [sb(pfx + f"ob{i}", [P, 1024], BF16) for i in range(2)]
        wq = kb.sem_pool(pfx + "wq", NW)
        xq = kb.sem_pool(pfx + "xq", 4)
        oq = kb.sem_pool(pfx + "oq", 4)
        if fwd:
            qT = [sb(f"c_qT{i}", [P, 1024], BF16) for i in range(4)]
            kT = [sb(f"c_kT{i}", [P, 256], BF16) for i in range(6)]
            va = [sb(f"c_va{i}", [P, 256], BF16) for i in range(6)]
            mixA = [sb(f"c_mixA{i}", [P, 8, P], BF16) for i in range(4)]
            mixG = [sb(f"c_mixG{i}", [P, 8, P], BF16) for i in range(4)]
            csb = [sb(f"c_cs{i}", [P, 256], F32) for i in range(2)]
            Gq = [sb(f"c_Gq{i}", [P, 256], F32) for i in range(2)]
            Gk = [sb(f"c_Gk{i}", [P, 256], F32) for i in range(2)]
            tB = [sb(f"c_tB{i}", [P, 512], F32) for i in range(2)]
            spt = [tA[0], tA[1], tB[0], tB[1]]
            tq = [sb(f"c_tq{i}", [P, 512], F32) for i in range(2)]
            ssq = sb("c_ssq", [P, 4], F32)
            sqj = sb("c_sqj", [P, 256], BF16)
            lnq = sb("c_lnq", [P, 4], F32)
            rsq = sb("c_rsq", [P, 4], F32)
            PT = [sb(f"c_PT{i}", [P, 512], BF16) for i in range(6)]
            rec = [sb(f"c_rec{i}", [P, 512], F32) for i in range(2)]
            osum = sb("c_osum", [P, 1024], F32)
            xr = [sb(f"c_xr{i}", [P, 512], F32) for i in range(2)]
            xo = [sb(f"c_xo{i}", [P, 512], F32) for i in range(2)]
            rq = kb.sem_pool("c_rq", 3)
        AFT = cmat[:, CM_AFF:CM_AFF + P] if fwd else cmat[:, CM_AFB:CM_AFB + P]
        GMT = cmat[:, CM_GMF:CM_GMF + P] if fwd else cmat[:, CM_GMB:CM_GMB + P]
        IND = cmat[:, CM_INF:CM_INF + 4] if fwd else cmat[:, CM_INB:CM_INB + 4]
        W2 = cst["W2"]
        w2c = 0 if fwd else 512
        wlr = cst["wlr"]

        kb.op(kb.dve, [], [S], lambda: nc.vector.memset(S[:, :, :], 0.0))
        for i in range(4):
            kb.op(kb.dve, [], [lrT[i]], lambda i=i: nc.vector.memset(lrT[i][:, :], 1.0))
        cnt = dict(w=0, x=0, t=0, ob=0, xr=0)

        def load_w(src_ap, ncols=8192):
            w = wsl[cnt["w"] % NW]
            cnt["w"] += 1
            kb.dma(kb.pool, wq, w[:, 0:ncols], src_ap, [dr_in], [w], max_dma_last_dim=4096)
            return w

        def proj(i, w, wv_fn, ncols):
            bk = nb()
            for k in range(KD):
                kb.mm(bk, bk[:, 0:ncols], hT, hT[:, k, i * P:(i + 1) * P], w, wv_fn(k),
                      start=(k == 0), stop=(k == KD - 1))
            return bk

        def gating_all(idxs):
            A = AF
            bk = {i: proj(i, wlr, lambda k: wlr[:, k * 32:(k + 1) * 32], 32) for i in idxs}
            for i in idxs:
                kb.op(kb.act, [bk[i]], [lr_sb[i]],
                      lambda i=i: nc.scalar.copy(out=lr_sb[i][:, :], in_=bk[i][:, 0:32]))
            b2 = {}
            for i in idxs:
                b2[i] = nb()
                kb.tr(b2[i], b2[i][0:32, 0:P], lr_sb[i], lr_sb[i][:, :], ident, ident[:, 0:P])
            for i in idxs:
                kb.op(kb.act, [b2[i]], [lrT[i]],
                      lambda i=i: nc.scalar.copy(out=lrT[i][0:32, :], in_=b2[i][0:32, 0:P]))
            b3 = {}
            for i in idxs:
                b3[i] = nb()
                kb.mm(b3[i], b3[i][:, :], lrT[i], lrT[i][0:33, :], W2, W2[0:33, w2c:w2c + 512])
            for i in idxs:
                kb.op(kb.act, [b3[i]], [spt[i]],
                      lambda i=i: nc.scalar.activation(out=spt[i][:, :], in_=b3[i][:, :], func=A.Exp, scale=-1.0))
            for i in idxs:
                kb.op(kb.act, [spt[i], kb.one_t], [spt[i]],
                      lambda i=i: nc.scalar.activation(out=spt[i][:, :], in_=spt[i][:, :], func=A.Ln,
                                                       bias=kb.one_t[:, :]))
            b4 = {}
            for i in idxs:
                b4[i] = nb()
                kb.mm(b4[i], b4[i][:, :], cmat, AFT, spt[i], spt[i][:, :])
            for i in idxs:
                kb.op(kb.act, [b4[i]], [big[pb[0] + i]],
                      lambda i=i: nc.scalar.activation(out=big[pb[0] + i][:, 0:512], in_=b4[i][:, :], func=A.Exp,
                                                       scale=-1.0 / 16))
                kb.op(kb.act, [b4[i]], [big[pb[0] + i]],
                      lambda i=i: nc.scalar.activation(out=big[pb[0] + i][:, 512:1024], in_=b4[i][:, :], func=A.Exp,
                                                       scale=1.0 / 16))
            b5 = {}
            for i in idxs:
                b5[i] = nb()
                for h in range(4):
                    kb.mm(b5[i], b5[i][:, 4 * h:4 * h + 4], spt[i], spt[i][:, h * P:(h + 1) * P], cmat, IND)
            for i in idxs:
                kb.op(kb.act, [b5[i]], [bsm[i]],
                      lambda i=i: nc.scalar.copy(out=bsm[i][:, :, :],
                                                 in_=b5[i][:, 0:16].rearrange("p (h c) -> p h c", c=4)))
            for i in idxs:
                kb.op(kb.dve, [bsm[i]], [dlt[i]],
                      lambda i=i: nc.vector.tensor_tensor(out=dlt[i][:, :, :], in0=bsm[i][:, :, 0:2],
                                                          in1=bsm[i][:, :, 2:4], op=ALU.subtract))
            for i in idxs:
                kb.op(kb.act, [bsm[i]], [gsc[pb[0] + i]],
                      lambda i=i: nc.scalar.activation(out=gsc[pb[0] + i][:, :, 0:4], in_=bsm[i][:, :, :], func=A.Exp,
                                                       scale=-1.0 / 16))
                kb.op(kb.act, [dlt[i]], [gsc[pb[0] + i]],
                      lambda i=i: nc.scalar.activation(out=gsc[pb[0] + i][:, :, 4:6], in_=dlt[i][:, :, :], func=A.Exp,
                                                       scale=-1.0 / 16))

        def to_T(src_t, nchunks, dst_t, dst_ap):
            b2 = nb()
            for h in range(nchunks):
                kb.tr(b2, b2[:, h * P:(h + 1) * P], src_t, src_t[:, h * P:(h + 1) * P], ident, ident[:, 0:P])
            kb.op(kb.act, [b2], [dst_t], lambda: nc.scalar.copy(out=dst_ap, in_=b2[:, 0:nchunks * P]))

        def cons_qg(i, bk):
            i = pb[0] + i
            t = tA[cnt["t"] % 2]
            cnt["t"] += 1
            kb.op(kb.dve, [bk, big[i]], [t],
                  lambda: nc.vector.scalar_tensor_tensor(out=t[:, :], in0=bk[:, :], scalar=QS,
                                                         in1=big[i][:, 0:512], op0=ALU.mult, op1=ALU.mult))
            return lambda: to_T(t, 4, qeT[i], qeT[i][:, :])

        def cons_kg(i, bk):
            i = pb[0] + i
            t = tA[cnt["t"] % 2]
            cnt["t"] += 1
            kb.op(kb.dve, [bk, big[i]], [t],
                  lambda: nc.vector.tensor_tensor(out=t[:, :], in0=bk[:, :], in1=big[i][:, 512:1024],
                                                  op=ALU.mult))
            kb.op(kb.act, [t], [ke[i]], lambda: nc.scalar.copy(out=ke[i][:, :], in_=t[:, :]))
            return lambda: to_T(t, 4, keT[i], keT[i][:, :])

        def cons_vg(i, bk, c):
            i = pb[0] + i
            kb.op(kb.act, [bk], [vg[i]],
                  lambda: nc.scalar.copy(out=vg[i][:, c * 512:(c + 1) * 512], in_=bk[:, :]))

        def cons_gate(i, bk, c):
            i = pb[0] + i
            t = tq[cnt["t"] % 2]
            cnt["t"] += 1
            kb.op(kb.act, [bk], [t],
                  lambda: nc.scalar.activation(out=t[:, :], in_=bk[:, :], func=AF.Exp, scale=-1.0))
            kb.op(kb.act, [t, kb.one_t], [t],
                  lambda: nc.scalar.activation(out=t[:, :], in_=t[:, :], func=AF.Ln, bias=kb.one_t[:, :]))
            kb.op(kb.act, [t], [t],
                  lambda: nc.scalar.activation(out=t[:, :], in_=t[:, :], func=AF.Exp, scale=-1.0))
            kb.op(kb.dve, [bk, t], [big[i]],
                  lambda: nc.vector.tensor_tensor(out=big[i][:, c * 512:(c + 1) * 512], in0=bk[:, :],
                                                  in1=t[:, :], op=ALU.mult))

        def rope_tables(b):
            cs = csb[b % 2]
            kb.dma(kb.sp, rq, cs[:, :], dr["rope"][b * P:(b + 1) * P, :], [dr_in], [cs])
            for G, g0, gs0 in ((Gq[b % 2], RW_GQ, RW_GQS), (Gk[b % 2], RW_GK, RW_GKS)):
                kb.op(kb.dve, [cs, rows], [G],
                      lambda G=G, g0=g0: nc.vector.tensor_tensor(out=G[:, 0:P], in0=cs[:, 0:P],
                                                                 in1=rows[:, g0:g0 + P], op=ALU.mult))
                kb.op(kb.dve, [cs, rows], [G],
                      lambda G=G, gs0=gs0: nc.vector.tensor_tensor(out=G[:, P:2 * P], in0=cs[:, P:2 * P],
                                                                   in1=rows[:, gs0:gs0 + P], op=ALU.mult))

        def norm_rope(bk, c0, nh, G):
            t = tq[cnt["t"] % 2]
            a = tA[cnt["t"] % 2]
            bb = tB[cnt["t"] % 2]
            cnt["t"] += 1
            n = nh * P
            src = bk[:, c0:c0 + n]
            src3 = src.rearrange("p (h d) -> p h d", d=P)
            kb.op(kb.act, [bk], [t], lambda: nc.scalar.activation(out=t[:, 0:n], in_=src, func=AF.Square))
            kb.op(kb.dve, [t], [ssq],
                  lambda: nc.vector.tensor_reduce(out=ssq[:, 0:nh], in_=t[:, 0:n].rearrange("p (h d) -> p h d", d=P),
                                                  axis=AX.X, op=ALU.add))
            kb.op(kb.act, [ssq, kb.eps_t], [lnq],
                  lambda: nc.scalar.activation(out=lnq[:, 0:nh], in_=ssq[:, 0:nh], func=AF.Ln, scale=1.0 / P,
                                               bias=kb.eps_t[:, :]))
            kb.op(kb.act, [lnq], [rsq],
                  lambda: nc.scalar.activation(out=rsq[:, 0:nh], in_=lnq[:, 0:nh], func=AF.Exp, scale=-0.5))
            a3 = a[:, 0:n].rearrange("p (h d) -> p h d", d=P)
            b3 = bb[:, 0:n].rearrange("p (h d) -> p h d", d=P)
            kb.op(kb.dve, [bk, G], [a],
                  lambda: nc.vector.tensor_tensor(out=a3, in0=src3,
                                                  in1=G[:, 0:P].unsqueeze(1).broadcast_to([P, nh, P]),
                                                  op=ALU.mult))
            kb.op(kb.dve, [bk, G], [bb],
                  lambda: nc.vector.tensor_tensor(out=b3[:, :, 0:64], in0=src3[:, :, 64:128],
                                                  in1=G[:, P:P + 64].unsqueeze(1).broadcast_to([P, nh, 64]),
                                                  op=ALU.mult))
            kb.op(kb.dve, [bk, G], [bb],
                  lambda: nc.vector.tensor_tensor(out=b3[:, :, 64:128], in0=src3[:, :, 0:64],
                                                  in1=G[:, P + 64:2 * P].unsqueeze(1).broadcast_to([P, nh, 64]),
                                                  op=ALU.mult))
            kb.op(kb.dve, [a, bb], [a],
                  lambda: nc.vector.tensor_tensor(out=a[:, 0:n], in0=a[:, 0:n], in1=bb[:, 0:n], op=ALU.add))
            kb.op(kb.dve, [a, rsq], [a],
                  lambda: nc.vector.tensor_tensor(out=a3, in0=a3,
                                                  in1=rsq[:, 0:nh].unsqueeze(2).broadcast_to([P, nh, P]),
                                                  op=ALU.mult))
            return a

        def cons_q(i, b, bk, c):
            a = norm_rope(bk, 0, 4, Gq[b % 2])
            return lambda: to_T(a, 4, qT[i], qT[i][:, c * 512:(c + 1) * 512])

        def cons_kv(b, bk):
            a = norm_rope(bk, 0, 2, Gk[b % 2])
            kb.op(kb.act, [bk], [va[b % 6]], lambda: nc.scalar.copy(out=va[b % 6][:, :], in_=bk[:, 256:512]))
            return lambda: to_T(a, 2, kT[b % 6], kT[b % 6][:, :])

        def gla_block(i, b, order, emit):
            bI = bN = None
            if emit:
                bA = nb4()
                for h in range(4):
                    kb.mm(bA, bA[:, h * P:(h + 1) * P], keT[i], keT[i][:, h * P:(h + 1) * P],
                          qeT[i], qeT[i][:, h * P:(h + 1) * P])
                kb.op(kb.dve, [bA, cmat], [aTs],
                      lambda: nc.vector.tensor_tensor(out=aTs[:, :].rearrange("p (h i) -> p h i", i=P),
                                                      in0=bA[:, :].rearrange("p (h i) -> p h i", i=P),
                                                      in1=GMT.unsqueeze(1).broadcast_to([P, 4, P]), op=ALU.mult))
                bI = [banks[0], banks[1]]
                for h in range(4):
                    o = bI[h // 2]
                    kb.mm(o, o[:, (h % 2) * 256:(h % 2) * 256 + 256], aTs, aTs[:, h * P:(h + 1) * P],
                          vg[i], vg[i][:, h * 256:(h + 1) * 256])
                bN = [banks[2], banks[3]]
            for c in order:
                r0 = 64 * c
                if emit:
                    for h in range(4):
                        kb.op(kb.act, [S, gsc[i]], [Sbf[h]],
                              lambda h=h: nc.scalar.activation(out=Sbf[h][:, :], in_=S[:, h, :], func=AF.Copy,
                                                               scale=gsc[i][:, h, 2 + c:3 + c]))
                    for h in range(4):
                        o = bN[h // 2]
                        kb.mm(o, o[r0:r0 + 64, (h % 2) * 256:(h % 2) * 256 + 256],
                              qeT[i], qeT[i][:, h * P + r0:h * P + r0 + 64], Sbf[h], Sbf[h][:, :])
                bM = [nb4(), nb4()]
                for h in range(4):
                    o = bM[h // 2]
                    kb.mm(o, o[:, (h % 2) * 256:(h % 2) * 256 + 256], ke[i], ke[i][r0:r0 + 64, h * P:(h + 1) * P],
                          vg[i], vg[i][r0:r0 + 64, h * 256:(h + 1) * 256])
                for h in range(4):
                    o = bM[h // 2]
                    kb.op(kb.dve, [S, gsc[i]], [S],
                          lambda h=h: nc.vector.tensor_scalar(out=S[:, h, :], in0=S[:, h, :],
                                                              scalar1=gsc[i][:, h, c:c + 1], scalar2=None,
                                                              op0=ALU.mult))
                    kb.op(kb.dve, [o, S, gsc[i]], [S],
                          lambda h=h, o=o: nc.vector.scalar_tensor_tensor(
                              out=S[:, h, :], in0=o[:, (h % 2) * 256:(h % 2) * 256 + 256],
                              scalar=gsc[i][:, h, 4 + c:5 + c], in1=S[:, h, :], op0=ALU.mult, op1=ALU.add))
            return bI, bN

        def attention(i, b):
            kbs = [kk for kk in (b - 1, b, b + 1) if kk >= 0]
            for h2 in range(2):
                for n_, kk in enumerate(kbs):
                    bS = nb4()
                    masked = kk != b
                    kb.mm(bS, bS[:, :], kT[kk % 6], kT[kk % 6][:, h2 * P:(h2 + 1) * P],
                          qT[i], qT[i][:, h2 * 512:(h2 + 1) * 512], start=True, stop=not masked)
                    if masked:
                        mk = cst["amask"]
                        mc = 0 if kk < b else 512
                        kb.mm(bS, bS[:, :], cst["identb"], cst["identb"][:, :], mk, mk[:, mc:mc + 512],
                              start=False, stop=True)
                    pt = PT[h2 * 3 + n_]
                    kb.op(kb.act, [bS], [pt],
                          lambda pt=pt, bS=bS: nc.scalar.activation(out=pt[:, :], in_=bS[:, :], func=AF.Exp,
                                                                    scale=QS))
            for h2 in range(2):
                bD = nb4()
                bO = nb4()
                pts = [PT[h2 * 3 + n_] for n_ in range(len(kbs))]
                for n_, kk in enumerate(kbs):
                    kb.mm(bD, bD[:, :], cst["onesb"], cst["onesb"][:, :], pts[n_], pts[n_][:, :],
                          start=(n_ == 0), stop=False)
                kb.mm(bD, bD[:, :], cst["onesf"], cst["onesf"][0:1, :], cst["esink"],
                      cst["esink"][0:1, h2 * 512:(h2 + 1) * 512], start=False, stop=True)
                for n_, kk in enumerate(kbs):
                    kb.mm(bO, bO[:, :], va[kk % 6], va[kk % 6][:, h2 * P:(h2 + 1) * P], pts[n_], pts[n_][:, :],
                          start=(n_ == 0), stop=(n_ == len(kbs) - 1))
                rc = rec[h2]
                kb.op(kb.dve, [bD], [rc], lambda rc=rc, bD=bD: nc.vector.reciprocal(out=rc[:, :], in_=bD[:, :]))
                kb.op(kb.dve, [bO, rc], [mixA[i]],
                      lambda rc=rc, bO=bO, h2=h2: nc.vector.tensor_tensor(
                          out=mixA[i][:, 4 * h2:4 * h2 + 4, :], in0=bO[:, :].rearrange("p (h t) -> p h t", t=P),
                          in1=rc[:, :].rearrange("p (h t) -> p h t", t=P), op=ALU.mult))

        def combine(i, b, bI, bN):
            ob = obuf[cnt["ob"] % 2]
            cnt["ob"] += 1
            kb.dma(kb.sp, xq, ob[:, :], dr["obs"][b * P:(b + 1) * P, :], [dr["obs_t"][b]], [ob])
            for hh in range(2):
                kb.op(kb.act, [bN[hh]], [osum],
                      lambda hh=hh: nc.scalar.copy(out=osum[:, hh * 512:(hh + 1) * 512], in_=bN[hh][:, :]))
            for hh in range(2):
                kb.op(kb.dve, [bI[hh], osum], [osum],
                      lambda hh=hh: nc.vector.tensor_tensor(out=osum[:, hh * 512:(hh + 1) * 512], in0=bI[hh][:, :],
                                                            in1=osum[:, hh * 512:(hh + 1) * 512], op=ALU.add))
            kb.op(kb.dve, [osum, ob], [osum],
                  lambda: nc.vector.tensor_tensor(out=osum[:, :], in0=osum[:, :], in1=ob[:, :], op=ALU.add))
            for h in range(4):
                kb.op(kb.act, [osum], [sqj, ssq],
                      lambda h=h: nc.scalar.activation(out=sqj[:, 0:256], in_=osum[:, h * 256:(h + 1) * 256],
                                                       func=AF.Square, accum_out=ssq[:, h:h + 1]))
            kb.op(kb.act, [ssq, kb.eps_t], [lnq],
                  lambda: nc.scalar.activation(out=lnq[:, :], in_=ssq[:, :], func=AF.Ln, scale=1.0 / 256,
                                               bias=kb.eps_t[:, :]))
            kb.op(kb.act, [lnq], [rsq],
                  lambda: nc.scalar.activation(out=rsq[:, :], in_=lnq[:, :], func=AF.Exp, scale=-0.5))
            for h in range(4):
                kb.op(kb.dve, [osum, rsq, big[i]], [osum],
                      lambda h=h: nc.vector.scalar_tensor_tensor(
                          out=osum[:, h * 256:(h + 1) * 256], in0=osum[:, h * 256:(h + 1) * 256],
                          scalar=rsq[:, h:h + 1], in1=big[i][:, h * 256:(h + 1) * 256],
                          op0=ALU.mult, op1=ALU.mult))
            kb.op(kb.dve, [osum, rows], [osum],
                  lambda: nc.vector.tensor_tensor(
                      out=osum[:, :].rearrange("p (h e) -> p h e", e=256),
                      in0=osum[:, :].rearrange("p (h e) -> p h e", e=256),
                      in1=rows[:, RW_GOG:RW_GOG + 256].unsqueeze(1).broadcast_to([P, 4, 256]), op=ALU.mult))
            for hh in range(2):
                b2 = nb4()
                for q in range(4):
                    kb.tr(b2, b2[:, q * P:(q + 1) * P], osum, osum[:, (hh * 4 + q) * P:(hh * 4 + q + 1) * P],
                          ident, ident[:, 0:P])
                kb.op(kb.act, [b2], [mixG[i]],
                      lambda hh=hh, b2=b2: nc.scalar.copy(out=mixG[i][:, 4 * hh:4 * hh + 4, :],
                                                          in_=b2[:, :].rearrange("p (k t) -> p k t", t=P)))

        def load_norm(b, i):
            xt = xin[cnt["x"] % NXI]
            cnt["x"] += 1
            kb.dma(kb.sp, xq, xt[:, :], dr["x"][b * P:(b + 1) * P, :], [dr_in], [xt])
            rmsnorm_to_T(kb, nt_tiles, xt, xt[:, :], P, CV_G1, hT,
                         lambda k0, nk: hT[:, k0:k0 + nk, i * P:(i + 1) * P],
                         [nb(), nb(), nb(), nb()], (ident, cvec))

        w_in = dr["w_in"]

        def chunk(c, idxs, cons):
            w = load_w(w_in[c])
            wv = w[:, :].rearrange("p (k c) -> p k c", k=KD)
            pend = None
            for i in idxs:
                bk = proj(i, w, lambda k: wv[:, k, :], 512)
                fin = cons(i, bk)
                if pend is not None:
                    pend()
                pend = fin
            if pend is not None:
                pend()

        if not fwd:
            tiles = [list(range(t * 4 + 3, t * 4 - 1, -1)) for t in range(7, -1, -1)]

            def m_piece(slot, b, emit):
                def run():
                    bI, bN = gla_block(slot, b, (1, 0), emit)
                    if emit:
                        ob = obuf[cnt["ob"] % 2]
                        cnt["ob"] += 1
                        for hh in range(2):
                            kb.op(kb.act, [bN[hh]], [oin],
                                  lambda hh=hh: nc.scalar.copy(out=oin[:, hh * 512:(hh + 1) * 512],
                                                               in_=bN[hh][:, :]))
                            kb.op(kb.dve, [bI[hh], oin], [ob],
                                  lambda hh=hh: nc.vector.tensor_tensor(
                                      out=ob[:, hh * 512:(hh + 1) * 512], in0=bI[hh][:, :],
                                      in1=oin[:, hh * 512:(hh + 1) * 512], op=ALU.add))
                        kb.dma(kb.sp, oq, dr["obs"][b * P:(b + 1) * P, :], ob[:, :], [ob], [dr["obs_t"][b]])
                return run

            pending = []
            for tno, blocks in enumerate(tiles):
                emits = [b <= NXB - 1 for b in blocks]
                base = 4 * (tno % 2)

                def p_stage1(blocks=blocks):
                    for i, b in enumerate(blocks):
                        load_norm(b, i)

                pieces = [p_stage1, lambda: gating_all(range(4))]
                if any(emits):
                    pieces.append(lambda emits=emits: chunk(3, [i for i in range(4) if emits[i]], cons_qg))
                pieces.append(lambda: chunk(4, range(4), cons_kg))
                pieces.append(lambda: chunk(5, range(4), lambda i, bk: cons_vg(i, bk, 0)))
                pieces.append(lambda: chunk(6, range(4), lambda i, bk: cons_vg(i, bk, 1)))
                pb[0] = base
                n = max(len(pieces), len(pending))
                for k in range(n):
                    if k < len(pieces):
                        pieces[k]()
                    if k < len(pending):
                        pending[k]()
                pending = [m_piece(base + i, b, emits[i]) for i, b in enumerate(blocks)]
            for m in pending:
                m()
        else:
            tiles = [[0, 1, 2, 3], [4, 5, 6, 7], [8, 9, 10, 11], [12, 13, 14, 15], [16]]
            def stage1_pieces(blocks):
                return [(lambda b=b, i=i: load_norm(b, i)) for i, b in enumerate(blocks + [blocks[-1] + 1])]

            for pc in stage1_pieces(tiles[0]):
                pc()
            for tno, blocks in enumerate(tiles):
                nbk = len(blocks)
                ext = blocks[-1] + 1
                nxt = stage1_pieces(tiles[tno + 1]) if tno + 1 < len(tiles) else []
                gating_all(range(nbk))
                chunk(3, range(nbk), cons_qg)
                chunk(4, range(nbk), cons_kg)
                chunk(5, range(nbk), lambda i, bk: cons_vg(i, bk, 0))
                chunk(6, range(nbk), lambda i, bk: cons_vg(i, bk, 1))
                chunk(7, range(nbk), lambda i, bk: cons_gate(i, bk, 0))
                chunk(8, range(nbk), lambda i, bk: cons_gate(i, bk, 1))
                w = load_w(w_in[2])
                wv = w[:, :].rearrange("p (k c) -> p k c", k=KD)
                pend = None
                for i, b in enumerate(blocks + [ext]):
                    rope_tables(b)
                    bk = proj(i, w, lambda k: wv[:, k, :], 512)
                    fin = cons_kv(b, bk)
                    if pend is not None:
                        pend()
                    pend = fin
                pend()
                for c in (0, 1):
                    w = load_w(w_in[c])
                    wv = w[:, :].rearrange("p (k c) -> p k c", k=KD)
                    pend = None
                    for i, b in enumerate(blocks):
                        rope_tables(b)
                        bk = proj(i, w, lambda k: wv[:, k, :], 512)
                        fin = cons_q(i, b, bk, c)
                        if pend is not None:
                            pend()
                        pend = fin
                    pend()
                prev = None
                for i, b in enumerate(blocks):
                    attention(i, b)
                    if prev is not None:
                        combine(*prev)
                    bI, bN = gla_block(i, b, (0, 1), True)
                    prev = (i, b, bI, bN)
                combine(*prev)
                for c in range(4):
                    w = load_w(dr["w_out"][c])
                    wv = w[:, :].rearrange("p (k c) -> p k c", k=KD)
                    for i, b in enumerate(blocks):
                        xrt = xr[cnt["xr"] % 2]
                        xot = xo[cnt["xr"] % 2]
                        cnt["xr"] += 1
                        kb.dma(kb.sp, xq, xrt[:, :], dr["x"][b * P:(b + 1) * P, c * 512:(c + 1) * 512],
                               [dr_in], [xrt])
                        bk = nb()
                        for k in range(KD):
                            mt = mixA[i] if k < 8 else mixG[i]
                            kb.mm(bk, bk[:, :], mt, mt[:, k % 8, :], w, wv[:, k, :],
                                  start=(k == 0), stop=(k == KD - 1))
                        kb.op(kb.dve, [bk, xrt], [xot],
                              lambda bk=bk, xrt=xrt, xot=xot: nc.vector.tensor_tensor(
                                  out=xot[:, :], in0=bk[:, :], in1=xrt[:, :], op=ALU.add))
                        kb.dma(kb.sp, oq, dr["x1s"][b * P:(b + 1) * P, c * 512:(c + 1) * 512], xot[:, :],
                               [xot], [dr["x1_t"][(b, c)]])
                    if nxt:
                        nxt.pop(0)()
                for pc in nxt:
                    pc()
        kb.barrier()


def build_program(mode="full"):
    nc = bass.Bass("TRN2", target_bir_lowering=False)

    def din(name, shape, dt=F32):
        return nc.dram_tensor(name, shape, dt, kind="ExternalInput").ap()

    dr = {}
    dr["x"] = din("x", [SEQ, D])
    dr["w_in"] = din("w_in", [9, P, KD * 512])
    wlr_d = din("w_lr", [P, KD * 32])
    w2_d = din("w2", [33, 1024])
    dr["w_out"] = din("w_out", [4, P, KD * 512])
    w_up = din("w_up", [NJ // 2, P, 2 * KD * 2 * P])
    w_down = din("w_down", [16, P, 11 * 512])
    cvec_d = din("cvec", [P, NCV])
    cmat_d = din("cmat", [P, NCM])
    rows_d = din("rows", [1, NRW])
    sink_d = din("sink", [1, 8])
    dr["rope"] = din("rope", [18 * P, 256])
    amask_d = din("amask", [P, 1024])
    out = nc.dram_tensor("out", [NOWN * P, D], F32, kind="ExternalOutput").ap()
    dr["x1s"] = nc.dram_tensor("x1s", [NXB * P, D], F32, kind="Internal").ap()
    dr["obs"] = nc.dram_tensor("obs", [NXB * P, 1024], BF16, kind="Internal").ap()
    dbg = None
    if mode.startswith("dbg"):
        dbg = nc.dram_tensor("dbg", [NXB * P, D], F32, kind="ExternalOutput").ap()

    with contextlib.ExitStack() as es:
        kb = KB(nc, es)

        def sb(name, shape, dt):
            return Tl(es.enter_context(nc.sbuf_tensor(name, shape, dt)), name)

        cst = {}
        cst["cvec"] = sb("c_cvec", [P, NCV], F32)
        cst["cmat"] = sb("c_cmat", [P, NCM], F32)
        cst["rows"] = sb("c_rows", [P, NRW], F32)
        cst["W2"] = sb("c_W2", [33, 1024], F32)
        cst["wlr"] = sb("c_wlr", [P, KD * 32], BF16)
        cst["amask"] = sb("c_amask", [P, 1024], BF16)
        cst["identb"] = sb("c_identb", [P, P], BF16)
        cst["onesb"] = sb("c_onesb", [P, P], BF16)
        cst["onesf"] = sb("c_onesf", [1, P], F32)
        cst["esink"] = sb("c_esink", [1, 1024], F32)
        sink_sb = sb("c_sink", [1, 8], F32)
        kb.eps_t = sb("c_eps", [P, 1], F32)
        kb.one_t = sb("c_one", [P, 1], F32)
        cq = kb.sem_pool("cq", 4)
        cq2 = kb.sem_pool("cq2", 3)
        dr_in = Tl(None, "dram_in")
        dr["dr_in"] = dr_in
        dr["obs_t"] = [Tl(None, f"obs{b}") for b in range(NXB)]
        dr["x1_t"] = {(b, c): Tl(None, f"x1_{b}_{c}") for b in range(NXB) for c in range(4)}
        kb.dma(kb.sp, cq, cst["cvec"][:, :], cvec_d, [dr_in], [cst["cvec"]])
        kb.dma(kb.sp, cq, cst["cmat"][:, :], cmat_d, [dr_in], [cst["cmat"]])
        kb.dma(kb.sp, cq, cst["rows"][:, :], rows_d.partition_broadcast(P), [dr_in], [cst["rows"]])
        kb.dma(kb.sp, cq, cst["W2"][:, :], w2_d, [dr_in], [cst["W2"]])
        kb.dma(kb.sp, cq, sink_sb[:, :], sink_d, [dr_in], [sink_sb])
        kb.dma(kb.pool, cq2, cst["wlr"][:, :], wlr_d, [dr_in], [cst["wlr"]])
        kb.dma(kb.pool, cq2, cst["amask"][:, :], amask_d, [dr_in], [cst["amask"]])
        kb.dma(kb.pool, cq2, cst["identb"][:, :], cmat_d[:, CM_ID:CM_ID + P], [dr_in], [cst["identb"]])
        kb.op(kb.dve, [], [kb.eps_t], lambda: nc.vector.memset(kb.eps_t[:, :], EPS))
        kb.op(kb.dve, [], [kb.one_t], lambda: nc.vector.memset(kb.one_t[:, :], 1.0))
        kb.op(kb.dve, [], [cst["onesb"]], lambda: nc.vector.memset(cst["onesb"][:, :], 1.0))
        kb.op(kb.dve, [], [cst["onesf"]], lambda: nc.vector.memset(cst["onesf"][:, :], 1.0))
        kb.op(kb.act, [sink_sb], [sink_sb],
              lambda: nc.scalar.activation(out=sink_sb[:, :], in_=sink_sb[:, :], func=AF.Exp))
        kb.op(kb.dve, [sink_sb], [cst["esink"]],
              lambda: nc.vector.tensor_copy(out=cst["esink"][:, :].rearrange("p (h t) -> p h t", t=P),
                                            in_=sink_sb[:, :].unsqueeze(2).broadcast_to([1, 8, P])))

        out_tl = {}

        def out_rows(r0, n):
            if r0 not in out_tl:
                out_tl[r0] = Tl(None, f"dram_out{r0}")
            return out_tl[r0], out[r0:r0 + n, :]

        if mode == "ffn":
            def x1_rows(r0, n):
                return dr_in, dr["x"][r0:r0 + n, :]
        else:
            x1_rt = Tl(None, "x1_all")

            def x1_rows(r0, n):
                return x1_rt, dr["x1s"][r0:r0 + n, :]

        if mode in ("full", "dbg_bwd", "dbg_mix"):
            phase_mixer(kb, cst, dr, fwd=False)
        if mode in ("full", "dbg_mix"):
            phase_mixer(kb, cst, dr, fwd=True)
        if mode in ("full", "ffn"):
            phase_ffn(kb, cst, x1_rows, out_rows, (w_up, dr_in), (w_down, dr_in))
        if mode == "dbg_bwd":
            with nc.sbuf_tensor("dbg_t", [P, 1024], BF16) as t0, nc.sbuf_tensor("dbg_f", [P, 1024], F32) as t1:
                tt0, tt1 = Tl(t0), Tl(t1)
                dq = kb.sem_pool("dq", 2)
                for b in range(NXB):
                    kb.dma(kb.sp, dq, tt0[:, :], dr["obs"][b * P:(b + 1) * P, :], [dr_in], [tt0])
                    kb.op(kb.dve, [tt0], [tt1], lambda: nc.vector.tensor_copy(out=tt1[:, :], in_=tt0[:, :]))
                    kb.dma(kb.sp, dq, dbg[b * P:(b + 1) * P, 0:1024], tt1[:, :], [tt1], [dr_in])
                kb.barrier()
        if mode == "dbg_mix":
            with nc.sbuf_tensor("dbg_t", [P, D], F32) as t0:
                tt0 = Tl(t0)
                dq = kb.sem_pool("dq", 2)
                for b in range(NXB):
                    kb.dma(kb.sp, dq, tt0[:, :], dr["x1s"][b * P:(b + 1) * P, :], [dr_in], [tt0])
                    kb.dma(kb.sp, dq, dbg[b * P:(b + 1) * P, :], tt0[:, :], [tt0], [dr_in])
                kb.barrier()
        print(f"[build] mode={mode} instructions={kb.nins} waits={kb.nwait} sems={kb.nsem}")
    return nc


def host_layout_common(inp):
    f = lambda a: np.asarray(a, dtype=np.float32)
    w_in = f(inp["w_in"][0])
    w_out = f(inp["w_out"][0])
    w_up = f(inp["w_up"][0])
    w_down = f(inp["w_down"][0])
    wi = w_in[:, :4608].reshape(KD, P, 9, 512)
    w_in_l = np.ascontiguousarray(wi.transpose(2, 1, 0, 3)).reshape(9, P, KD * 512)
    wo = w_out.reshape(KD, P, 4, 512)
    w_out_l = np.ascontiguousarray(wo.transpose(2, 1, 0, 3)).reshape(4, P, KD * 512)
    wu = w_up.reshape(KD, P, 2, NJ // 2, 2, P)
    w_up_l = np.ascontiguousarray(wu.transpose(3, 1, 4, 0, 2, 5)).reshape(NJ // 2, P, 2 * KD * 2 * P)
    wd = w_down.reshape(4, 11, P, 4, 512)
    w_down_l = np.ascontiguousarray(wd.transpose(3, 0, 2, 1, 4)).reshape(16, P, 11 * 512)
    return dict(w_in=w_in_l, w_out=w_out_l, w_up=w_up_l, w_down=w_down_l,
                sink=f(inp["attn_sink"][0]).reshape(1, 8).copy())


def host_cvec(inp, flip):
    cv = np.zeros((P, NCV), np.float32)
    cv[:, CV_G1:CV_G1 + 16] = np.asarray(inp["norm1_g"][0]).reshape(KD, P).T
    cv[:, CV_G2:CV_G2 + 16] = np.asarray(inp["norm2_g"][0]).reshape(KD, P).T
    cw = np.asarray(inp["conv_w"][0], dtype=np.float32)
    cb = np.asarray(inp["conv_b"][0], dtype=np.float32)
    taps = (2, 1, 0) if flip else (0, 1, 2)
    cv[:, CV_CW0:CV_CW0 + 88] = cw[taps[0]].reshape(88, P).T
    cv[:, CV_CW1:CV_CW1 + 88] = cw[taps[1]].reshape(88, P).T
    cv[:, CV_CW2:CV_CW2 + 88] = cw[taps[2]].reshape(88, P).T
    cv[:, CV_CB:CV_CB + 88] = cb.reshape(88, P).T
    return cv


def host_cmat(flip=False):
    cm = np.zeros((P, NCM), np.float32)
    cm[:, CM_ID:CM_ID + P] = np.eye(P, dtype=np.float32)
    idx = np.arange(P)
    ch = idx // 64
    l = idx % 64
    same = ch[:, None] == ch[None, :]
    L = same & (l[None, :] <= l[:, None])
    Lref = same & (l[None, :] <= 32)
    U = same & (l[None, :] >= l[:, None])
    Uref = same & (l[None, :] >= 31)
    cm[:, CM_AFF:CM_AFF + P] = (L.astype(np.float32) - Lref.astype(np.float32)).T
    cm[:, CM_AFB:CM_AFB + P] = (U.astype(np.float32) - Uref.astype(np.float32)).T
    f_strict = flip
    b_strict = not flip
    mf = same & ((l[None, :] < l[:, None]) if f_strict else (l[None, :] <= l[:, None]))
    mb = same & ((l[None, :] > l[:, None]) if b_strict else (l[None, :] >= l[:, None]))
    cm[:, CM_GMF:CM_GMF + P] = mf.astype(np.float32).T
    cm[:, CM_GMB:CM_GMB + P] = mb.astype(np.float32).T
    for c in range(2):
        cm[:, CM_INF + c] = (ch == c)
        cm[:, CM_INF + 2 + c] = (ch == c) & (l <= 32)
        cm[:, CM_INB + c] = (ch == c)
        cm[:, CM_INB + 2 + c] = (ch == c) & (l >= 31)
    return cm


def host_core_consts(inp, flip):
    f = lambda a: np.asarray(a, dtype=np.float32)
    w_in = f(inp["w_in"][0])
    lr_f, lr_b = w_in[:, 4608:4624], w_in[:, 4624:4640]
    wa_f, wa_b = f(inp["gla_wa2_fwd"][0]), f(inp["gla_wa2_bwd"][0])
    ba_f, ba_b = f(inp["gla_ba_fwd"][0]), f(inp["gla_ba_bwd"][0])
    if flip:
        lr_f, lr_b, wa_f, wa_b, ba_f, ba_b = lr_b, lr_f, wa_b, wa_f, ba_b, ba_f
    wlr = np.concatenate([lr_f, lr_b], axis=1).reshape(KD, P, 32)
    w_lr = np.ascontiguousarray(wlr.transpose(1, 0, 2)).reshape(P, KD * 32)
    w2 = np.zeros((33, 1024), np.float32)
    w2[0:16, 0:512] = wa_f
    w2[16:32, 512:1024] = wa_b
    w2[32, 0:512] = ba_f
    w2[32, 512:1024] = ba_b
    rows = np.zeros((1, NRW), np.float32)
    gq, gk = f(inp["attn_q_norm_g"][0]), f(inp["attn_k_norm_g"][0])
    rows[0, RW_GQ:RW_GQ + P] = gq
    rows[0, RW_GQS:RW_GQS + P] = np.concatenate([gq[64:], gq[:64]])
    rows[0, RW_GK:RW_GK + P] = gk
    rows[0, RW_GKS:RW_GKS + P] = np.concatenate([gk[64:], gk[:64]])
    rows[0, RW_GOG:RW_GOG + 256] = f(inp["gla_out_norm_g"][0])
    t = np.arange(18 * P)
    pos = (SEQ - 1 - t) if flip else t
    inv = 1.0 / (10000.0 ** (np.arange(64, dtype=np.float32) / 64))
    ang = pos.astype(np.float32)[:, None] * inv[None, :]
    cos, sin = np.cos(ang).astype(np.float32), np.sin(ang).astype(np.float32)
    rope = np.concatenate([cos, cos, -sin, sin], axis=1).astype(np.float32)
    j = np.arange(P)[:, None]
    i = np.arange(P)[None, :]
    mL = np.where(j >= i, 0.0, NEG).astype(np.float32)
    mR = np.where(j <= i, 0.0, NEG).astype(np.float32)
    amask = np.concatenate([np.tile(mL, (1, 4)), np.tile(mR, (1, 4))], axis=1)
    return dict(w_lr=w_lr, w2=w2, rows=rows, rope=rope, amask=np.ascontiguousarray(amask),
                cvec=host_cvec(inp, flip), cmat=host_cmat(flip))


def make_in_maps(inputs, cores=range(8)):
    com = host_layout_common(inputs)
    cc = [host_core_consts(inputs, False), host_core_consts(inputs, True)]
    x = np.asarray(inputs["x"], dtype=np.float32)
    maps = []
    for c in cores:
        b, half = c // 2, c % 2
        xl = x[b] if half == 0 else x[b][::-1]
        m = dict(x=np.ascontiguousarray(xl))
        m.update(com)
        m.update(cc[half])
        maps.append(m)
    return maps


_NC_CACHE = {}


def kernel(**inputs):
    if "full" not in _NC_CACHE:
        _NC_CACHE["full"] = build_program("full")
    nc = _NC_CACHE["full"]
    maps = make_in_maps(inputs)
    res = run_bass_kernel_spmd(nc, maps, core_ids=list(range(8)))
    B = inputs["x"].shape[0]
    out = np.empty((B, SEQ, D), np.float32)
    for c in range(8):
        b, half = c // 2, c % 2
        o = res.results[c]["out"]
        if half == 0:
            out[b, :2048] = o
        else:
            out[b, 2048:] = o[::-1]
    return out
```
